# Optimizing a Trainium2 kernel written in Bass

```python
import jax, jax.numpy as jnp
from jax import lax
import numpy as np

D_MODEL = 1024
BATCH = 8
SEQ = 2048
DEPTH = 1
DEC_BATCH = 128
DEC_SEQ = 4
PAST_LEN = 16384
PAGE_SIZE = 128

N_META = 16
D_MIX = D_MODEL
D_TM = D_MIX // 2
TM_HEAD = 64
TM_HEADS = D_TM // TM_HEAD
DECAY_RANK = 64
AAA_RANK = 64
GATE_RANK = 128
D_TM_PROJ = 3 * D_TM + DECAY_RANK + AAA_RANK + GATE_RANK
D_LRU = D_MIX - D_TM
LRU_BLOCKS = 8
LRU_BLOCK = D_LRU // LRU_BLOCKS
LRU_CONV_W = 4
LRU_C = 8.0
D_IN_PROJ = D_TM_PROJ + 2 * D_LRU
D_FF = 3 * D_MODEL
FFN_CONV_W = 3
EPS = 1e-6
GN_EPS = 64e-5

kernel_name = 'hymba_rwkv7_rglru_convffn_step'


def _rms(x, g):
    xf = x.astype(jnp.float32)
    y = xf * lax.rsqrt(jnp.mean(xf * xf, -1, keepdims=True) + EPS)
    return (y * g.astype(jnp.float32)).astype(x.dtype)


def _causal_dwconv(x, buf, w, b):
    width = w.shape[0]
    seq = x.shape[1]
    xc = jnp.concatenate([buf.astype(x.dtype), x], axis=1)
    y = b + xc[:, 0:seq] * w[0]
    for j in range(1, width):
        y = y + xc[:, j:j + seq] * w[j]
    return y.astype(x.dtype), xc[:, seq:]


def _rwkv7(u, shift_buf, s0, mu, w0, w_up, a0, a_up, g_up, k_k, k_a, r_k, gn_g, gn_b):
    f32 = jnp.float32
    bsz, seq = u.shape[0], u.shape[1]
    prev = jnp.concatenate([shift_buf[:, None].astype(u.dtype), u[:, :-1]], axis=1)
    um = (u + (prev - u) * mu).astype(f32)
    cuts = [D_TM, 2 * D_TM, 3 * D_TM, 3 * D_TM + DECAY_RANK, 3 * D_TM + DECAY_RANK + AAA_RANK]
    r, k, v, xw, xa, xg = jnp.split(um, cuts, axis=-1)
    w_log = -jax.nn.softplus(-(w0 + jnp.tanh(xw) @ w_up)) - 0.5
    decay = jnp.exp(-jnp.exp(w_log))
    a = jax.nn.sigmoid(a0 + xa @ a_up)
    g = jax.nn.sigmoid(xg) @ g_up
    kk = k * k_k
    k = k * (1.0 + (a - 1.0) * k_a)
    heads = lambda t: t.astype(f32).reshape(bsz, seq, TM_HEADS, TM_HEAD)
    rh, kh, vh, dh, ah, kkh = map(heads, (r, k, v, decay, a, kk))
    kkh = kkh * lax.rsqrt(jnp.maximum(jnp.sum(kkh * kkh, -1, keepdims=True), 1e-24))

    def step(s, inp):
        r_t, d_t, k_t, v_t, kk_t, a_t = inp
        s_kk = jnp.einsum('bhvk,bhk->bhv', s, kk_t)
        s = (s * d_t[:, :, None, :]
             - s_kk[..., None] * (kk_t * a_t)[:, :, None, :]
             + v_t[..., None] * k_t[:, :, None, :])
        return s, jnp.einsum('bhvk,bhk->bhv', s, r_t)

    xs = tuple(jnp.moveaxis(t, 1, 0) for t in (rh, dh, kh, vh, kkh, ah))
    s_new, ys = lax.scan(step, s0.astype(f32), xs)
    y = jnp.moveaxis(ys, 0, 1)
    mean = jnp.mean(y, -1, keepdims=True)
    var = jnp.mean(jnp.square(y - mean), -1, keepdims=True)
    y = ((y - mean) * lax.rsqrt(var + GN_EPS)).reshape(bsz, seq, D_TM) * gn_g + gn_b
    bonus = (jnp.sum(rh * kh * r_k, -1, keepdims=True) * vh).reshape(bsz, seq, D_TM)
    out = (y + bonus) * g
    return out.astype(u.dtype), u[:, -1], s_new


def _rglru(xb, gate, conv_buf, h0, pos, conv_w, conv_b, wa, ba, wx, bx, lam, out_g):
    f32 = jnp.float32
    bsz, seq = xb.shape[0], xb.shape[1]
    xc, new_buf = _causal_dwconv(xb, conv_buf, conv_w, conv_b)
    xcf = xc.astype(f32)
    blocks = xcf.reshape(bsz, seq, LRU_BLOCKS, LRU_BLOCK)
    r_g = jax.nn.sigmoid(jnp.einsum('blhi,hij->blhj', blocks, wa).reshape(bsz, seq, D_LRU) + ba)
    i_g = jax.nn.sigmoid(jnp.einsum('blhi,hij->blhj', blocks, wx).reshape(bsz, seq, D_LRU) + bx)
    log_a = -LRU_C * r_g * jax.nn.softplus(-lam.astype(f32))
    a = jnp.exp(log_a)
    mult = jnp.where((pos == 0)[None, :, None], 1.0, jnp.sqrt(-jnp.expm1(2.0 * log_a)))
    b = xcf * i_g * mult
    b = b.at[:, 0].add(a[:, 0] * h0.astype(f32))

    def comb(l, r):
        return (l[0] * r[0], r[0] * l[1] + r[1])

    _, h = lax.associative_scan(comb, (a, b), axis=1)
    y = _rms(h * jax.nn.gelu(gate.astype(f32)), out_g)
    return y.astype(xb.dtype), new_buf, h[:, -1]


def _layer(x, pos, tm_shift, tm_wkv, lru_conv, lru_h, ffn_conv, w):
    (norm1_g, w_in, tm_mu, tm_w0, tm_w_up, tm_a0, tm_a_up, tm_g_up, tm_k_k, tm_k_a, tm_r_k,
     tm_gn_g, tm_gn_b, lru_conv_w, lru_conv_b, lru_wa, lru_ba, lru_wx, lru_bx, lru_lambda,
     lru_out_g, w_out, norm2_g, ffn_w_up, ffn_w_gate, ffn_conv_w, ffn_conv_b, ffn_w_down) = w
    xn = _rms(x, norm1_g)
    u = jnp.einsum('bld,de->ble', xn, w_in)
    u_tm, u_lx, u_lg = jnp.split(u, [D_TM_PROJ, D_TM_PROJ + D_LRU], axis=-1)
    y_tm, new_shift, new_wkv = _rwkv7(u_tm, tm_shift, tm_wkv, tm_mu, tm_w0, tm_w_up, tm_a0, tm_a_up,
                                      tm_g_up, tm_k_k, tm_k_a, tm_r_k, tm_gn_g, tm_gn_b)
    y_lru, new_lconv, new_h = _rglru(u_lx, u_lg, lru_conv, lru_h, pos, lru_conv_w, lru_conv_b,
                                     lru_wa, lru_ba, lru_wx, lru_bx, lru_lambda, lru_out_g)
    x = x + jnp.einsum('ble,ed->bld', jnp.concatenate([y_tm, y_lru], axis=-1), w_out)
    xn = _rms(x, norm2_g)
    up = jnp.einsum('bld,df->blf', xn, ffn_w_up)
    upc, new_fconv = _causal_dwconv(up, ffn_conv, ffn_conv_w, ffn_conv_b)
    hid = jax.nn.gelu(upc) * jnp.einsum('bld,df->blf', xn, ffn_w_gate)
    x = x + jnp.einsum('blf,fd->bld', hid, ffn_w_down)
    return x, (new_shift, new_wkv, new_lconv, new_h, new_fconv)


def setup_inputs(seed: int = 0) -> dict:
    key = jax.random.key(seed)
    ks = iter(jax.random.split(key, 48))
    nrm = lambda shape, scale: scale * jax.random.normal(next(ks), shape, jnp.float32)
    uni = lambda shape, lo, hi: jax.random.uniform(next(ks), shape, jnp.float32, lo, hi)
    a_init = uni((DEPTH, D_LRU), 0.9, 0.999) ** (1.0 / LRU_C)
    lam = jnp.log(a_init) - jnp.log1p(-a_init)
    return {
        'x_prompt': nrm((BATCH, SEQ, D_MODEL), 1.0),
        'x_sample': nrm((DEC_BATCH, DEC_SEQ, D_MODEL), 1.0),
        'state_tm_shift': nrm((DEPTH, DEC_BATCH, D_TM_PROJ), 1.0),
        'state_tm_wkv': nrm((DEPTH, DEC_BATCH, TM_HEADS, TM_HEAD, TM_HEAD), 0.1),
        'state_lru_conv': nrm((DEPTH, DEC_BATCH, LRU_CONV_W - 1, D_LRU), 1.0),
        'state_lru_h': nrm((DEPTH, DEC_BATCH, D_LRU), 0.5),
        'state_ffn_conv': nrm((DEPTH, DEC_BATCH, FFN_CONV_W - 1, D_FF), 1.0),
        'meta_tokens': nrm((N_META, D_MODEL), 1.0),
        'norm1_g': 1.0 + nrm((DEPTH, D_MODEL), 0.01),
        'w_in': nrm((DEPTH, D_MODEL, D_IN_PROJ), D_MODEL ** -0.5),
        'tm_mu': uni((DEPTH, D_TM_PROJ), 0.0, 1.0),
        'tm_w0': uni((DEPTH, D_TM), -6.0, 1.0),
        'tm_w_up': nrm((DEPTH, DECAY_RANK, D_TM), 0.1),
        'tm_a0': nrm((DEPTH, D_TM), 0.1),
        'tm_a_up': nrm((DEPTH, AAA_RANK, D_TM), 0.1),
        'tm_g_up': nrm((DEPTH, GATE_RANK, D_TM), GATE_RANK ** -0.5),
        'tm_k_k': 0.85 + nrm((DEPTH, D_TM), 0.02),
        'tm_k_a': 1.0 + nrm((DEPTH, D_TM), 0.02),
        'tm_r_k': nrm((DEPTH, TM_HEADS, TM_HEAD), 0.1),
        'tm_gn_g': 1.0 + nrm((DEPTH, D_TM), 0.01),
        'tm_gn_b': nrm((DEPTH, D_TM), 0.01),
        'lru_conv_w': nrm((DEPTH, LRU_CONV_W, D_LRU), LRU_CONV_W ** -0.5),
        'lru_conv_b': nrm((DEPTH, D_LRU), 0.01),
        'lru_wa': nrm((DEPTH, LRU_BLOCKS, LRU_BLOCK, LRU_BLOCK), LRU_BLOCK ** -0.5),
        'lru_ba': nrm((DEPTH, D_LRU), 0.1),
        'lru_wx': nrm((DEPTH, LRU_BLOCKS, LRU_BLOCK, LRU_BLOCK), LRU_BLOCK ** -0.5),
        'lru_bx': nrm((DEPTH, D_LRU), 0.1),
        'lru_lambda': lam,
        'lru_out_g': 1.0 + nrm((DEPTH, D_LRU), 0.01),
        'w_out': nrm((DEPTH, D_MIX, D_MODEL), D_MIX ** -0.5),
        'norm2_g': 1.0 + nrm((DEPTH, D_MODEL), 0.01),
        'ffn_w_up': nrm((DEPTH, D_MODEL, D_FF), D_MODEL ** -0.5),
        'ffn_w_gate': nrm((DEPTH, D_MODEL, D_FF), D_MODEL ** -0.5),
        'ffn_conv_w': nrm((DEPTH, FFN_CONV_W, D_FF), FFN_CONV_W ** -0.5),
        'ffn_conv_b': nrm((DEPTH, D_FF), 0.01),
        'ffn_w_down': nrm((DEPTH, D_FF, D_MODEL), D_FF ** -0.5),
        'norm_f_g': 1.0 + nrm((D_MODEL,), 0.01),
    }


def reference(x_prompt, x_sample, state_tm_shift, state_tm_wkv, state_lru_conv, state_lru_h,
              state_ffn_conv, meta_tokens, norm1_g, w_in, tm_mu, tm_w0, tm_w_up, tm_a0, tm_a_up,
              tm_g_up, tm_k_k, tm_k_a, tm_r_k, tm_gn_g, tm_gn_b, lru_conv_w, lru_conv_b, lru_wa,
              lru_ba, lru_wx, lru_bx, lru_lambda, lru_out_g, w_out, norm2_g, ffn_w_up, ffn_w_gate,
              ffn_conv_w, ffn_conv_b, ffn_w_down, norm_f_g):
    f32 = jnp.float32
    bsz = x_prompt.shape[0]
    xp = jnp.concatenate(
        [jnp.broadcast_to(meta_tokens[None].astype(x_prompt.dtype), (bsz, N_META, D_MODEL)), x_prompt],
        axis=1)
    xs = x_sample
    pos_p = jnp.arange(xp.shape[1])
    pos_s = PAST_LEN + jnp.arange(xs.shape[1])
    z_shift = jnp.zeros((bsz, D_TM_PROJ), xp.dtype)
    z_wkv = jnp.zeros((bsz, TM_HEADS, TM_HEAD, TM_HEAD), f32)
    z_lconv = jnp.zeros((bsz, LRU_CONV_W - 1, D_LRU), xp.dtype)
    z_h = jnp.zeros((bsz, D_LRU), f32)
    z_fconv = jnp.zeros((bsz, FFN_CONV_W - 1, D_FF), xp.dtype)
    p_states = [[] for _ in range(5)]
    s_states = [[] for _ in range(5)]
    for l in range(DEPTH):
        w = (norm1_g[l], w_in[l], tm_mu[l], tm_w0[l], tm_w_up[l], tm_a0[l], tm_a_up[l], tm_g_up[l],
             tm_k_k[l], tm_k_a[l], tm_r_k[l], tm_gn_g[l], tm_gn_b[l], lru_conv_w[l], lru_conv_b[l],
             lru_wa[l], lru_ba[l], lru_wx[l], lru_bx[l], lru_lambda[l], lru_out_g[l], w_out[l],
             norm2_g[l], ffn_w_up[l], ffn_w_gate[l], ffn_conv_w[l], ffn_conv_b[l], ffn_w_down[l])
        xp, ps = _layer(xp, pos_p, z_shift, z_wkv, z_lconv, z_h, z_fconv, w)
        xs, ss = _layer(xs, pos_s, state_tm_shift[l], state_tm_wkv[l], state_lru_conv[l],
                        state_lru_h[l], state_ffn_conv[l], w)
        for i in range(5):
            p_states[i].append(ps[i])
            s_states[i].append(ss[i])
    y_prompt = _rms(xp, norm_f_g)[:, N_META:]
    y_sample = _rms(xs, norm_f_g)
    p_tm_shift, p_tm_wkv, p_lru_conv, p_lru_h, p_ffn_conv = [jnp.stack(s) for s in p_states]
    s_tm_shift, s_tm_wkv, s_lru_conv, s_lru_h, s_ffn_conv = [jnp.stack(s) for s in s_states]
    return (y_prompt, y_sample, p_tm_shift, p_tm_wkv, p_lru_conv, p_lru_h, p_ffn_conv,
            s_tm_shift, s_tm_wkv, s_lru_conv, s_lru_h, s_ffn_conv)
```

```python
import numpy as np
from contextlib import ExitStack
import concourse.bass as bass
import concourse.mybir as mybir
from concourse.bass_utils import run_bass_kernel_spmd

F32 = mybir.dt.float32
BF16 = mybir.dt.bfloat16
AF = mybir.ActivationFunctionType
ALU = mybir.AluOpType

COMPUTE = ("pe", "act", "dve", "pool")
QUEUES = ("sp",)
SAME_ENGINE_SYNC = {"pe": False, "act": True, "dve": True, "pool": True, "sp": False}

D = 1024
DTM = 1792
DIN = 2816
DFF = 3072
NCORE = 8
SEQ = 2048
C0 = 0.6065306597126334


class _Rec:
    def __init__(self):
        self.call = None

    def __getattr__(self, name):
        def f(*a, **k):
            self.call = (name, a, k)
            return self
        return f


def _record(fn):
    r = _Rec()
    fn(r)
    assert r.call is not None
    return r.call


class TK:
    def __init__(self, nc, stack):
        self.nc = nc
        self.stack = stack
        self.streams = {e: [] for e in COMPUTE + QUEUES}
        self.count = {e: 0 for e in COMPUTE}
        self.known = {e: {} for e in COMPUTE + QUEUES}
        self.keys = {}
        self.sems = {}
        self.dcount = {}
        self.snaps = {}
        self.n_waits = 0
        for e in COMPUTE:
            self.sems[("eng", e)] = stack.enter_context(nc.semaphore("prog_" + e))

    def _dsem(self, name):
        k = ("dma", name)
        if k not in self.sems:
            self.sems[k] = self.stack.enter_context(self.nc.semaphore("d_" + name))
            self.dcount[name] = 0
        return self.sems[k]

    @staticmethod
    def _flat(keys):
        out = []
        for k in keys:
            if isinstance(k, list):
                out.extend(TK._flat(k))
            else:
                out.append(k)
        return out

    def _deps(self, eng, reads, writes):
        reads = self._flat(reads)
        writes = self._flat(writes)
        need = {}

        def add(d):
            if d is None:
                return
            kind, name, val = d
            if kind == "eng" and name == eng and not SAME_ENGINE_SYNC[eng]:
                return
            k = (kind, name)
            if self.known[eng].get(k, 0) >= val:
                return
            if need.get(k, 0) < val:
                need[k] = val

        for k in reads:
            st = self.keys.get(k)
            if st is not None:
                add(st["w"])
        for k in writes:
            st = self.keys.get(k)
            if st is not None:
                add(st["w"])
                for kk, vv in st["r"].items():
                    add((kk[0], kk[1], vv))
        waits = []
        for k, val in sorted(need.items(), key=lambda kv: -kv[1]):
            if self.known[eng].get(k, 0) >= val:
                continue
            waits.append((self.sems[k], val))
            self.known[eng][k] = val
            sn = self.snaps.get((k[0], k[1], val))
            if sn:
                kn = self.known[eng]
                for kk, vv in sn.items():
                    if kn.get(kk, 0) < vv:
                        kn[kk] = vv
        self.n_waits += len(waits)
        return waits

    def _update(self, mydep, reads, writes):
        reads = self._flat(reads)
        writes = self._flat(writes)
        kind, name, val = mydep
        for k in reads:
            st = self.keys.setdefault(k, {"w": None, "r": {}})
            if st["r"].get((kind, name), 0) < val:
                st["r"][(kind, name)] = val
        for k in writes:
            self.keys[k] = {"w": mydep, "r": {}}

    def op(self, eng, fn, reads=(), writes=()):
        waits = self._deps(eng, reads, writes)
        self.count[eng] += 1
        mydep = ("eng", eng, self.count[eng])
        sn = dict(self.known[eng])
        if SAME_ENGINE_SYNC[eng] is False or True:
            sn[("eng", eng)] = self.count[eng]
        self.snaps[mydep] = sn
        sem = self.sems[("eng", eng)]

        call = _record(fn)

        def closure(e, waits=waits, call=call, sem=sem):
            for s, v in waits:
                e.wait_ge(s, v)
            getattr(e, call[0])(*call[1], **call[2]).then_inc(sem, 1)

        self.streams[eng].append(closure)
        self._update(mydep, reads, writes)

    def dma(self, q, fn, semname, reads=(), writes=()):
        waits = self._deps(q, reads, writes)
        sem = self._dsem(semname)
        self.dcount[semname] += 16
        mydep = ("dma", semname, self.dcount[semname])
        self.snaps[mydep] = dict(self.known[q])

        call = _record(fn)

        def closure(e, waits=waits, call=call, sem=sem):
            for s, v in waits:
                e.wait_ge(s, v)
            getattr(e, call[0])(*call[1], **call[2]).then_inc(sem, 16)

        self.streams[q].append(closure)
        self._update(mydep, reads, writes)

    def final_wait(self, q="sp"):
        waits = []
        for name, c in self.dcount.items():
            if c > 0:
                waits.append((self.sems[("dma", name)], c))
        for e in COMPUTE:
            if self.count[e] > 0:
                waits.append((self.sems[("eng", e)], self.count[e]))

        def closure(e, waits=waits):
            for s, v in waits:
                e.wait_ge(s, v)

        self.streams[q].append(closure)

    def emit(self):
        nc = self.nc
        with nc.Block() as block:
            @block.tensor
            def _(e):
                for c in self.streams["pe"]:
                    c(e)

            @block.scalar
            def _(e):
                for c in self.streams["act"]:
                    c(e)

            @block.vector
            def _(e):
                for c in self.streams["dve"]:
                    c(e)

            @block.gpsimd
            def _(e):
                for c in self.streams["pool"]:
                    c(e)

            @block.sync
            def _(e):
                for c in self.streams["sp"]:
                    c(e)


VEC_SPEC = [("mu", 14), ("w0", 4), ("a0", 4), ("k_k", 4), ("k_a", 4), ("r_k", 4), ("gn_g", 4), ("gn_b", 4),
            ("lcw", 16), ("lcb", 4), ("ba", 4), ("bx", 4), ("lam", 4), ("outg", 4), ("fcw", 72), ("fcb", 24),
            ("g1", 8), ("g2", 8)]
VOFF = {}
_o = 0
for _n, _c in VEC_SPEC:
    VOFF[_n] = _o
    _o += _c
NV = _o

TILES = [("meta", 1, 16)] + [("prompt", 1, 256)] * 8 + [("sample", 16, 4)]
NSLOT = 5


def build(tiles=None, STAGE=99, WORDER=None, info=None):
    import os
    tiles = TILES if tiles is None else tiles
    nc = bass.Bass("TRN2", target_bir_lowering=False)
    di = lambda name, shape: nc.dram_tensor(name, shape, F32, kind="ExternalInput")
    do = lambda name, shape: nc.dram_tensor(name, shape, F32, kind="ExternalOutput")
    xp_d = di("xp", [SEQ, D]).ap()
    meta_d = di("meta", [16, D]).ap()
    xs_d = di("xs", [64, D]).ap()
    st_u_d = di("st_u", [128, 22, 16, 3]).ap()
    st_w_d = di("st_w", [128, 16, 4, 64]).ap()
    st_h_d = di("st_h", [128, 4, 16]).ap()
    st_f_d = di("st_f", [128, 24, 16, 2]).ap()
    vecs_d = di("vecs", [128, NV]).ap()
    gf_t = di("gf", [D])
    w_in_d = di("w_in", [D, DIN]).ap()
    w_out_d = di("w_out", [D, D]).ap()
    w_upf_d = di("w_upf", [D, DFF]).ap()
    w_gate_d = di("w_gate", [D, DFF]).ap()
    w_down_d = di("w_down", [DFF, D]).ap()
    tmwup_d = di("tmwup", [64, 512]).ap()
    tmaup_d = di("tmaup", [64, 512]).ap()
    tmgup_d = di("tmgup", [128, 512]).ap()
    wabd_d = di("wabd", [4, 128, 128]).ap()
    wxbd_d = di("wxbd", [4, 128, 128]).ap()
    cmask_d = di("cmask", [4, 128, 128]).ap()
    cones_d = di("cones", [3, 128, 128]).ap()

    wsc_d = nc.dram_tensor("wsc", [51, 128, 2048], BF16).ap()
    y_p_d = do("y_p", [SEQ, D]).ap()
    y_s_d = do("y_s", [64, D]).ap()
    o_u_d = {"prompt": do("op_u", [128, 22, 1, 3]).ap(), "sample": do("os_u", [128, 22, 16, 3]).ap()}
    o_w_d = {"prompt": do("op_w", [128, 1, 4, 64]).ap(), "sample": do("os_w", [128, 16, 4, 64]).ap()}
    o_h_d = {"prompt": do("op_h", [128, 4, 1]).ap(), "sample": do("os_h", [128, 4, 16]).ap()}
    o_f_d = {"prompt": do("op_f", [128, 24, 1, 2]).ap(), "sample": do("os_f", [128, 24, 16, 2]).ap()}

    with ExitStack() as st:
        tk = TK(nc, st)
        sb = lambda name, shape, dt=F32: st.enter_context(nc.sbuf_tensor("s_" + name, shape, dt))
        ps = lambda name, shape, dt=F32: st.enter_context(nc.psum_tensor("p_" + name, shape, dt))
        op = tk.op

        xtm2 = [sb("xtm%d" % i, [128, 2, D]) for i in range(2)]
        gfbc = sb("gfbc", [128, D])
        xsb = sb("xsb", [128, D], BF16)
        xnT2 = [sb("xnT%d" % i, [128, 8, 256], BF16) for i in range(2)]
        UW = 259
        u = sb("u", [128, 22, UW])
        carry = sb("carry", [128, 22, 16, 3])
        carryP = sb("carryP", [128, 22, 1, 3])
        WstP = sb("WstP", [128, 1, 4, 64])
        hstP = sb("hstP", [128, 4, 1])
        fcarryP = sb("fcarryP", [128, 24, 1, 2])
        ring = [sb("ring%d" % i, [128, 2048], BF16) for i in range(NSLOT)]
        vecs = sb("vecs", [128, NV])
        vder = sb("vder", [128, 8])
        stat = sb("stat", [128, 8])
        tA = sb("tA", [128, 4, 256])
        tK = sb("tK", [128, 4, 256])
        tB = sb("tB", [128, 4, 256])
        tG = sb("tG", [128, 4, 256])
        tGate = sb("tGate", [128, 4, 256])
        tBon = sb("tBon", [128, 4, 256])
        tE4 = sb("tE4", [128, 4, 256])
        tE24 = sb("tE24", [128, 4, 256])
        ones_t = sb("ones_t", [128, 128])
        dC = sb("dC", [128, 4, 16])
        rT = sb("rT", [128, 4, 256], BF16)
        kT = sb("kT", [128, 4, 256], BF16)
        bTm = sb("bTm", [128, 8, 256], BF16)
        ktTm = sb("ktTm", [128, 8, 256], BF16)
        BhFm = sb("BhFm", [128, 8, 256], BF16)
        KhFm = sb("KhFm", [128, 8, 256], BF16)
        Vbm = sb("Vbm", [128, 8, 256], BF16)
        thb = sb("thb", [128, 256], BF16)
        xab = sb("xab", [128, 256], BF16)
        sgb = sb("sgb", [128, 256], BF16)
        xcb = sb("xcb", [128, 4, 256], BF16)
        ycatT = sb("ycatT", [128, 8, 256], BF16)
        hidT = sb("hidT", [128, 6, 256], BF16)
        upbuf = [sb("upbuf%d" % i, [128, 258]) for i in range(2)]
        upc = [sb("upc%d" % i, [128, 256]) for i in range(2)]
        upc2 = [sb("upc2_%d" % i, [128, 256]) for i in range(2)]
        fcarry = sb("fcarry", [128, 24, 16, 2])
        Wst = sb("Wst", [128, 16, 4, 64])
        Wbd = sb("Wbd", [128, 4, 128], BF16)
        hst = sb("hst", [128, 4, 16])
        Pb = [sb("Pb%d" % i, [128, 1024], BF16) for i in range(2)]
        PTb = [sb("PTb%d" % i, [128, 1024], BF16) for i in range(2)]
        Zb = [sb("Zb%d" % i, [128, 1024], BF16) for i in range(2)]
        AkT = sb("AkT", [128, 1024], BF16)
        BrT = sb("BrT", [128, 1024], BF16)
        BkT = sb("BkT", [128, 1024], BF16)
        Vte = sb("Vte", [128, 8, 128], BF16)
        Bte = sb("Bte", [128, 8, 128], BF16)
        Kte = sb("Kte", [128, 8, 128], BF16)
        Xb = sb("Xb", [128, 8, 64], BF16)
        Une = sb("Une", [128, 8, 128], BF16)
        wupb = sb("wupb", [128, 512], BF16)
        aupb = sb("aupb", [128, 512], BF16)
        gupb = sb("gupb", [128, 512], BF16)
        wabd = sb("wabd", [128, 4, 128], BF16)
        wxbd = sb("wxbd", [128, 4, 128], BF16)
        cmask = sb("cmask", [128, 4, 128], BF16)
        cones = sb("cones", [128, 3, 128])
        pm = [ps("pm%d" % i, [128, 512]) for i in range(2)]
        ptb = ps("ptb", [128, 1024], BF16)
        pt32 = ps("pt32", [128, 512])
        pc = [ps("pc%d" % i, [128, 512]) for i in range(4)]
        pmi = [0]
        pci = [0]

        PT32K = ["pt32a", "pt32b"]

        def next_pm():
            pmi[0] ^= 1
            return pm[pmi[0]], ["pm%da" % pmi[0], "pm%db" % pmi[0]]

        def next_pc():
            pci[0] = (pci[0] + 1) % 4
            return pc[pci[0]], "pc%d" % pci[0]

        V = lambda name, j=0, n=1: vecs[:, VOFF[name] + j:VOFF[name] + j + n]

        tk.dma("sp", lambda e: e.dma_start(out=vecs[:], in_=vecs_d), "c_vecs", writes=["vecs"])
        tk.dma("sp", lambda e: e.dma_start(out=gfbc[:], in_=bass.AP(gf_t, 0, [[0, 128], [1, D]])), "c_gf", writes=["gfbc"])
        tk.dma("sp", lambda e: e.dma_start(out=cones[:], in_=cones_d.rearrange("c p j -> p c j")), "c_ones", writes=["cones"])
        tk.dma("pool", lambda e: e.dma_start(out=wupb[0:64, :], in_=tmwup_d), "c_w1", writes=["wupb"])
        tk.dma("pool", lambda e: e.dma_start(out=aupb[64:128, :], in_=tmaup_d), "c_w2", writes=["aupb"])
        tk.dma("pool", lambda e: e.dma_start(out=gupb[:], in_=tmgup_d), "c_w3", writes=["gupb"])
        tk.dma("pool", lambda e: e.dma_start(out=wabd[:], in_=wabd_d.rearrange("c p j -> p c j")), "c_w4", writes=["wabd"])
        tk.dma("pool", lambda e: e.dma_start(out=wxbd[:], in_=wxbd_d.rearrange("c p j -> p c j")), "c_w5", writes=["wxbd"])
        tk.dma("pool", lambda e: e.dma_start(out=cmask[:], in_=cmask_d.rearrange("c p j -> p c j")), "c_w6", writes=["cmask"])
        ident = cmask[:, 0, :]
        m_su = cmask[:, 1, :]
        m_sl = cmask[:, 2, :]
        m_ui = cmask[:, 3, :]
        blkavg = cones[:, 0, :]
        blkones = cones[:, 1, :]
        ones512 = cones[:, 2, :]
        op("dve", lambda e: e.memset(ones_t[:], 1.0), writes=["ones_t"])
        for (bf_, nm_) in ((bTm, "bTm"), (ktTm, "ktTm"), (BhFm, "BhFm"), (KhFm, "KhFm"), (Vbm, "Vbm")):
            op("pool", lambda e, bf_=bf_: e.memset(bf_[:], 0.0), writes=[(nm_, h) for h in range(8)])
        op("pool", lambda e: e.memset(Wbd[:], 0.0), writes=["Wbd"])
        op("pool", lambda e: e.memset(Une[:], 0.0), writes=["Une"])
        op("pool", lambda e: e.memset(wupb[64:128, :], 0.0), writes=["wupb"])
        op("pool", lambda e: e.memset(aupb[0:64, :], 0.0), writes=["aupb"])
        op("dve", lambda e: e.tensor_scalar(out=vder[:, 0:4], in0=V("k_a", 0, 4), scalar1=-1.0, scalar2=1.0, op0=ALU.mult, op1=ALU.add),
           reads=["vecs"], writes=["vder"])
        op("act", lambda e: e.activation(out=vder[:, 4:8], in_=V("lam", 0, 4), func=AF.Exp, scale=-1.0), reads=["vecs", "vder"], writes=["vder"])
        op("act", lambda e: e.activation(out=vder[:, 4:8], in_=vder[:, 4:8], func=AF.Ln, bias=1.0), reads=["vder"], writes=["vder"])
        op("dve", lambda e: e.tensor_scalar(out=vder[:, 4:8], in0=vder[:, 4:8], scalar1=-8.0, scalar2=None, op0=ALU.mult), reads=["vder"], writes=["vder"])

        converted = set()

        class GWS:
            def __init__(self, order, ahead=4):
                self.order = order
                self.ahead = ahead
                self.rec = []
                self.pos = 0
                self.nissued = 0
                self.free = list(range(NSLOT))
                self.inst = {}
                self.live = {}

            def _issue(self, seq, pidx):
                slot = self.free.pop(0)
                a, b = WSRC[pidx][1]
                view = ring[slot][:, 0:a * b].rearrange("p (a b) -> p a b", a=a)
                key = "ring%d" % slot
                if pidx not in converted:
                    converted.add(pidx)
                    tk.dma("pool", lambda e: e.dma_start(out=view, in_=WSRC[pidx][0]), "ringS%d" % slot, writes=[key])
                    tk.dma("sp", lambda e: e.dma_start(out=wsc_d[pidx], in_=ring[slot][:, :]), "ringW%d" % slot, reads=[key], writes=[("wsc", pidx)])
                else:
                    tk.dma("sp", lambda e: e.dma_start(out=ring[slot][:, :], in_=wsc_d[pidx]), "ringH%d" % slot, reads=[("wsc", pidx)], writes=[key])
                self.inst[seq] = (view, key, slot, pidx)

            def get(self, pidx):
                seq = self.pos
                self.pos += 1
                self.rec.append(pidx)
                if self.order is not None:
                    assert self.order[seq] == pidx, (seq, pidx, self.order[seq])
                while self.nissued <= seq:
                    assert self.free, "weight ring exhausted"
                    self._issue(self.nissued, pidx if self.order is None else self.order[self.nissued])
                    self.nissued += 1
                if self.order is not None:
                    while self.nissued < len(self.order) and self.nissued <= seq + self.ahead and len(self.free) > 0:
                        self._issue(self.nissued, self.order[self.nissued])
                        self.nissued += 1
                view, key, slot, _ = self.inst[seq]
                self.live.setdefault(pidx, []).append(seq)
                return view, key

            def done(self, pidx):
                seq = self.live[pidx].pop(0)
                self.free.append(self.inst[seq][2])

        def weight_items_src():
            items = []
            w_in_v = w_in_d.rearrange("(dc p) e -> p dc e", p=128)
            for i in range(11):
                items.append((w_in_v[:, :, i * 256:(i + 1) * 256], (8, 256)))
            w_out_v = w_out_d.rearrange("(ec p) d -> p ec d", p=128)
            for i in range(4):
                items.append((w_out_v[:, 2 * i:2 * i + 2, :], (2, 1024)))
            w_up_v = w_upf_d.rearrange("(dc p) f -> p dc f", p=128)
            w_gate_v = w_gate_d.rearrange("(dc p) f -> p dc f", p=128)
            for i in range(12):
                items.append((w_up_v[:, :, i * 256:(i + 1) * 256], (8, 256)))
                items.append((w_gate_v[:, :, i * 256:(i + 1) * 256], (8, 256)))
            w_down_v = w_down_d.rearrange("(fc p) d -> p fc d", p=128)
            for i in range(12):
                items.append((w_down_v[:, 2 * i:2 * i + 2, :], (2, 1024)))
            return items

        WSRC = weight_items_src()
        WS = GWS(WORDER)

        def rms_gen(par, nt, rows_list, gname):
            xtm_ = xtm2[par]
            xnT_ = xnT2[par]
            for s in range(nt):
                rows = rows_list[s]
                xk = ("xtm", par, s)
                op("act", lambda e, s=s, rows=rows: e.activation(out=xsb[:rows, :], in_=xtm_[:rows, s, :], func=AF.Square, accum_out=stat[:rows, 0:1]),
                   reads=[xk], writes=["xsb", "stat"])
                op("act", lambda e, rows=rows: e.activation(out=stat[:rows, 1:2], in_=stat[:rows, 0:1], func=AF.Ln, scale=1.0 / D, bias=1e-6),
                   reads=["stat"], writes=["stat1"])
                op("act", lambda e, rows=rows: e.activation(out=stat[:rows, 2:3], in_=stat[:rows, 1:2], func=AF.Exp, scale=-0.5), reads=["stat1"], writes=["stat2"])
                yield
                op("act", lambda e, s=s, rows=rows: e.activation(out=xsb[:rows, :], in_=xtm_[:rows, s, :], func=AF.Copy, scale=stat[:rows, 2:3]),
                   reads=[xk, "stat2"], writes=["xsb"])
                for dc in range(8):
                    op("pe", lambda e, dc=dc, rows=rows: e.transpose(ptb[:, dc * 128:dc * 128 + rows], xsb[:rows, dc * 128:(dc + 1) * 128], ident[:rows, :rows]),
                       reads=["xsb", "cmask"], writes=["ptb"])
                yield
                pv = ptb[:, :].rearrange("p (a b) -> p a b", a=8)[:, :, 0:rows]
                gsc = V(gname, 0, 8)
                gbc = bass.AP(gsc.tensor, gsc.offset, [gsc.ap[0], [1, 8], [0, rows]])
                op("dve", lambda e, s=s, rows=rows, pv=pv, gbc=gbc: e.tensor_tensor(out=xnT_[:, :, s * 128:s * 128 + rows], in0=pv, in1=gbc, op=ALU.mult),
                   reads=["ptb", "vecs"], writes=[("xnT", par)])
                yield

        def tile_geom(t):
            kind_, NS_, L_ = t
            NC_ = NS_ * L_
            nt_ = (NC_ + 127) // 128
            return NC_, nt_, [min(128, NC_ - s_ * 128) for s_ in range(nt_)]

        def front_gen(tidx, prow_):
            par = tidx % 2
            kind_ = tiles[tidx][0]
            NC_, nt_, rows_ = tile_geom(tiles[tidx])
            xtm_ = xtm2[par]
            if kind_ == "sample":
                tk.dma("sp", lambda e: e.dma_start(out=xtm_[:64, 0, :], in_=xs_d), "xld%d0" % par, writes=[("xtm", par, 0)])
            elif kind_ == "meta":
                tk.dma("sp", lambda e: e.dma_start(out=xtm_[:16, 0, :], in_=meta_d), "xld%d0" % par, writes=[("xtm", par, 0)])
            else:
                for s_ in range(2):
                    tk.dma("sp", lambda e, s_=s_, r0=prow_ + s_ * 128: e.dma_start(out=xtm_[:, s_, :], in_=xp_d[r0:r0 + 128, :]), "xld%d%d" % (par, s_), writes=[("xtm", par, s_)])
            yield
            for _ in rms_gen(par, nt_, rows_, "g1"):
                yield

        def make_tile(ti):
            kind, NS, L = tiles[ti]
            NC = NS * L
            C = min(L, 128)
            NCHK = L // C
            nt = (NC + 127) // 128
            rows_list = [min(128, NC - s * 128) for s in range(nt)]
            skind = "sample" if kind == "sample" else "prompt"
            W3 = 3 + L
            par = ti % 2
            xtm = xtm2[par]
            xnT = xnT2[par]
            XN = ("xnT", par)
            XT = lambda s_: ("xtm", par, s_)

            def Uv(e_, lo, hi):
                return u[:, e_, 0:NS * W3].rearrange("p (s w) -> p s w", s=NS)[:, :, lo:hi]

            def Fv(buf, fc):
                return buf[:, fc, 0:NC].rearrange("p (s l) -> p s l", s=NS)

            def F2(buf2d):
                return buf2d[:, 0:NC].rearrange("p (s l) -> p s l", s=NS)


            R_ = lambda fc: Uv(fc, 3, W3)
            K_ = lambda fc: Uv(4 + fc, 3, W3)
            V_ = lambda fc: Uv(8 + fc, 3, W3)
            E = lambda fc: tE4[:, fc, 0:NC]
            E2 = lambda fc: tE24[:, fc, 0:NC]
            E3 = lambda fc: tE4[:, fc, 0:NC].rearrange("p (s l) -> p s l", s=NS)
            E23 = lambda fc: tE24[:, fc, 0:NC].rearrange("p (s l) -> p s l", s=NS)
            FC = range(4)
            ukeys = [("u", e_) for e_ in range(22)]
            prow = prows[ti]
            nprow = prows[ti + 1] if ti + 1 < len(tiles) else 0

            if skind == "sample":
                carry_, Wst_, hst_, fcarry_ = carry, Wst, hst, fcarry
            else:
                carry_, Wst_, hst_, fcarry_ = carryP, WstP, hstP, fcarryP
            KC, KW, KH, KF = "carry_" + skind, "Wst_" + skind, "hst_" + skind, "fcarry_" + skind

            def phase_A():
                ukeys = [("u", e_) for e_ in range(22)]
                if kind == "meta":
                    op("dve", lambda e: e.memset(carry_[:], 0.0), reads=[KC], writes=[KC])
                    op("dve", lambda e: e.memset(Wst_[:], 0.0), reads=[KW], writes=[KW])
                    op("dve", lambda e: e.memset(hst_[:], 0.0), reads=[KH], writes=[KH])
                    op("dve", lambda e: e.memset(fcarry_[:], 0.0), reads=[(KF, f_) for f_ in range(24)], writes=[(KF, f_) for f_ in range(24)])
                uh = u[:, :, 0:NS * W3].rearrange("p e (s w) -> p e s w", s=NS)[:, :, :, 0:3]
                for e0 in range(0, 22, 11):
                    op("dve", lambda e, e0=e0: e.tensor_copy(out=u[:, e0:e0 + 11, 0:NS * W3].rearrange("p e (s w) -> p e s w", s=NS)[:, :, :, 0:3],
                                                             in_=carry_[:, e0:e0 + 11, 0:NS, :]),
                       reads=[KC], writes=ukeys[e0:e0 + 11])


                for pi_ in range(11):
                    wv, wk = WS.get(pi_)
                    for j in range(2):
                        e_ = pi_ * 2 + j
                        pmt, pmk = next_pm()
                        for dc in range(8):
                            op("pe", lambda e, wv=wv, dc=dc, j=j, pmt=pmt: e.matmul(pmt[:, 0:NC], wv[:, dc, j * 128:(j + 1) * 128], xnT[:, dc, 0:NC], start=(dc == 0), stop=(dc == 7)),
                               reads=[wk, XN], writes=[pmk])
                        op("act", lambda e, e_=e_, pmt=pmt: e.copy(out=Uv(e_, 3, W3), in_=pmt[:, 0:NC].rearrange("p (s l) -> p s l", s=NS)),
                           reads=[pmk], writes=[("u", e_)])
                    WS.done(pi_)
                    yield
                for e0 in range(0, 22, 11):
                    op("dve", lambda e, e0=e0: e.tensor_copy(out=carry_[:, e0:e0 + 11, 0:NS, :],
                                                             in_=u[:, e0:e0 + 11, 0:NS * W3].rearrange("p e (s w) -> p e s w", s=NS)[:, :, :, L:L + 3]),
                       reads=ukeys[e0:e0 + 11], writes=[KC])

                tEs = [tE4[:, 0, :], tE4[:, 1, :]]
                yield
                for e0 in range(0, 14, 2):
                    for q in range(2):
                        e_ = e0 + q
                        op("pool", lambda e, e_=e_, q=q: e.tensor_tensor(out=tEs[q][:, 0:NC].rearrange("p (s l) -> p s l", s=NS), in0=Uv(e_, 2, 2 + L), in1=Uv(e_, 3, W3), op=ALU.subtract),
                           reads=[("u", e_)], writes=[("tE", q)])
                    for q in range(2):
                        e_ = e0 + q
                        op("dve", lambda e, e_=e_, q=q: e.scalar_tensor_tensor(out=Uv(e_, 3, W3), in0=tEs[q][:, 0:NC].rearrange("p (s l) -> p s l", s=NS), scalar=V("mu", e_), in1=Uv(e_, 3, W3), op0=ALU.mult, op1=ALU.add),
                           reads=[("tE", q), ("u", e_), "vecs"], writes=[("u", e_)])
                yield
                op("act", lambda e: e.activation(out=F2(thb), in_=Uv(12, 3, W3), func=AF.Tanh), reads=[("u", 12)], writes=["thb"])
                yield
                op("dve", lambda e: e.tensor_copy(out=F2(xab), in_=Uv(12, 3, W3)), reads=[("u", 12)], writes=["xab"])
                yield
                op("act", lambda e: e.activation(out=F2(sgb), in_=Uv(13, 3, W3), func=AF.Sigmoid), reads=[("u", 13)], writes=["sgb"])
                yield
                for fc in range(4):
                    cs = slice(fc * 128, (fc + 1) * 128)
                    pmt, pmk = next_pm()
                    op("pe", lambda e, pmt=pmt, cs=cs: e.matmul(pmt[:, 0:NC], wupb[:, cs], thb[:, 0:NC], start=True, stop=True), reads=["wupb", "thb"], writes=[pmk])
                    op("act", lambda e, pmt=pmt, fc=fc: e.activation(out=tG[:, fc, 0:NC], in_=pmt[:, 0:NC], func=AF.Sigmoid, bias=V("w0", fc)), reads=[pmk, "vecs"], writes=[("tG", fc)])
                    pmt, pmk = next_pm()
                    op("pe", lambda e, pmt=pmt, cs=cs: e.matmul(pmt[:, 0:NC], aupb[:, cs], xab[:, 0:NC], start=True, stop=True), reads=["aupb", "xab"], writes=[pmk])
                    op("act", lambda e, pmt=pmt, fc=fc: e.activation(out=tA[:, fc, 0:NC], in_=pmt[:, 0:NC], func=AF.Sigmoid, bias=V("a0", fc)), reads=[pmk, "vecs"], writes=[("tA", fc)])
                    pmt, pmk = next_pm()
                    op("pe", lambda e, pmt=pmt, cs=cs: e.matmul(pmt[:, 0:NC], gupb[:, cs], sgb[:, 0:NC], start=True, stop=True), reads=["gupb", "sgb"], writes=[pmk])
                    op("act", lambda e, pmt=pmt, fc=fc: e.copy(out=tGate[:, fc, 0:NC], in_=pmt[:, 0:NC]), reads=[pmk], writes=[("tGate", fc)])
                yield
                for fc in FC:
                    op("pool", lambda e, fc=fc: e.tensor_scalar(out=Fv(tK, fc), in0=K_(fc), scalar1=V("k_k", fc), scalar2=0.0, op0=ALU.mult, op1=ALU.add),
                       reads=[("u", 4 + fc), "vecs"], writes=[("tK", fc)])
                yield
                for fc in FC:
                    op("pool", lambda e, fc=fc: e.tensor_tensor(out=E(fc), in0=tK[:, fc, 0:NC], in1=tK[:, fc, 0:NC], op=ALU.mult), reads=[("tK", fc)], writes=[("tE", fc)])
                yield
                for fc in FC:
                    pmt, pmk = next_pm()
                    op("pe", lambda e, pmt=pmt, fc=fc: e.matmul(pmt[:, 0:NC], blkones, E(fc), start=True, stop=True), reads=["cones", ("tE", fc)], writes=[pmk])
                    op("dve", lambda e, pmt=pmt, fc=fc: e.tensor_scalar(out=E2(fc), in0=pmt[:, 0:NC], scalar1=1e-24, scalar2=None, op0=ALU.max), reads=[pmk], writes=[("tE2", fc)])
                yield
                for fc in FC:
                    op("act", lambda e, fc=fc: e.activation(out=E2(fc), in_=E2(fc), func=AF.Ln), reads=[("tE2", fc)], writes=[("tE2", fc)])
                yield
                for fc in FC:
                    op("act", lambda e, fc=fc: e.activation(out=E2(fc), in_=E2(fc), func=AF.Exp, scale=-0.5), reads=[("tE2", fc)], writes=[("tE2", fc)])
                yield
                for fc in FC:
                    op("pool", lambda e, fc=fc: e.tensor_scalar(out=E(fc), in0=tA[:, fc, 0:NC], scalar1=V("k_a", fc), scalar2=vder[:, fc:fc + 1], op0=ALU.mult, op1=ALU.add),
                       reads=[("tA", fc), "vecs", "vder", ("tE", fc)], writes=[("tE", fc)])
                yield
                for fc in FC:
                    op("dve", lambda e, fc=fc: e.tensor_tensor(out=tK[:, fc, 0:NC], in0=tK[:, fc, 0:NC], in1=E2(fc), op=ALU.mult), reads=[("tK", fc), ("tE2", fc)], writes=[("tK", fc)])
                yield
                for fc in FC:
                    op("dve", lambda e, fc=fc: e.tensor_tensor(out=K_(fc), in0=K_(fc), in1=E3(fc), op=ALU.mult), reads=[("u", 4 + fc), ("tE", fc)], writes=[("u", 4 + fc)])
                yield
                for fc in FC:
                    op("pool", lambda e, fc=fc: e.tensor_tensor(out=tB[:, fc, 0:NC], in0=tK[:, fc, 0:NC], in1=tA[:, fc, 0:NC], op=ALU.mult), reads=[("tK", fc), ("tA", fc)], writes=[("tB", fc)])
                yield
                for fc in FC:
                    op("dve", lambda e, fc=fc: e.scalar_tensor_tensor(out=E3(fc), in0=R_(fc), scalar=V("r_k", fc), in1=K_(fc), op0=ALU.mult, op1=ALU.mult),
                       reads=[("u", fc), ("u", 4 + fc), "vecs", ("tE", fc)], writes=[("tE", fc)])
                yield
                for fc in FC:
                    pmt, pmk = next_pm()
                    op("pe", lambda e, pmt=pmt, fc=fc: e.matmul(pmt[:, 0:NC], blkones, E(fc), start=True, stop=True), reads=["cones", ("tE", fc)], writes=[pmk])
                    op("dve", lambda e, pmt=pmt, fc=fc: e.tensor_tensor(out=Fv(tBon, fc), in0=pmt[:, 0:NC].rearrange("p (s l) -> p s l", s=NS), in1=V_(fc), op=ALU.mult),
                       reads=[pmk, ("u", 8 + fc)], writes=[("tBon", fc)])
                yield
                for b in range(NS):
                    for ch in range(NCHK):
                        c0 = b * L + ch * C
                        for fc in FC:
                            op("dve", lambda e, fc=fc, c0=c0: e.tensor_tensor_scan(out=tE4[:, fc, c0:c0 + C], data0=ones_t[:, 0:C], data1=tG[:, fc, c0:c0 + C], initial=0.0, op0=ALU.mult, op1=ALU.add),
                               reads=[("tG", fc), "ones_t", ("tE", fc)], writes=[("tE", fc)])
                yield
                for fc in FC:
                    op("pool", lambda e, fc=fc: e.tensor_tensor(out=E2(fc), in0=E(fc), in1=tG[:, fc, 0:NC], op=ALU.subtract), reads=[("tE", fc), ("tG", fc), ("tE2", fc)], writes=[("tE2", fc)])
                yield
                for fc in FC:
                    op("act", lambda e, fc=fc: e.activation(out=E2(fc), in_=E2(fc), func=AF.Exp, scale=-C0), reads=[("tE2", fc)], writes=[("tE2", fc)])
                yield
                for fc in FC:
                    op("dve", lambda e, fc=fc: e.tensor_tensor(out=kT[:, fc, 0:NC], in0=tK[:, fc, 0:NC], in1=E2(fc), op=ALU.mult), reads=[("tK", fc), ("tE2", fc)], writes=[("kT", fc)])
                yield
                for fc in FC:
                    op("act", lambda e, fc=fc: e.activation(out=E2(fc), in_=E(fc), func=AF.Exp, scale=-C0), reads=[("tE", fc), ("tE2", fc)], writes=[("tE2", fc)])
                yield
                for fc in FC:
                    op("dve", lambda e, fc=fc: e.tensor_tensor(out=Fv(rT, fc), in0=R_(fc), in1=E23(fc), op=ALU.mult), reads=[("u", fc), ("tE2", fc)], writes=[("rT", fc)])
                yield
                for fc in FC:
                    e1v = tE24[:, fc, 0:NC].rearrange("p (n c) -> p n c", c=C)[:, :, C - 1:C]
                    op("pool", lambda e, fc=fc, e1v=e1v: e.tensor_copy(out=dC[:, fc, 0:NS * NCHK].rearrange("p (n o) -> p n o", o=1), in_=e1v), reads=[("tE2", fc)], writes=[("dC", fc)])
                yield
                for fc in FC:
                    op("act", lambda e, fc=fc: e.activation(out=E2(fc), in_=E(fc), func=AF.Exp, scale=C0), reads=[("tE", fc), ("tE2", fc)], writes=[("tE2", fc)])
                HH = [(2 * fc + h2, fc, slice(64 * h2, 64 * h2 + 64)) for fc in range(4) for h2 in range(2)]
                yield
                for (h, fc, rs) in HH:
                    op("dve", lambda e, fc=fc, h=h, rs=rs: e.tensor_tensor(out=bTm[rs, h, 0:NC], in0=tB[rs, fc, 0:NC], in1=tE24[rs, fc, 0:NC], op=ALU.mult),
                       reads=[("tB", fc), ("tE2", fc)], writes=[("bTm", h)])
                yield
                for (h, fc, rs) in HH:
                    op("dve", lambda e, fc=fc, h=h, rs=rs: e.tensor_tensor(out=ktTm[rs, h, 0:NC].rearrange("p (s l) -> p s l", s=NS), in0=K_(fc)[rs], in1=E23(fc)[rs], op=ALU.mult),
                       reads=[("u", 4 + fc), ("tE2", fc)], writes=[("ktTm", h)])
                yield
                for (h, fc, rs) in HH:
                    op("act", lambda e, fc=fc, h=h, rs=rs: e.copy(out=Vbm[rs, h, 0:NC].rearrange("p (s l) -> p s l", s=NS), in_=V_(fc)[rs]), reads=[("u", 8 + fc)], writes=[("Vbm", h)])
                yield
                for (h, fc, rs) in HH:
                    dcb = dC[rs, fc, 0:1]
                    dcb = bass.AP(dcb.tensor, dcb.offset, [dcb.ap[0], [1, NS * NCHK], [0, C]])
                    op("pool", lambda e, h=h, rs=rs, dcb=dcb: e.tensor_tensor(out=BhFm[rs, h, 0:NC].rearrange("p (n c) -> p n c", c=C), in0=bTm[rs, h, 0:NC].rearrange("p (n c) -> p n c", c=C), in1=dcb, op=ALU.mult),
                       reads=[("bTm", h), ("dC", fc)], writes=[("BhFm", h)])
                yield
                for (h, fc, rs) in HH:
                    dcb = dC[rs, fc, 0:1]
                    dcb = bass.AP(dcb.tensor, dcb.offset, [dcb.ap[0], [1, NS * NCHK], [0, C]])
                    op("pool", lambda e, h=h, rs=rs, dcb=dcb: e.tensor_tensor(out=KhFm[rs, h, 0:NC].rearrange("p (n c) -> p n c", c=C), in0=ktTm[rs, h, 0:NC].rearrange("p (n c) -> p n c", c=C), in1=dcb, op=ALU.mult),
                       reads=[("ktTm", h), ("dC", fc)], writes=[("KhFm", h)])

                yield

            def phase_B():
                fr_it = front_gen(ti + 1, nprow) if ti + 1 < len(tiles) else iter(())
                def lru_gen():
                    yield
                    for fc in range(4):
                        ex = 14 + fc
                        op("act", lambda e, fc=fc, ex=ex: e.activation(out=Fv(tK, fc), in_=Uv(ex, 3, W3), func=AF.Identity, scale=V("lcw", 12 + fc), bias=V("lcb", fc)),
                           reads=[("u", ex), "vecs", ("tK", fc)], writes=[("tK", fc)])
                        for j in range(3):
                            op("dve", lambda e, fc=fc, ex=ex, j=j: e.scalar_tensor_tensor(out=Fv(tK, fc), in0=Uv(ex, j, j + L), scalar=V("lcw", j * 4 + fc), in1=Fv(tK, fc), op0=ALU.mult, op1=ALU.add),
                               reads=[("u", ex), "vecs", ("tK", fc)], writes=[("tK", fc)])
                        op("act", lambda e, fc=fc: e.copy(out=xcb[:, fc, 0:NC], in_=tK[:, fc, 0:NC]), reads=[("tK", fc)], writes=[("xcb", fc)])
                    yield
                    for fc in range(4):
                        pmt, pmk = next_pm()
                        op("pe", lambda e, pmt=pmt, fc=fc: e.matmul(pmt[:, 0:NC], wabd[:, fc, :], xcb[:, fc, 0:NC], start=True, stop=True), reads=["wabd", ("xcb", fc)], writes=[pmk])
                        op("act", lambda e, pmt=pmt, fc=fc: e.activation(out=tB[:, fc, 0:NC], in_=pmt[:, 0:NC], func=AF.Sigmoid, bias=V("ba", fc)), reads=[pmk, "vecs", ("tB", fc)], writes=[("tB", fc)])
                        pmt, pmk = next_pm()
                        op("pe", lambda e, pmt=pmt, fc=fc: e.matmul(pmt[:, 0:NC], wxbd[:, fc, :], xcb[:, fc, 0:NC], start=True, stop=True), reads=["wxbd", ("xcb", fc)], writes=[pmk])
                        op("act", lambda e, pmt=pmt, fc=fc: e.activation(out=tG[:, fc, 0:NC], in_=pmt[:, 0:NC], func=AF.Sigmoid, bias=V("bx", fc)), reads=[pmk, "vecs", ("tG", fc)], writes=[("tG", fc)])
                    yield
                    for fc in range(4):
                        op("pool", lambda e, fc=fc: e.tensor_tensor(out=tG[:, fc, 0:NC], in0=tG[:, fc, 0:NC], in1=tK[:, fc, 0:NC], op=ALU.mult), reads=[("tG", fc), ("tK", fc)], writes=[("tG", fc)])
                    yield
                    for fc in range(4):
                        op("act", lambda e, fc=fc: e.activation(out=tB[:, fc, 0:NC], in_=tB[:, fc, 0:NC], func=AF.Exp, scale=vder[:, 4 + fc:5 + fc]), reads=[("tB", fc), "vder"], writes=[("tB", fc)])
                    yield
                    for fc in range(4):
                        op("pool", lambda e, fc=fc: e.tensor_tensor(out=tE4[:, fc, 0:NC], in0=tB[:, fc, 0:NC], in1=tB[:, fc, 0:NC], op=ALU.mult), reads=[("tB", fc), ("tE", fc)], writes=[("tE", fc)])
                    yield
                    for fc in range(4):
                        op("act", lambda e, fc=fc: e.activation(out=tE4[:, fc, 0:NC], in_=tE4[:, fc, 0:NC], func=AF.Ln, scale=-1.0, bias=1.0), reads=[("tE", fc)], writes=[("tE", fc)])
                    yield
                    for fc in range(4):
                        op("act", lambda e, fc=fc: e.activation(out=tE4[:, fc, 0:NC], in_=tE4[:, fc, 0:NC], func=AF.Exp, scale=0.5), reads=[("tE", fc)], writes=[("tE", fc)])
                    yield
                    for fc in range(4):
                        if kind == "meta":
                            op("dve", lambda e, fc=fc: e.memset(tE4[:, fc, 0:1], 1.0), reads=[("tE", fc)], writes=[("tE", fc)])
                        op("dve", lambda e, fc=fc: e.tensor_tensor(out=tG[:, fc, 0:NC], in0=tG[:, fc, 0:NC], in1=tE4[:, fc, 0:NC], op=ALU.mult), reads=[("tG", fc), ("tE", fc)], writes=[("tG", fc)])
                    yield
                    for fc in range(4):
                        for b in range(NS):
                            op("dve", lambda e, fc=fc, b=b: e.tensor_tensor_scan(out=tE4[:, fc, b * L:(b + 1) * L], data0=tB[:, fc, b * L:(b + 1) * L], data1=tG[:, fc, b * L:(b + 1) * L],
                                                                                initial=hst_[:, fc, b:b + 1], op0=ALU.mult, op1=ALU.add),
                               reads=[("tB", fc), ("tG", fc), KH, ("tE", fc)], writes=[("tE", fc)])
                        op("dve", lambda e, fc=fc: e.tensor_copy(out=hst_[:, fc, 0:NS].rearrange("p (s o) -> p s o", o=1), in_=Fv(tE4, fc)[:, :, L - 1:L]), reads=[("tE", fc), KH], writes=[KH])
                    yield
                    for fc in range(4):
                        eg = 18 + fc
                        op("act", lambda e, fc=fc, eg=eg: e.activation(out=Uv(eg, 3, W3), in_=Uv(eg, 3, W3), func=AF.Gelu_apprx_tanh), reads=[("u", eg)], writes=[("u", eg)])
                    yield
                    for fc in range(4):
                        eg = 18 + fc
                        op("dve", lambda e, fc=fc, eg=eg: e.tensor_tensor(out=Fv(tK, fc), in0=Fv(tE4, fc), in1=Uv(eg, 3, W3), op=ALU.mult), reads=[("tE", fc), ("u", eg), ("tK", fc)], writes=[("tK", fc)])
                        op("pool", lambda e, fc=fc: e.tensor_tensor(out=tE4[:, fc, 0:NC], in0=tK[:, fc, 0:NC], in1=tK[:, fc, 0:NC], op=ALU.mult), reads=[("tK", fc), ("tE", fc)], writes=[("tE", fc)])
                    yield
                    for fc in range(4):
                        op("pe", lambda e, fc=fc: e.matmul(pt32[:, 0:NC], ones512, tE4[:, fc, 0:NC], start=(fc == 0), stop=(fc == 3)), reads=["cones", ("tE", fc)], writes=[PT32K])
                    yield
                    op("act", lambda e: e.activation(out=E2(0), in_=pt32[:, 0:NC], func=AF.Ln, bias=1e-6), reads=[PT32K, ("tE2", 0)], writes=[("tE2", 0)])
                    op("act", lambda e: e.activation(out=E2(0), in_=E2(0), func=AF.Exp, scale=-0.5), reads=[("tE2", 0)], writes=[("tE2", 0)])
                    yield
                    for fc in range(4):
                        op("dve", lambda e, fc=fc: e.scalar_tensor_tensor(out=ycatT[:, 4 + fc, 0:NC], in0=tK[:, fc, 0:NC], scalar=V("outg", fc), in1=E2(0), op0=ALU.mult, op1=ALU.mult),
                           reads=[("tK", fc), ("tE2", 0), "vecs"], writes=[("ycatT", 4 + fc)])


                    yield

                lru_it = lru_gen()

                lru_cnt = [0]

                def lru_step(n=1):
                    for _ in range(n):
                        next(lru_it, None)
                    lru_cnt[0] += 1
                    if lru_cnt[0] >= 3:
                        next(fr_it, None)

                nlev = int(np.log2(C)) - 1
                fk = lambda nm: [(nm, fc) for fc in range(4)]
                lanes = [(b, ch) for b in range(NS) for ch in range(NCHK)]
                LPB = max(1, 1024 // (8 * C))
                assert (len(lanes) + LPB - 1) // LPB <= 2
                UPB = 512 // C

                def lane_c0(li):
                    b_, ch_ = lanes[li]
                    return b_ * L + ch_ * C

                def ubuf(li):
                    return li // LPB

                def uoff(li, h):
                    return ((li % LPB) * 8 + h) * C

                def mask_bc(m, n):
                    return bass.AP(m.tensor, m.offset, [[m.ap[0][0], C], [0, n], [1, C]])

                def unit_banks(lane_ids):
                    units = [(li, h) for li in lane_ids for h in range(8)]
                    return [units[i:i + UPB] for i in range(0, len(units), UPB)]

                def bank_key(nm, us):
                    return (nm, us[0][0], us[0][1])

                def emit_product(us, fn_ops, evac):
                    pct, pck = next_pc()
                    for ui_, (li, h) in enumerate(us):
                        lap, rap, rkeys = fn_ops(li, h)
                        op("pe", lambda e, pct=pct, ui_=ui_, lap=lap, rap=rap: e.matmul(pct[:C, ui_ * C:(ui_ + 1) * C], lap, rap, start=True, stop=True), reads=rkeys, writes=[pck])
                    evac(pct, pck, us)

                def sb_view(buf_list, us):
                    li0, h0 = us[0]
                    o0 = uoff(li0, h0)
                    return buf_list[ubuf(li0)][:C, o0:o0 + len(us) * C]

                def masked_evac(dst_list, dk, msk):
                    def ev(pct, pck, us):
                        n = len(us)
                        op("dve", lambda e: e.tensor_tensor(out=sb_view(dst_list, us).rearrange("p (u c) -> p u c", c=C),
                                                            in0=pct[:C, 0:n * C].rearrange("p (u c) -> p u c", c=C), in1=mask_bc(msk, n), op=ALU.mult),
                           reads=[pck, "cmask"], writes=[bank_key(dk, us)])
                    return ev

                def fm_ops(la, lm, ra, rm, lk, rk):
                    def f(li, h):
                        fc = h // 2
                        cs_ = slice(lane_c0(li), lane_c0(li) + C)
                        lap = la[:, h, cs_] if lm else la[:, fc, cs_]
                        rap = ra[:, h, cs_] if rm else ra[:, fc, cs_]
                        return lap, rap, [(lk, h if lm else fc), (rk, h if rm else fc)]
                    return f

                all_banks = unit_banks(range(len(lanes)))
                for us in all_banks:
                    emit_product(us, fm_ops(bTm, True, kT, False, "bTm", "kT"), masked_evac(Pb, "P", m_su))
                    emit_product(us, fm_ops(kT, False, bTm, True, "kT", "bTm"), masked_evac(PTb, "PT", m_sl))
                    lru_step()
                for us in all_banks:
                    n = len(us)
                    op("dve", lambda e, us=us, n=n: e.tensor_tensor(out=sb_view(Zb, us).rearrange("p (u c) -> p u c", c=C), in0=mask_bc(ident, n),
                                                                  in1=sb_view(Pb, us).rearrange("p (u c) -> p u c", c=C), op=ALU.subtract),
                       reads=[bank_key("P", us), "cmask"], writes=[bank_key("Z", us)])
                for lev in range(1, nlev + 1):
                    last = (lev == nlev)
                    for us in all_banks:
                        kP, kPT, kZ = bank_key("P", us), bank_key("PT", us), bank_key("Z", us)
                        pct, pck = next_pc()
                        for ui_, (li, h) in enumerate(us):
                            sl = slice(uoff(li, h), uoff(li, h) + C)
                            bi = ubuf(li)
                            op("pe", lambda e, pct=pct, ui_=ui_, sl=sl, bi=bi: e.matmul(pct[:C, ui_ * C:(ui_ + 1) * C], Pb[bi][:C, sl], PTb[bi][:C, sl], start=True, stop=True),
                               reads=[kP, kPT], writes=[pck])
                        if not last:
                            pct2, pck2 = next_pc()
                            for ui_, (li, h) in enumerate(us):
                                sl = slice(uoff(li, h), uoff(li, h) + C)
                                bi = ubuf(li)
                                op("pe", lambda e, pct2=pct2, ui_=ui_, sl=sl, bi=bi: e.matmul(pct2[:C, ui_ * C:(ui_ + 1) * C], PTb[bi][:C, sl], Pb[bi][:C, sl], start=True, stop=True),
                                   reads=[kP, kPT], writes=[pck2])
                        n = len(us)
                        op("act", lambda e, pct=pct, us=us, n=n: e.copy(out=sb_view(PTb, us), in_=pct[:C, 0:n * C]), reads=[pck, kPT], writes=[kPT])
                        if not last:
                            op("act", lambda e, pct2=pct2, us=us, n=n: e.copy(out=sb_view(Pb, us), in_=pct2[:C, 0:n * C]), reads=[pck2, kP], writes=[kP])
                    for us in all_banks:
                        kPT, kZ = bank_key("PT", us), bank_key("Z", us)
                        n = len(us)
                        pct3, pck3 = next_pc()
                        for ui_, (li, h) in enumerate(us):
                            sl = slice(uoff(li, h), uoff(li, h) + C)
                            bi = ubuf(li)
                            op("pe", lambda e, pct3=pct3, ui_=ui_, sl=sl, bi=bi: e.matmul(pct3[:C, ui_ * C:(ui_ + 1) * C], PTb[bi][:C, sl], Zb[bi][:C, sl], start=True, stop=True),
                               reads=[kPT, kZ], writes=[pck3])
                        op("dve", lambda e, pct3=pct3, us=us, n=n: e.tensor_tensor(out=sb_view(Zb, us), in0=pct3[:C, 0:n * C], in1=sb_view(Zb, us), op=ALU.add),
                           reads=[pck3, kZ], writes=[kZ])
                    lru_step()

                Alist = lambda t: [t, t]
                for li, (b, ch) in enumerate(lanes):
                    c0 = lane_c0(li)
                    cs = slice(c0, c0 + C)
                    lbanks = unit_banks([li])
                    zkeys = [bank_key("Z", us) for us in all_banks if any(u_[0] == li for u_ in us)]
                    for (srcb, dst, nm, dnm) in ((Vbm, Vte, "Vbm", "Vte"), (BhFm, Bte, "BhFm", "Bte"), (KhFm, Kte, "KhFm", "Kte")):
                        for h in range(8):
                            op("pe", lambda e, srcb=srcb, h=h: e.transpose(ptb[:C, h * 128:(h + 1) * 128], srcb[:, h, cs], ident),
                               reads=[(nm, h), "cmask"], writes=["ptb"])
                        op("act", lambda e, dst=dst: e.copy(out=dst[:C, :, :], in_=ptb[:C, :].rearrange("p (h f) -> p h f", h=8)), reads=["ptb"], writes=[dnm])
                    def a_evac(dst, dk, msk):
                        def ev(pct, pck, us):
                            n = len(us)
                            o0 = us[0][1] * C
                            op("dve", lambda e: e.tensor_tensor(out=dst[:C, o0:o0 + n * C].rearrange("p (u c) -> p u c", c=C),
                                                                in0=pct[:C, 0:n * C].rearrange("p (u c) -> p u c", c=C), in1=mask_bc(msk, n), op=ALU.mult),
                               reads=[pck, "cmask"], writes=[(dk, us[0][1])])
                        return ev
                    akeys = {}
                    for us in lbanks:
                        emit_product(us, fm_ops(ktTm, True, kT, False, "ktTm", "kT"), a_evac(AkT, "AkT", m_su))
                        emit_product(us, fm_ops(bTm, True, rT, False, "bTm", "rT"), a_evac(BrT, "BrT", m_ui))
                        emit_product(us, fm_ops(ktTm, True, rT, False, "ktTm", "rT"), a_evac(BkT, "BkT", m_ui))
                        for (_, h) in us:
                            akeys[h] = us[0][1]
                    lru_step()
                    for h2 in range(2):
                        rs = slice(64 * h2, 64 * h2 + 64)
                        op("pool", lambda e, rs=rs, h2=h2: e.tensor_copy(out=Wbd[rs, :, 64 * h2:64 * h2 + 64], in_=Wst_[rs, b, :, :]), reads=[KW], writes=["Wbd"])
                    pct, pck = next_pc()
                    for h in range(8):
                        fc, h2 = divmod(h, 2)
                        vs = slice(64 * h2, 64 * h2 + 64)
                        op("pe", lambda e, pct=pct, h=h, fc=fc, vs=vs: e.matmul(pct[:C, h * 64:(h + 1) * 64], kT[:, fc, cs], Wbd[:, fc, vs], start=True, stop=False),
                           reads=[("kT", fc), "Wbd"], writes=[pck])
                        op("pe", lambda e, pct=pct, h=h, vs=vs: e.matmul(pct[:C, h * 64:(h + 1) * 64], AkT[:C, h * C:(h + 1) * C], Vte[:C, h, vs], start=False, stop=True),
                           reads=[("AkT", akeys[h]), "Vte"], writes=[pck])
                    op("act", lambda e, pct=pct: e.copy(out=Xb[:C, :, :], in_=pct[:C, :].rearrange("p (u v) -> p u v", v=64)), reads=[pck], writes=["Xb"])
                    lru_step()
                    pct, pck = next_pc()
                    for h in range(8):
                        zsl = slice(uoff(li, h), uoff(li, h) + C)
                        op("pe", lambda e, pct=pct, h=h, zsl=zsl: e.matmul(pct[:C, h * 64:(h + 1) * 64], Zb[ubuf(li)][:C, zsl], Xb[:C, h, :], start=True, stop=True),
                           reads=zkeys + ["Xb"], writes=[pck])
                    une0 = Une[:C, 0, 0:1]
                    une_d = bass.AP(une0.tensor, une0.offset, [[une0.ap[0][0], C], [256, 4], [192, 2], [1, 64]])
                    op("act", lambda e, pct=pct, une_d=une_d: e.activation(out=une_d, in_=pct[:C, :].rearrange("p (f t v) -> p f t v", f=4, t=2), func=AF.Copy, scale=-1.0),
                       reads=[pck], writes=["Une"])
                    lru_step()
                    pct, pck = next_pc()
                    for fc in range(4):
                        o_ = pct[:, fc * C:(fc + 1) * C]
                        op("pe", lambda e, o_=o_, fc=fc: e.matmul(o_, Wbd[:, fc, :], rT[:, fc, cs], start=True, stop=False), reads=["Wbd", ("rT", fc)], writes=[pck])
                        for h2 in range(2):
                            h = 2 * fc + h2
                            op("pe", lambda e, o_=o_, h=h: e.matmul(o_, Une[:C, h, :], BrT[:C, h * C:(h + 1) * C], start=False, stop=False),
                               reads=["Une", ("BrT", akeys[h])], writes=[pck])
                            op("pe", lambda e, o_=o_, h=h, h2=h2: e.matmul(o_, Vte[:C, h, :], BkT[:C, h * C:(h + 1) * C], start=False, stop=(h2 == 1)),
                               reads=["Vte", ("BkT", akeys[h])], writes=[pck])
                    op("dve", lambda e, pct=pct: e.tensor_copy(out=tA[:, :, cs], in_=pct[:, 0:4 * C].rearrange("p (f c) -> p f c", c=C)),
                       reads=[pck], writes=fk("tA"))
                    lru_step()
                    pct, pck = next_pc()
                    for fc in range(4):
                        o_ = pct[:, fc * 64:(fc + 1) * 64]
                        for h2 in range(2):
                            h = 2 * fc + h2
                            vs = slice(64 * h2, 64 * h2 + 64)
                            op("pe", lambda e, o_=o_, h=h, vs=vs, h2=h2: e.matmul(o_, Bte[:C, h, :], Une[:C, h, vs], start=(h2 == 0), stop=False),
                               reads=["Bte", "Une"], writes=[pck])
                            op("pe", lambda e, o_=o_, h=h, vs=vs, h2=h2: e.matmul(o_, Kte[:C, h, :], Vte[:C, h, vs], start=False, stop=(h2 == 1)),
                               reads=["Kte", "Vte"], writes=[pck])
                    ci = b * NCHK + ch
                    for fc in range(4):
                        op("dve", lambda e, pct=pct, fc=fc: e.scalar_tensor_tensor(out=Wst_[:, b, fc, :], in0=Wst_[:, b, fc, :], scalar=dC[:, fc, ci:ci + 1],
                                                                                in1=pct[:, fc * 64:(fc + 1) * 64], op0=ALU.mult, op1=ALU.add),
                           reads=[pck, ("dC", fc), KW, "Wbd"], writes=[KW])
                for _ in lru_it:
                    pass
                for _ in fr_it:
                    pass
                outproj_part((4, 5, 6, 7))
                for fc in FC:
                    pmt, pmk = next_pm()
                    op("pe", lambda e, pmt=pmt, fc=fc: e.matmul(pmt[:, 0:NC], blkavg, tA[:, fc, 0:NC], start=True, stop=True), reads=["cones", ("tA", fc)], writes=[pmk])
                    op("dve", lambda e, pmt=pmt, fc=fc: e.tensor_tensor(out=E(fc), in0=tA[:, fc, 0:NC], in1=pmt[:, 0:NC], op=ALU.subtract), reads=[pmk, ("tA", fc), ("tE", fc)], writes=[("tE", fc)])
                for fc in FC:
                    op("act", lambda e, fc=fc: e.activation(out=E2(fc), in_=E(fc), func=AF.Square), reads=[("tE", fc), ("tE2", fc)], writes=[("tE2", fc)])
                for fc in FC:
                    pmt, pmk = next_pm()
                    op("pe", lambda e, pmt=pmt, fc=fc: e.matmul(pmt[:, 0:NC], blkavg, E2(fc), start=True, stop=True), reads=["cones", ("tE2", fc)], writes=[pmk])
                    op("act", lambda e, pmt=pmt, fc=fc: e.activation(out=E2(fc), in_=pmt[:, 0:NC], func=AF.Ln, bias=64e-5), reads=[pmk, ("tE2", fc)], writes=[("tE2", fc)])
                for fc in FC:
                    op("act", lambda e, fc=fc: e.activation(out=E2(fc), in_=E2(fc), func=AF.Exp, scale=-0.5), reads=[("tE2", fc)], writes=[("tE2", fc)])
                for fc in FC:
                    op("dve", lambda e, fc=fc: e.tensor_tensor(out=E(fc), in0=E(fc), in1=E2(fc), op=ALU.mult), reads=[("tE", fc), ("tE2", fc)], writes=[("tE", fc)])
                for fc in FC:
                    op("pool", lambda e, fc=fc: e.tensor_scalar(out=E(fc), in0=E(fc), scalar1=V("gn_g", fc), scalar2=V("gn_b", fc), op0=ALU.mult, op1=ALU.add), reads=[("tE", fc), "vecs"], writes=[("tE", fc)])
                for fc in FC:
                    op("pool", lambda e, fc=fc: e.tensor_tensor(out=E(fc), in0=E(fc), in1=tBon[:, fc, 0:NC], op=ALU.add), reads=[("tE", fc), ("tBon", fc)], writes=[("tE", fc)])
                for fc in FC:
                    op("dve", lambda e, fc=fc: e.tensor_tensor(out=ycatT[:, fc, 0:NC], in0=E(fc), in1=tGate[:, fc, 0:NC], op=ALU.mult), reads=[("tE", fc), ("tGate", fc)], writes=[("ycatT", fc)])

                for _ in lru_it:
                    pass


            shared = {}

            def outproj_part(ecs):
                if "wo" not in shared:
                    shared["wo"] = [WS.get(11 + i) for i in range(4)]
                wo = shared["wo"]
                groups = [(s, half) for s in range(nt) for half in range(2)]
                if True:
                    for gi, (s, half) in enumerate(groups):
                        rows = rows_list[s]
                        for ec in ecs:
                            wv, wk = wo[ec // 2]
                            op("pe", lambda e, gi=gi, s=s, rows=rows, ec=ec, wv=wv, half=half: e.matmul(pc[gi][:rows, :], ycatT[:, ec, s * 128:s * 128 + rows], wv[:, ec % 2, half * 512:(half + 1) * 512], start=(ec == 4), stop=(ec == 3)),
                               reads=[("ycatT", ec), wk], writes=["pc%d" % gi])

            def phase_C():
                outproj_part((0, 1, 2, 3))
                groups = [(s, half) for s in range(nt) for half in range(2)]
                for gi, (s, half) in enumerate(groups):
                    rows = rows_list[s]
                    op("dve", lambda e, gi=gi, s=s, rows=rows, half=half: e.tensor_tensor(out=xtm[:rows, s, half * 512:(half + 1) * 512], in0=xtm[:rows, s, half * 512:(half + 1) * 512], in1=pc[gi][:rows, :], op=ALU.add),
                       reads=["pc%d" % gi, XT(s)], writes=[XT(s)])
                for i in range(4):
                    WS.done(11 + i)
                for _ in rms_gen(par, nt, rows_list, "g2"):
                    pass


            def phase_D(bg, BGSTEP):

                W2 = 2 + L
                accs = [(s, half) for s in range(nt) for half in range(2)]
                full = (kind != "meta")

                def emit_down(pd, wd, wdk):
                    for j in range(2):
                        f = pd * 2 + j
                        for ai, (s, half) in enumerate(accs):
                            rows = rows_list[s]
                            op("pe", lambda e, ai=ai, s=s, rows=rows, half=half, f=f, j=j: e.matmul(pc[ai][:rows, :], hidT[:, f % 6, s * 128:s * 128 + rows], wd[:, j, half * 512:(half + 1) * 512], start=(f == 0), stop=(f == 23)),
                               reads=[("hidT", f % 6), wdk], writes=["pc%d" % ai])

                for pi_ in range(12):
                    wu, wuk = WS.get(15 + 2 * pi_)
                    if full:
                        wg, wgk = WS.get(16 + 2 * pi_)
                        if pi_ >= 2:
                            wd, wdk = WS.get(39 + pi_ - 2)
                    JJ = range(2)
                    fs = [pi_ * 2 + j for j in JJ]
                    ubv = [upbuf[j][:, 0:NS * W2].rearrange("p (s w) -> p s w", s=NS) for j in JJ]
                    ubk = ["upbuf%d" % j for j in JJ]
                    ucv = [upc[j][:, 0:NC].rearrange("p (s l) -> p s l", s=NS) for j in JJ]
                    uc2v = [upc2[j][:, 0:NC].rearrange("p (s l) -> p s l", s=NS) for j in JJ]
                    uc2k = ["upc2_%d" % j for j in JJ]
                    uck = ["upc%d" % j for j in JJ]
                    pu = []
                    for j in JJ:
                        pmt, pmk = pm[pi_ % 2][:, j * 256:(j + 1) * 256], ["pm%da" % (pi_ % 2), "pm%db" % (pi_ % 2)]
                        pu.append((pmt, pmk))
                        for dc in range(8):
                            op("pe", lambda e, pmt=pmt, dc=dc, j=j: e.matmul(pmt[:, 0:NC], wu[:, dc, j * 128:(j + 1) * 128], xnT[:, dc, 0:NC], start=(dc == 0), stop=(dc == 7)),
                               reads=[wuk, XN], writes=[pmk])
                    if full and pi_ >= 2:
                        emit_down(pi_ - 2, wd, wdk)
                        WS.done(39 + pi_ - 2)
                    WS.done(15 + 2 * pi_)
                    for j in JJ:
                        op("pool", lambda e, j=j: e.tensor_copy(out=ubv[j][:, :, 0:2], in_=fcarry_[:, fs[j], 0:NS, :]), reads=[(KF, fs[j]), ubk[j]], writes=[ubk[j]])
                    for j in JJ:
                        pmt, pmk = pu[j]
                        op("act", lambda e, j=j, pmt=pmt: e.copy(out=ubv[j][:, :, 2:W2], in_=pmt[:, 0:NC].rearrange("p (s l) -> p s l", s=NS)), reads=[pmk, ubk[j]], writes=[ubk[j]])
                    for j in JJ:
                        op("pool", lambda e, j=j: e.tensor_copy(out=fcarry_[:, fs[j], 0:NS, :], in_=ubv[j][:, :, L:L + 2]), reads=[ubk[j], (KF, fs[j])], writes=[(KF, fs[j])])
                    if full:
                        for j in JJ:
                            pmt, pmk = pu[j]
                            op("act", lambda e, j=j, pmt=pmt: e.activation(out=upc[j][:, 0:NC], in_=pmt[:, 0:NC], func=AF.Identity, scale=V("fcw", 48 + fs[j]), bias=V("fcb", fs[j])),
                               reads=[pmk, "vecs", uck[j]], writes=[uck[j]])
                        for j in JJ:
                            op("pool", lambda e, j=j: e.tensor_scalar(out=uc2v[j], in0=ubv[j][:, :, 0:L], scalar1=V("fcw", fs[j]), scalar2=0.0, op0=ALU.mult, op1=ALU.add),
                               reads=[ubk[j], "vecs", uc2k[j]], writes=[uc2k[j]])
                        for j in JJ:
                            op("dve", lambda e, j=j: e.scalar_tensor_tensor(out=ucv[j], in0=ubv[j][:, :, 1:1 + L], scalar=V("fcw", 24 + fs[j]), in1=ucv[j], op0=ALU.mult, op1=ALU.add),
                               reads=[ubk[j], "vecs", uck[j]], writes=[uck[j]])
                        for j in JJ:
                            op("pool", lambda e, j=j: e.tensor_tensor(out=ucv[j], in0=ucv[j], in1=uc2v[j], op=ALU.add), reads=[uck[j], uc2k[j]], writes=[uck[j]])
                        for j in JJ:
                            op("act", lambda e, j=j: e.activation(out=upc[j][:, 0:NC], in_=upc[j][:, 0:NC], func=AF.Gelu_apprx_tanh), reads=[uck[j]], writes=[uck[j]])
                        pg = []
                        for j in JJ:
                            pmt, pmk = pt32[:, j * 256:(j + 1) * 256], PT32K
                            pg.append((pmt, pmk))
                            for dc in range(8):
                                op("pe", lambda e, pmt=pmt, dc=dc, j=j: e.matmul(pmt[:, 0:NC], wg[:, dc, j * 128:(j + 1) * 128], xnT[:, dc, 0:NC], start=(dc == 0), stop=(dc == 7)),
                                   reads=[wgk, XN], writes=[pmk])
                        for j in JJ:
                            pmt, pmk = pg[j]
                            op("dve", lambda e, j=j, pmt=pmt: e.tensor_tensor(out=hidT[:, fs[j] % 6, 0:NC], in0=upc[j][:, 0:NC], in1=pmt[:, 0:NC], op=ALU.mult), reads=[pmk, uck[j]], writes=[("hidT", fs[j] % 6)])
                    if full:
                        WS.done(16 + 2 * pi_)
                    for _ in range(BGSTEP):
                        next(bg, None)
                if full:
                    for pd in (10, 11):
                        wd, wdk = WS.get(39 + pd)
                        emit_down(pd, wd, wdk)
                        WS.done(39 + pd)
                if full:
                    for ai, (s, half) in enumerate(accs):
                        rows = rows_list[s]
                        op("dve", lambda e, ai=ai, s=s, rows=rows, half=half: e.tensor_tensor(out=xtm[:rows, s, half * 512:(half + 1) * 512], in0=xtm[:rows, s, half * 512:(half + 1) * 512], in1=pc[ai][:rows, :], op=ALU.add),
                           reads=["pc%d" % ai, XT(s)], writes=[XT(s)])
                    for s in range(nt):
                        rows = rows_list[s]
                        op("act", lambda e, s=s, rows=rows: e.activation(out=xsb[:rows, :], in_=xtm[:rows, s, :], func=AF.Square, accum_out=stat[:rows, 4:5]),
                           reads=[XT(s)], writes=["xsb", "stat4"])
                        op("act", lambda e, rows=rows: e.activation(out=stat[:rows, 5:6], in_=stat[:rows, 4:5], func=AF.Ln, scale=1.0 / D, bias=1e-6), reads=["stat4"], writes=["stat5"])
                        op("act", lambda e, rows=rows: e.activation(out=stat[:rows, 6:7], in_=stat[:rows, 5:6], func=AF.Exp, scale=-0.5), reads=["stat5"], writes=["stat6"])
                        op("dve", lambda e, s=s, rows=rows: e.scalar_tensor_tensor(out=xtm[:rows, s, :], in0=xtm[:rows, s, :], scalar=stat[:rows, 6:7], in1=gfbc[:rows, :], op0=ALU.mult, op1=ALU.mult),
                           reads=[XT(s), "stat6", "gfbc"], writes=[XT(s)])
                        if kind == "sample":
                            tk.dma("sp", lambda e: e.dma_start(out=y_s_d, in_=xtm[:64, 0, :]), "yst%d0" % par, reads=[XT(0)])
                        else:
                            tk.dma("sp", lambda e, s=s, r0=prow + s * 128: e.dma_start(out=y_p_d[r0:r0 + 128, :], in_=xtm[:, s, :]), "yst%d%d" % (par, s), reads=[XT(s)])
                last_prompt = (kind == "prompt" and all(t[0] != "prompt" for t in tiles[ti + 1:]))
                if kind == "sample" or last_prompt:
                    tk.dma("sp", lambda e: e.dma_start(out=o_u_d[skind], in_=carry_[:, :, 0:NS, :]), "o_u", reads=[KC])
                    tk.dma("sp", lambda e: e.dma_start(out=o_w_d[skind], in_=Wst_[:, 0:NS, :, :]), "o_w", reads=[KW])
                    tk.dma("sp", lambda e: e.dma_start(out=o_h_d[skind], in_=hst_[:, :, 0:NS], allow_slow_non_contiguous=True), "o_h", reads=[KH])
                    tk.dma("sp", lambda e: e.dma_start(out=o_f_d[skind], in_=fcarry_[:, :, 0:NS, :]), "o_f", reads=[(KF, f_) for f_ in range(24)])


                for _ in bg:
                    pass

            return phase_A, phase_B, phase_C, phase_D

        prows = []
        _p = 0
        for (k_, _, _) in tiles:
            prows.append(_p)
            if k_ == "prompt":
                _p += 256
        tk.dma("sp", lambda e: e.dma_start(out=carry[:], in_=st_u_d), "st_u", writes=["carry_sample"])
        tk.dma("sp", lambda e: e.dma_start(out=hst[:], in_=st_h_d), "st_h", writes=["hst_sample"])
        tk.dma("sp", lambda e: e.dma_start(out=fcarry[:], in_=st_f_d), "st_f", writes=[("fcarry_sample", f_) for f_ in range(24)])
        for _ in front_gen(0, 0):
            pass
        nextA = None
        for ti in range(len(tiles)):
            pA, pB, pC, pD = make_tile(ti)
            if nextA is None:
                for _ in pA():
                    pass
            pB()
            pC()
            if ti == 0:
                tk.dma("sp", lambda e: e.dma_start(out=Wst[:], in_=st_w_d), "st_w", writes=["Wst_sample"])
            if ti + 1 < len(tiles):
                nA = make_tile(ti + 1)
                bg = nA[0]()
                nextA = True
            else:
                bg = iter(())
                nextA = None
            pD(bg, 3)

        tk.final_wait("sp")
        if info is not None:
            info["worder"] = list(WS.rec)
        if WORDER is not None:
            tk.emit()
    return nc


def _chunks(v, n):
    v = np.asarray(v, np.float32).reshape(n, 128)
    return np.ascontiguousarray(v.T)


_NC_CACHE = {}


def kernel(x_prompt, x_sample, state_tm_shift, state_tm_wkv, state_lru_conv, state_lru_h, state_ffn_conv,
           meta_tokens, norm1_g, w_in, tm_mu, tm_w0, tm_w_up, tm_a0, tm_a_up, tm_g_up, tm_k_k, tm_k_a, tm_r_k,
           tm_gn_g, tm_gn_b, lru_conv_w, lru_conv_b, lru_wa, lru_ba, lru_wx, lru_bx, lru_lambda, lru_out_g,
           w_out, norm2_g, ffn_w_up, ffn_w_gate, ffn_conv_w, ffn_conv_b, ffn_w_down, norm_f_g):
    f = lambda a: np.ascontiguousarray(np.asarray(a, np.float32))
    cols = {"mu": _chunks(tm_mu[0], 14), "w0": _chunks(tm_w0[0], 4), "a0": _chunks(tm_a0[0], 4), "k_k": _chunks(tm_k_k[0], 4),
            "k_a": _chunks(tm_k_a[0], 4), "r_k": _chunks(np.asarray(tm_r_k[0]).reshape(-1), 4), "gn_g": _chunks(tm_gn_g[0], 4),
            "gn_b": _chunks(tm_gn_b[0], 4),
            "lcw": np.concatenate([_chunks(lru_conv_w[0][j], 4) for j in range(4)], 1), "lcb": _chunks(lru_conv_b[0], 4),
            "ba": _chunks(lru_ba[0], 4), "bx": _chunks(lru_bx[0], 4), "lam": _chunks(lru_lambda[0], 4), "outg": _chunks(lru_out_g[0], 4),
            "fcw": np.concatenate([_chunks(ffn_conv_w[0][j], 24) for j in range(3)], 1), "fcb": _chunks(ffn_conv_b[0], 24),
            "g1": _chunks(norm1_g[0], 8), "g2": _chunks(norm2_g[0], 8)}
    vecs = np.ascontiguousarray(np.concatenate([cols[n] for n, _ in VEC_SPEC], 1).astype(np.float32))
    assert vecs.shape == (128, NV)

    def bd(w):
        w = np.asarray(w, np.float32)
        out = np.zeros((4, 128, 128), np.float32)
        for c in range(4):
            out[c, 0:64, 0:64] = w[2 * c]
            out[c, 64:128, 64:128] = w[2 * c + 1]
        return out

    eye = np.eye(128, dtype=np.float32)
    su = np.triu(np.ones((128, 128), np.float32), 1)
    cmask = np.stack([eye, su, np.ascontiguousarray(su.T), np.triu(np.ones((128, 128), np.float32), 0)])
    blk = np.zeros((128, 128), np.float32)
    blk[0:64, 0:64] = 1.0
    blk[64:, 64:] = 1.0
    cones = np.stack([blk / 64.0, blk, np.full((128, 128), 1.0 / 512.0, np.float32)])

    shared = {"vecs": vecs, "gf": f(norm_f_g), "w_in": f(w_in[0]), "w_out": f(w_out[0]), "w_upf": f(ffn_w_up[0]),
              "w_gate": f(ffn_w_gate[0]), "w_down": f(ffn_w_down[0]), "tmwup": f(tm_w_up[0]), "tmaup": f(tm_a_up[0]),
              "tmgup": f(tm_g_up[0]), "wabd": bd(lru_wa[0]), "wxbd": bd(lru_wx[0]), "cmask": cmask, "cones": cones,
              "meta": f(meta_tokens)}
    xp = np.asarray(x_prompt, np.float32)
    xs = np.asarray(x_sample, np.float32)
    sh = np.asarray(state_tm_shift, np.float32)[0]
    wk = np.asarray(state_tm_wkv, np.float32)[0]
    lc = np.asarray(state_lru_conv, np.float32)[0]
    lh = np.asarray(state_lru_h, np.float32)[0]
    fcv = np.asarray(state_ffn_conv, np.float32)[0]
    in_maps = []
    for c in range(NCORE):
        bs = slice(16 * c, 16 * c + 16)
        st_u = np.zeros((128, 22, 16, 3), np.float32)
        st_u[:, 0:14, :, 2] = sh[bs].reshape(16, 14, 128).transpose(2, 1, 0)
        st_u[:, 14:18, :, :] = lc[bs].reshape(16, 3, 4, 128).transpose(3, 2, 0, 1)
        st_w = wk[bs].reshape(16, 4, 2, 64, 64).transpose(2, 4, 0, 1, 3).reshape(128, 16, 4, 64)
        st_h = lh[bs].reshape(16, 4, 128).transpose(2, 1, 0)
        st_f = fcv[bs].reshape(16, 2, 24, 128).transpose(3, 2, 0, 1)
        m = dict(shared)
        m.update({"xp": f(xp[c]), "xs": f(xs[bs].reshape(64, D)), "st_u": f(st_u), "st_w": f(st_w), "st_h": f(st_h), "st_f": f(st_f)})
        in_maps.append(m)
    if "nc" not in _NC_CACHE:
        info = {}
        build(info=info)
        _NC_CACHE["nc"] = build(WORDER=info["worder"])
    nc = _NC_CACHE["nc"]
    res = run_bass_kernel_spmd(nc, in_maps, core_ids=list(range(NCORE)))
    R = res.results
    y_prompt = np.stack([R[c]["y_p"] for c in range(NCORE)]).astype(np.float32)
    y_sample = np.concatenate([R[c]["y_s"].reshape(16, 4, D) for c in range(NCORE)], 0).astype(np.float32)

    def unpack(pre, nb):
        shift, wkv, lconv, lhh, fconv = [], [], [], [], []
        for c in range(NCORE):
            ou = R[c][pre + "_u"]
            shift.append(ou[:, 0:14, :, 2].transpose(2, 1, 0).reshape(nb, DTM))
            lconv.append(ou[:, 14:18, :, :].transpose(2, 3, 1, 0).reshape(nb, 3, 512))
            ow = R[c][pre + "_w"]
            wkv.append(ow.reshape(2, 64, nb, 4, 64).transpose(2, 3, 0, 4, 1).reshape(nb, 8, 64, 64))
            lhh.append(R[c][pre + "_h"].transpose(2, 1, 0).reshape(nb, 512))
            fconv.append(R[c][pre + "_f"].transpose(2, 3, 1, 0).reshape(nb, 2, DFF))
        cat = lambda l: np.ascontiguousarray(np.concatenate(l, 0)[None].astype(np.float32))
        return cat(shift), cat(wkv), cat(lconv), cat(lhh), cat(fconv)

    p = unpack("op", 1)
    s = unpack("os", 16)
    return (y_prompt, y_sample) + p + s
```

```python
import numpy as np
from contextlib import ExitStack
import concourse.bass as bass
import concourse.mybir as mybir
from concourse.bass_utils import run_bass_kernel_spmd

F32 = mybir.dt.float32
BF16 = mybir.dt.bfloat16
AF = mybir.ActivationFunctionType
ALU = mybir.AluOpType

COMPUTE = ("pe", "act", "dve", "pool")
QUEUES = ("sp",)
SAME_ENGINE_SYNC = {"pe": False, "act": True, "dve": True, "pool": True, "sp": False}

D = 1024
DTM = 1792
DIN = 2816
DFF = 3072
NCORE = 8
SEQ = 2048
C0 = 0.6065306597126334


class _Rec:
    def __init__(self):
        self.call = None

    def __getattr__(self, name):
        def f(*a, **k):
            self.call = (name, a, k)
            return self
        return f


def _record(fn):
    r = _Rec()
    fn(r)
    assert r.call is not None
    return r.call


class TK:
    def __init__(self, nc, stack):
        self.nc = nc
        self.stack = stack
        self.streams = {e: [] for e in COMPUTE + QUEUES}
        self.count = {e: 0 for e in COMPUTE}
        self.known = {e: {} for e in COMPUTE + QUEUES}
        self.keys = {}
        self.sems = {}
        self.dcount = {}
        self.snaps = {}
        self.n_waits = 0
        for e in COMPUTE:
            self.sems[("eng", e)] = stack.enter_context(nc.semaphore("prog_" + e))

    def _dsem(self, name):
        k = ("dma", name)
        if k not in self.sems:
            self.sems[k] = self.stack.enter_context(self.nc.semaphore("d_" + name))
            self.dcount[name] = 0
        return self.sems[k]

    @staticmethod
    def _flat(keys):
        out = []
        for k in keys:
            if isinstance(k, list):
                out.extend(TK._flat(k))
            else:
                out.append(k)
        return out

    def _deps(self, eng, reads, writes):
        reads = self._flat(reads)
        writes = self._flat(writes)
        need = {}

        def add(d):
            if d is None:
                return
            kind, name, val = d
            if kind == "eng" and name == eng and not SAME_ENGINE_SYNC[eng]:
                return
            k = (kind, name)
            if self.known[eng].get(k, 0) >= val:
                return
            if need.get(k, 0) < val:
                need[k] = val

        for k in reads:
            st = self.keys.get(k)
            if st is not None:
                add(st["w"])
        for k in writes:
            st = self.keys.get(k)
            if st is not None:
                add(st["w"])
                for kk, vv in st["r"].items():
                    add((kk[0], kk[1], vv))
        waits = []
        for k, val in sorted(need.items(), key=lambda kv: -kv[1]):
            if self.known[eng].get(k, 0) >= val:
                continue
            waits.append((self.sems[k], val))
            self.known[eng][k] = val
            sn = self.snaps.get((k[0], k[1], val))
            if sn:
                kn = self.known[eng]
                for kk, vv in sn.items():
                    if kn.get(kk, 0) < vv:
                        kn[kk] = vv
        self.n_waits += len(waits)
        return waits

    def _update(self, mydep, reads, writes):
        reads = self._flat(reads)
        writes = self._flat(writes)
        kind, name, val = mydep
        for k in reads:
            st = self.keys.setdefault(k, {"w": None, "r": {}})
            if st["r"].get((kind, name), 0) < val:
                st["r"][(kind, name)] = val
        for k in writes:
            self.keys[k] = {"w": mydep, "r": {}}

    def op(self, eng, fn, reads=(), writes=()):
        waits = self._deps(eng, reads, writes)
        self.count[eng] += 1
        mydep = ("eng", eng, self.count[eng])
        sn = dict(self.known[eng])
        if SAME_ENGINE_SYNC[eng] is False or True:
            sn[("eng", eng)] = self.count[eng]
        self.snaps[mydep] = sn
        sem = self.sems[("eng", eng)]

        call = _record(fn)

        def closure(e, waits=waits, call=call, sem=sem):
            for s, v in waits:
                e.wait_ge(s, v)
            getattr(e, call[0])(*call[1], **call[2]).then_inc(sem, 1)

        self.streams[eng].append(closure)
        self._update(mydep, reads, writes)

    def dma(self, q, fn, semname, reads=(), writes=()):
        waits = self._deps(q, reads, writes)
        sem = self._dsem(semname)
        self.dcount[semname] += 16
        mydep = ("dma", semname, self.dcount[semname])
        self.snaps[mydep] = dict(self.known[q])

        call = _record(fn)

        def closure(e, waits=waits, call=call, sem=sem):
            for s, v in waits:
                e.wait_ge(s, v)
            getattr(e, call[0])(*call[1], **call[2]).then_inc(sem, 16)

        self.streams[q].append(closure)
        self._update(mydep, reads, writes)

    def final_wait(self, q="sp"):
        waits = []
        for name, c in self.dcount.items():
            if c > 0:
                waits.append((self.sems[("dma", name)], c))
        for e in COMPUTE:
            if self.count[e] > 0:
                waits.append((self.sems[("eng", e)], self.count[e]))

        def closure(e, waits=waits):
            for s, v in waits:
                e.wait_ge(s, v)

        self.streams[q].append(closure)

    def emit(self):
        nc = self.nc
        with nc.Block() as block:
            @block.tensor
            def _(e):
                for c in self.streams["pe"]:
                    c(e)

            @block.scalar
            def _(e):
                for c in self.streams["act"]:
                    c(e)

            @block.vector
            def _(e):
                for c in self.streams["dve"]:
                    c(e)

            @block.gpsimd
            def _(e):
                for c in self.streams["pool"]:
                    c(e)

            @block.sync
            def _(e):
                for c in self.streams["sp"]:
                    c(e)


VEC_SPEC = [("mu", 14), ("w0", 4), ("a0", 4), ("k_k", 4), ("k_a", 4), ("r_k", 4), ("gn_g", 4), ("gn_b", 4),
            ("lcw", 16), ("lcb", 4), ("ba", 4), ("bx", 4), ("lam", 4), ("outg", 4), ("fcw", 72), ("fcb", 24),
            ("g1", 8), ("g2", 8)]
VOFF = {}
_o = 0
for _n, _c in VEC_SPEC:
    VOFF[_n] = _o
    _o += _c
NV = _o

TILES = [("meta", 1, 16)] + [("prompt", 1, 256)] * 8 + [("sample", 16, 4)]
NSLOT = 5


def build(tiles=None, STAGE=99, WORDER=None, info=None):
    import os
    tiles = TILES if tiles is None else tiles
    nc = bass.Bass("TRN2", target_bir_lowering=False)
    di = lambda name, shape: nc.dram_tensor(name, shape, F32, kind="ExternalInput")
    do = lambda name, shape: nc.dram_tensor(name, shape, F32, kind="ExternalOutput")
    xp_d = di("xp", [SEQ, D]).ap()
    meta_d = di("meta", [16, D]).ap()
    xs_d = di("xs", [64, D]).ap()
    st_u_d = di("st_u", [128, 22, 16, 3]).ap()
    st_w_d = di("st_w", [128, 16, 4, 64]).ap()
    st_h_d = di("st_h", [128, 4, 16]).ap()
    st_f_d = di("st_f", [128, 24, 16, 2]).ap()
    vecs_d = di("vecs", [128, NV]).ap()
    gf_t = di("gf", [D])
    w_in_d = di("w_in", [D, DIN]).ap()
    w_out_d = di("w_out", [D, D]).ap()
    w_upf_d = di("w_upf", [D, DFF]).ap()
    w_gate_d = di("w_gate", [D, DFF]).ap()
    w_down_d = di("w_down", [DFF, D]).ap()
    tmwup_d = di("tmwup", [64, 512]).ap()
    tmaup_d = di("tmaup", [64, 512]).ap()
    tmgup_d = di("tmgup", [128, 512]).ap()
    wabd_d = di("wabd", [4, 128, 128]).ap()
    wxbd_d = di("wxbd", [4, 128, 128]).ap()
    cmask_d = di("cmask", [4, 128, 128]).ap()
    cones_d = di("cones", [3, 128, 128]).ap()

    wsc_d = nc.dram_tensor("wsc", [51, 128, 2048], BF16).ap()
    y_p_d = do("y_p", [SEQ, D]).ap()
    y_s_d = do("y_s", [64, D]).ap()
    o_u_d = {"prompt": do("op_u", [128, 22, 1, 3]).ap(), "sample": do("os_u", [128, 22, 16, 3]).ap()}
    o_w_d = {"prompt": do("op_w", [128, 1, 4, 64]).ap(), "sample": do("os_w", [128, 16, 4, 64]).ap()}
    o_h_d = {"prompt": do("op_h", [128, 4, 1]).ap(), "sample": do("os_h", [128, 4, 16]).ap()}
    o_f_d = {"prompt": do("op_f", [128, 24, 1, 2]).ap(), "sample": do("os_f", [128, 24, 16, 2]).ap()}

    with ExitStack() as st:
        tk = TK(nc, st)
        sb = lambda name, shape, dt=F32: st.enter_context(nc.sbuf_tensor("s_" + name, shape, dt))
        ps = lambda name, shape, dt=F32: st.enter_context(nc.psum_tensor("p_" + name, shape, dt))
        op = tk.op

        xtm2 = [sb("xtm%d" % i, [128, 2, D]) for i in range(2)]
        gfbc = sb("gfbc", [128, D])
        xsb = sb("xsb", [128, D], BF16)
        xnT2 = [sb("xnT%d" % i, [128, 8, 256], BF16) for i in range(2)]
        UW = 259
        u = sb("u", [128, 22, UW])
        carry = sb("carry", [128, 22, 16, 3])
        carryP = sb("carryP", [128, 22, 1, 3])
        WstP = sb("WstP", [128, 1, 4, 64])
        hstP = sb("hstP", [128, 4, 1])
        fcarryP = sb("fcarryP", [128, 24, 1, 2])
        ring = [sb("ring%d" % i, [128, 2048], BF16) for i in range(NSLOT)]
        vecs = sb("vecs", [128, NV])
        vder = sb("vder", [128, 8])
        stat = sb("stat", [128, 8])
        tA = sb("tA", [128, 4, 256])
        tK = sb("tK", [128, 4, 256])
        tB = sb("tB", [128, 4, 256])
        tG = sb("tG", [128, 4, 256])
        tGate = sb("tGate", [128, 4, 256])
        tBon = sb("tBon", [128, 4, 256])
        tE4 = sb("tE4", [128, 4, 256])
        tE24 = sb("tE24", [128, 4, 256])
        ones_t = sb("ones_t", [128, 128])
        dC = sb("dC", [128, 4, 16])
        rT = sb("rT", [128, 4, 256], BF16)
        kT = sb("kT", [128, 4, 256], BF16)
        bTm = sb("bTm", [128, 8, 256], BF16)
        ktTm = sb("ktTm", [128, 8, 256], BF16)
        BhFm = sb("BhFm", [128, 8, 256], BF16)
        KhFm = sb("KhFm", [128, 8, 256], BF16)
        Vbm = sb("Vbm", [128, 8, 256], BF16)
        thb = sb("thb", [128, 256], BF16)
        xab = sb("xab", [128, 256], BF16)
        sgb = sb("sgb", [128, 256], BF16)
        xcb = sb("xcb", [128, 4, 256], BF16)
        ycatT = sb("ycatT", [128, 8, 256], BF16)
        hidT = sb("hidT", [128, 6, 256], BF16)
        upbuf = [sb("upbuf%d" % i, [128, 258]) for i in range(2)]
        upc = [sb("upc%d" % i, [128, 256]) for i in range(2)]
        upc2 = [sb("upc2_%d" % i, [128, 256]) for i in range(2)]
        fcarry = sb("fcarry", [128, 24, 16, 2])
        Wst = sb("Wst", [128, 16, 4, 64])
        Wbd = sb("Wbd", [128, 4, 128], BF16)
        hst = sb("hst", [128, 4, 16])
        Pb = [sb("Pb%d" % i, [128, 1024], BF16) for i in range(2)]
        PTb = [sb("PTb%d" % i, [128, 1024], BF16) for i in range(2)]
        Zb = [sb("Zb%d" % i, [128, 1024], BF16) for i in range(2)]
        AkT = sb("AkT", [128, 1024], BF16)
        BrT = sb("BrT", [128, 1024], BF16)
        BkT = sb("BkT", [128, 1024], BF16)
        Vte = sb("Vte", [128, 8, 128], BF16)
        Bte = sb("Bte", [128, 8, 128], BF16)
        Kte = sb("Kte", [128, 8, 128], BF16)
        Xb = sb("Xb", [128, 8, 64], BF16)
        Une = sb("Une", [128, 8, 128], BF16)
        wupb = sb("wupb", [128, 512], BF16)
        aupb = sb("aupb", [128, 512], BF16)
        gupb = sb("gupb", [128, 512], BF16)
        wabd = sb("wabd", [128, 4, 128], BF16)
        wxbd = sb("wxbd", [128, 4, 128], BF16)
        cmask = sb("cmask", [128, 4, 128], BF16)
        cones = sb("cones", [128, 3, 128])
        pm = [ps("pm%d" % i, [128, 512]) for i in range(2)]
        ptb = ps("ptb", [128, 1024], BF16)
        pt32 = ps("pt32", [128, 512])
        pc = [ps("pc%d" % i, [128, 512]) for i in range(4)]
        pmi = [0]
        pci = [0]

        PT32K = ["pt32a", "pt32b"]

        def next_pm():
            pmi[0] ^= 1
            return pm[pmi[0]], ["pm%da" % pmi[0], "pm%db" % pmi[0]]

        def next_pc():
            pci[0] = (pci[0] + 1) % 4
            return pc[pci[0]], "pc%d" % pci[0]

        V = lambda name, j=0, n=1: vecs[:, VOFF[name] + j:VOFF[name] + j + n]

        tk.dma("sp", lambda e: e.dma_start(out=vecs[:], in_=vecs_d), "c_vecs", writes=["vecs"])
        tk.dma("sp", lambda e: e.dma_start(out=gfbc[:], in_=bass.AP(gf_t, 0, [[0, 128], [1, D]])), "c_gf", writes=["gfbc"])
        tk.dma("sp", lambda e: e.dma_start(out=cones[:], in_=cones_d.rearrange("c p j -> p c j")), "c_ones", writes=["cones"])
        tk.dma("pool", lambda e: e.dma_start(out=wupb[0:64, :], in_=tmwup_d), "c_w1", writes=["wupb"])
        tk.dma("pool", lambda e: e.dma_start(out=aupb[64:128, :], in_=tmaup_d), "c_w2", writes=["aupb"])
        tk.dma("pool", lambda e: e.dma_start(out=gupb[:], in_=tmgup_d), "c_w3", writes=["gupb"])
        tk.dma("pool", lambda e: e.dma_start(out=wabd[:], in_=wabd_d.rearrange("c p j -> p c j")), "c_w4", writes=["wabd"])
        tk.dma("pool", lambda e: e.dma_start(out=wxbd[:], in_=wxbd_d.rearrange("c p j -> p c j")), "c_w5", writes=["wxbd"])
        tk.dma("pool", lambda e: e.dma_start(out=cmask[:], in_=cmask_d.rearrange("c p j -> p c j")), "c_w6", writes=["cmask"])
        ident = cmask[:, 0, :]
        m_su = cmask[:, 1, :]
        m_sl = cmask[:, 2, :]
        m_ui = cmask[:, 3, :]
        blkavg = cones[:, 0, :]
        blkones = cones[:, 1, :]
        ones512 = cones[:, 2, :]
        op("dve", lambda e: e.memset(ones_t[:], 1.0), writes=["ones_t"])
        for (bf_, nm_) in ((bTm, "bTm"), (ktTm, "ktTm"), (BhFm, "BhFm"), (KhFm, "KhFm"), (Vbm, "Vbm")):
            op("pool", lambda e, bf_=bf_: e.memset(bf_[:], 0.0), writes=[(nm_, h) for h in range(8)])
        op("pool", lambda e: e.memset(Wbd[:], 0.0), writes=["Wbd"])
        op("pool", lambda e: e.memset(Une[:], 0.0), writes=["Une"])
        op("pool", lambda e: e.memset(wupb[64:128, :], 0.0), writes=["wupb"])
        op("pool", lambda e: e.memset(aupb[0:64, :], 0.0), writes=["aupb"])
        op("dve", lambda e: e.tensor_scalar(out=vder[:, 0:4], in0=V("k_a", 0, 4), scalar1=-1.0, scalar2=1.0, op0=ALU.mult, op1=ALU.add),
           reads=["vecs"], writes=["vder"])
        op("act", lambda e: e.activation(out=vder[:, 4:8], in_=V("lam", 0, 4), func=AF.Exp, scale=-1.0), reads=["vecs", "vder"], writes=["vder"])
        op("act", lambda e: e.activation(out=vder[:, 4:8], in_=vder[:, 4:8], func=AF.Ln, bias=1.0), reads=["vder"], writes=["vder"])
        op("dve", lambda e: e.tensor_scalar(out=vder[:, 4:8], in0=vder[:, 4:8], scalar1=-8.0, scalar2=None, op0=ALU.mult), reads=["vder"], writes=["vder"])

        converted = set()

        class GWS:
            def __init__(self, order, ahead=4):
                self.order = order
                self.ahead = ahead
                self.rec = []
                self.pos = 0
                self.nissued = 0
                self.free = list(range(NSLOT))
                self.inst = {}
                self.live = {}

            def _issue(self, seq, pidx):
                slot = self.free.pop(0)
                a, b = WSRC[pidx][1]
                view = ring[slot][:, 0:a * b].rearrange("p (a b) -> p a b", a=a)
                key = "ring%d" % slot
                if pidx not in converted:
                    converted.add(pidx)
                    tk.dma("pool", lambda e: e.dma_start(out=view, in_=WSRC[pidx][0]), "ringS%d" % slot, writes=[key])
                    tk.dma("sp", lambda e: e.dma_start(out=wsc_d[pidx], in_=ring[slot][:, :]), "ringW%d" % slot, reads=[key], writes=[("wsc", pidx)])
                else:
                    tk.dma("sp", lambda e: e.dma_start(out=ring[slot][:, :], in_=wsc_d[pidx]), "ringH%d" % slot, reads=[("wsc", pidx)], writes=[key])
                self.inst[seq] = (view, key, slot, pidx)

            def get(self, pidx):
                seq = self.pos
                self.pos += 1
                self.rec.append(pidx)
                if self.order is not None:
                    assert self.order[seq] == pidx, (seq, pidx, self.order[seq])
                while self.nissued <= seq:
                    assert self.free, "weight ring exhausted"
                    self._issue(self.nissued, pidx if self.order is None else self.order[self.nissued])
                    self.nissued += 1
                if self.order is not None:
                    while self.nissued < len(self.order) and self.nissued <= seq + self.ahead and len(self.free) > 0:
                        self._issue(self.nissued, self.order[self.nissued])
                        self.nissued += 1
                view, key, slot, _ = self.inst[seq]
                self.live.setdefault(pidx, []).append(seq)
                return view, key

            def done(self, pidx):
                seq = self.live[pidx].pop(0)
                self.free.append(self.inst[seq][2])

        def weight_items_src():
            items = []
            w_in_v = w_in_d.rearrange("(dc p) e -> p dc e", p=128)
            for i in range(11):
                items.append((w_in_v[:, :, i * 256:(i + 1) * 256], (8, 256)))
            w_out_v = w_out_d.rearrange("(ec p) d -> p ec d", p=128)
            for i in range(4):
                items.append((w_out_v[:, 2 * i:2 * i + 2, :], (2, 1024)))
            w_up_v = w_upf_d.rearrange("(dc p) f -> p dc f", p=128)
            w_gate_v = w_gate_d.rearrange("(dc p) f -> p dc f", p=128)
            for i in range(12):
                items.append((w_up_v[:, :, i * 256:(i + 1) * 256], (8, 256)))
                items.append((w_gate_v[:, :, i * 256:(i + 1) * 256], (8, 256)))
            w_down_v = w_down_d.rearrange("(fc p) d -> p fc d", p=128)
            for i in range(12):
                items.append((w_down_v[:, 2 * i:2 * i + 2, :], (2, 1024)))
            return items

        WSRC = weight_items_src()
        WS = GWS(WORDER)

        def rms_gen(par, nt, rows_list, gname):
            xtm_ = xtm2[par]
            xnT_ = xnT2[par]
            for s in range(nt):
                rows = rows_list[s]
                xk = ("xtm", par, s)
                op("act", lambda e, s=s, rows=rows: e.activation(out=xsb[:rows, :], in_=xtm_[:rows, s, :], func=AF.Square, accum_out=stat[:rows, 0:1]),
                   reads=[xk], writes=["xsb", "stat"])
                op("act", lambda e, rows=rows: e.activation(out=stat[:rows, 1:2], in_=stat[:rows, 0:1], func=AF.Ln, scale=1.0 / D, bias=1e-6),
                   reads=["stat"], writes=["stat1"])
                op("act", lambda e, rows=rows: e.activation(out=stat[:rows, 2:3], in_=stat[:rows, 1:2], func=AF.Exp, scale=-0.5), reads=["stat1"], writes=["stat2"])
                yield
                op("act", lambda e, s=s, rows=rows: e.activation(out=xsb[:rows, :], in_=xtm_[:rows, s, :], func=AF.Copy, scale=stat[:rows, 2:3]),
                   reads=[xk, "stat2"], writes=["xsb"])
                for dc in range(8):
                    op("pe", lambda e, dc=dc, rows=rows: e.transpose(ptb[:, dc * 128:dc * 128 + rows], xsb[:rows, dc * 128:(dc + 1) * 128], ident[:rows, :rows]),
                       reads=["xsb", "cmask"], writes=["ptb"])
                yield
                pv = ptb[:, :].rearrange("p (a b) -> p a b", a=8)[:, :, 0:rows]
                gsc = V(gname, 0, 8)
                gbc = bass.AP(gsc.tensor, gsc.offset, [gsc.ap[0], [1, 8], [0, rows]])
                op("dve", lambda e, s=s, rows=rows, pv=pv, gbc=gbc: e.tensor_tensor(out=xnT_[:, :, s * 128:s * 128 + rows], in0=pv, in1=gbc, op=ALU.mult),
                   reads=["ptb", "vecs"], writes=[("xnT", par)])
                yield

        def tile_geom(t):
            kind_, NS_, L_ = t
            NC_ = NS_ * L_
            nt_ = (NC_ + 127) // 128
            return NC_, nt_, [min(128, NC_ - s_ * 128) for s_ in range(nt_)]

        def front_gen(tidx, prow_):
            par = tidx % 2
            kind_ = tiles[tidx][0]
            NC_, nt_, rows_ = tile_geom(tiles[tidx])
            xtm_ = xtm2[par]
            if kind_ == "sample":
                tk.dma("sp", lambda e: e.dma_start(out=xtm_[:64, 0, :], in_=xs_d), "xld%d0" % par, writes=[("xtm", par, 0)])
            elif kind_ == "meta":
                tk.dma("sp", lambda e: e.dma_start(out=xtm_[:16, 0, :], in_=meta_d), "xld%d0" % par, writes=[("xtm", par, 0)])
            else:
                for s_ in range(2):
                    tk.dma("sp", lambda e, s_=s_, r0=prow_ + s_ * 128: e.dma_start(out=xtm_[:, s_, :], in_=xp_d[r0:r0 + 128, :]), "xld%d%d" % (par, s_), writes=[("xtm", par, s_)])
            yield
            for _ in rms_gen(par, nt_, rows_, "g1"):
                yield

        def make_tile(ti):
            kind, NS, L = tiles[ti]
            NC = NS * L
            C = min(L, 128)
            NCHK = L // C
            nt = (NC + 127) // 128
            rows_list = [min(128, NC - s * 128) for s in range(nt)]
            skind = "sample" if kind == "sample" else "prompt"
            W3 = 3 + L
            par = ti % 2
            xtm = xtm2[par]
            xnT = xnT2[par]
            XN = ("xnT", par)
            XT = lambda s_: ("xtm", par, s_)

            def Uv(e_, lo, hi):
                return u[:, e_, 0:NS * W3].rearrange("p (s w) -> p s w", s=NS)[:, :, lo:hi]

            def Fv(buf, fc):
                return buf[:, fc, 0:NC].rearrange("p (s l) -> p s l", s=NS)

            def F2(buf2d):
                return buf2d[:, 0:NC].rearrange("p (s l) -> p s l", s=NS)


            R_ = lambda fc: Uv(fc, 3, W3)
            K_ = lambda fc: Uv(4 + fc, 3, W3)
            V_ = lambda fc: Uv(8 + fc, 3, W3)
            E = lambda fc: tE4[:, fc, 0:NC]
            E2 = lambda fc: tE24[:, fc, 0:NC]
            E3 = lambda fc: tE4[:, fc, 0:NC].rearrange("p (s l) -> p s l", s=NS)
            E23 = lambda fc: tE24[:, fc, 0:NC].rearrange("p (s l) -> p s l", s=NS)
            FC = range(4)
            ukeys = [("u", e_) for e_ in range(22)]
            prow = prows[ti]
            nprow = prows[ti + 1] if ti + 1 < len(tiles) else 0

            if skind == "sample":
                carry_, Wst_, hst_, fcarry_ = carry, Wst, hst, fcarry
            else:
                carry_, Wst_, hst_, fcarry_ = carryP, WstP, hstP, fcarryP
            KC, KW, KH, KF = "carry_" + skind, "Wst_" + skind, "hst_" + skind, "fcarry_" + skind

            def phase_A():
                ukeys = [("u", e_) for e_ in range(22)]
                if kind == "meta":
                    op("dve", lambda e: e.memset(carry_[:], 0.0), reads=[KC], writes=[KC])
                    op("dve", lambda e: e.memset(Wst_[:], 0.0), reads=[KW], writes=[KW])
                    op("dve", lambda e: e.memset(hst_[:], 0.0), reads=[KH], writes=[KH])
                    op("dve", lambda e: e.memset(fcarry_[:], 0.0), reads=[(KF, f_) for f_ in range(24)], writes=[(KF, f_) for f_ in range(24)])
                uh = u[:, :, 0:NS * W3].rearrange("p e (s w) -> p e s w", s=NS)[:, :, :, 0:3]
                for e0 in range(0, 22, 11):
                    op("dve", lambda e, e0=e0: e.tensor_copy(out=u[:, e0:e0 + 11, 0:NS * W3].rearrange("p e (s w) -> p e s w", s=NS)[:, :, :, 0:3],
                                                             in_=carry_[:, e0:e0 + 11, 0:NS, :]),
                       reads=[KC], writes=ukeys[e0:e0 + 11])


                for pi_ in range(11):
                    wv, wk = WS.get(pi_)
                    for j in range(2):
                        e_ = pi_ * 2 + j
                        pmt, pmk = next_pm()
                        for dc in range(8):
                            op("pe", lambda e, wv=wv, dc=dc, j=j, pmt=pmt: e.matmul(pmt[:, 0:NC], wv[:, dc, j * 128:(j + 1) * 128], xnT[:, dc, 0:NC], start=(dc == 0), stop=(dc == 7)),
                               reads=[wk, XN], writes=[pmk])
                        op("act", lambda e, e_=e_, pmt=pmt: e.copy(out=Uv(e_, 3, W3), in_=pmt[:, 0:NC].rearrange("p (s l) -> p s l", s=NS)),
                           reads=[pmk], writes=[("u", e_)])
                    WS.done(pi_)
                    yield
                for e0 in range(0, 22, 11):
                    op("dve", lambda e, e0=e0: e.tensor_copy(out=carry_[:, e0:e0 + 11, 0:NS, :],
                                                             in_=u[:, e0:e0 + 11, 0:NS * W3].rearrange("p e (s w) -> p e s w", s=NS)[:, :, :, L:L + 3]),
                       reads=ukeys[e0:e0 + 11], writes=[KC])

                tEs = [tE4[:, 0, :], tE4[:, 1, :]]
                yield
                for e0 in range(0, 14, 2):
                    for q in range(2):
                        e_ = e0 + q
                        op("pool", lambda e, e_=e_, q=q: e.tensor_tensor(out=tEs[q][:, 0:NC].rearrange("p (s l) -> p s l", s=NS), in0=Uv(e_, 2, 2 + L), in1=Uv(e_, 3, W3), op=ALU.subtract),
                           reads=[("u", e_)], writes=[("tE", q)])
                    for q in range(2):
                        e_ = e0 + q
                        op("dve", lambda e, e_=e_, q=q: e.scalar_tensor_tensor(out=Uv(e_, 3, W3), in0=tEs[q][:, 0:NC].rearrange("p (s l) -> p s l", s=NS), scalar=V("mu", e_), in1=Uv(e_, 3, W3), op0=ALU.mult, op1=ALU.add),
                           reads=[("tE", q), ("u", e_), "vecs"], writes=[("u", e_)])
                yield
                op("act", lambda e: e.activation(out=F2(thb), in_=Uv(12, 3, W3), func=AF.Tanh), reads=[("u", 12)], writes=["thb"])
                yield
                op("dve", lambda e: e.tensor_copy(out=F2(xab), in_=Uv(12, 3, W3)), reads=[("u", 12)], writes=["xab"])
                yield
                op("act", lambda e: e.activation(out=F2(sgb), in_=Uv(13, 3, W3), func=AF.Sigmoid), reads=[("u", 13)], writes=["sgb"])
                yield
                for fc in range(4):
                    cs = slice(fc * 128, (fc + 1) * 128)
                    pmt, pmk = next_pm()
                    op("pe", lambda e, pmt=pmt, cs=cs: e.matmul(pmt[:, 0:NC], wupb[:, cs], thb[:, 0:NC], start=True, stop=True), reads=["wupb", "thb"], writes=[pmk])
                    op("act", lambda e, pmt=pmt, fc=fc: e.activation(out=tG[:, fc, 0:NC], in_=pmt[:, 0:NC], func=AF.Sigmoid, bias=V("w0", fc)), reads=[pmk, "vecs"], writes=[("tG", fc)])
                    pmt, pmk = next_pm()
                    op("pe", lambda e, pmt=pmt, cs=cs: e.matmul(pmt[:, 0:NC], aupb[:, cs], xab[:, 0:NC], start=True, stop=True), reads=["aupb", "xab"], writes=[pmk])
                    op("act", lambda e, pmt=pmt, fc=fc: e.activation(out=tA[:, fc, 0:NC], in_=pmt[:, 0:NC], func=AF.Sigmoid, bias=V("a0", fc)), reads=[pmk, "vecs"], writes=[("tA", fc)])
                    pmt, pmk = next_pm()
                    op("pe", lambda e, pmt=pmt, cs=cs: e.matmul(pmt[:, 0:NC], gupb[:, cs], sgb[:, 0:NC], start=True, stop=True), reads=["gupb", "sgb"], writes=[pmk])
                    op("act", lambda e, pmt=pmt, fc=fc: e.copy(out=tGate[:, fc, 0:NC], in_=pmt[:, 0:NC]), reads=[pmk], writes=[("tGate", fc)])
                yield
                for fc in FC:
                    op("pool", lambda e, fc=fc: e.tensor_scalar(out=Fv(tK, fc), in0=K_(fc), scalar1=V("k_k", fc), scalar2=0.0, op0=ALU.mult, op1=ALU.add),
                       reads=[("u", 4 + fc), "vecs"], writes=[("tK", fc)])
                yield
                for fc in FC:
                    op("pool", lambda e, fc=fc: e.tensor_tensor(out=E(fc), in0=tK[:, fc, 0:NC], in1=tK[:, fc, 0:NC], op=ALU.mult), reads=[("tK", fc)], writes=[("tE", fc)])
                yield
                for fc in FC:
                    pmt, pmk = next_pm()
                    op("pe", lambda e, pmt=pmt, fc=fc: e.matmul(pmt[:, 0:NC], blkones, E(fc), start=True, stop=True), reads=["cones", ("tE", fc)], writes=[pmk])
                    op("dve", lambda e, pmt=pmt, fc=fc: e.tensor_scalar(out=E2(fc), in0=pmt[:, 0:NC], scalar1=1e-24, scalar2=None, op0=ALU.max), reads=[pmk], writes=[("tE2", fc)])
                yield
                for fc in FC:
                    op("act", lambda e, fc=fc: e.activation(out=E2(fc), in_=E2(fc), func=AF.Ln), reads=[("tE2", fc)], writes=[("tE2", fc)])
                yield
                for fc in FC:
                    op("act", lambda e, fc=fc: e.activation(out=E2(fc), in_=E2(fc), func=AF.Exp, scale=-0.5), reads=[("tE2", fc)], writes=[("tE2", fc)])
                yield
                for fc in FC:
                    op("pool", lambda e, fc=fc: e.tensor_scalar(out=E(fc), in0=tA[:, fc, 0:NC], scalar1=V("k_a", fc), scalar2=vder[:, fc:fc + 1], op0=ALU.mult, op1=ALU.add),
                       reads=[("tA", fc), "vecs", "vder", ("tE", fc)], writes=[("tE", fc)])
                yield
                for fc in FC:
                    op("dve", lambda e, fc=fc: e.tensor_tensor(out=tK[:, fc, 0:NC], in0=tK[:, fc, 0:NC], in1=E2(fc), op=ALU.mult), reads=[("tK", fc), ("tE2", fc)], writes=[("tK", fc)])
                yield
                for fc in FC:
                    op("dve", lambda e, fc=fc: e.tensor_tensor(out=K_(fc), in0=K_(fc), in1=E3(fc), op=ALU.mult), reads=[("u", 4 + fc), ("tE", fc)], writes=[("u", 4 + fc)])
                yield
                for fc in FC:
                    op("pool", lambda e, fc=fc: e.tensor_tensor(out=tB[:, fc, 0:NC], in0=tK[:, fc, 0:NC], in1=tA[:, fc, 0:NC], op=ALU.mult), reads=[("tK", fc), ("tA", fc)], writes=[("tB", fc)])
                yield
                for fc in FC:
                    op("dve", lambda e, fc=fc: e.scalar_tensor_tensor(out=E3(fc), in0=R_(fc), scalar=V("r_k", fc), in1=K_(fc), op0=ALU.mult, op1=ALU.mult),
                       reads=[("u", fc), ("u", 4 + fc), "vecs", ("tE", fc)], writes=[("tE", fc)])
                yield
                for fc in FC:
                    pmt, pmk = next_pm()
                    op("pe", lambda e, pmt=pmt, fc=fc: e.matmul(pmt[:, 0:NC], blkones, E(fc), start=True, stop=True), reads=["cones", ("tE", fc)], writes=[pmk])
                    op("dve", lambda e, pmt=pmt, fc=fc: e.tensor_tensor(out=Fv(tBon, fc), in0=pmt[:, 0:NC].rearrange("p (s l) -> p s l", s=NS), in1=V_(fc), op=ALU.mult),
                       reads=[pmk, ("u", 8 + fc)], writes=[("tBon", fc)])
                yield
                for b in range(NS):
                    for ch in range(NCHK):
                        c0 = b * L + ch * C
                        for fc in FC:
                            op("dve", lambda e, fc=fc, c0=c0: e.tensor_tensor_scan(out=tE4[:, fc, c0:c0 + C], data0=ones_t[:, 0:C], data1=tG[:, fc, c0:c0 + C], initial=0.0, op0=ALU.mult, op1=ALU.add),
                               reads=[("tG", fc), "ones_t", ("tE", fc)], writes=[("tE", fc)])
                yield
                for fc in FC:
                    op("pool", lambda e, fc=fc: e.tensor_tensor(out=E2(fc), in0=E(fc), in1=tG[:, fc, 0:NC], op=ALU.subtract), reads=[("tE", fc), ("tG", fc), ("tE2", fc)], writes=[("tE2", fc)])
                yield
                for fc in FC:
                    op("act", lambda e, fc=fc: e.activation(out=E2(fc), in_=E2(fc), func=AF.Exp, scale=-C0), reads=[("tE2", fc)], writes=[("tE2", fc)])
                yield
                for fc in FC:
                    op("dve", lambda e, fc=fc: e.tensor_tensor(out=kT[:, fc, 0:NC], in0=tK[:, fc, 0:NC], in1=E2(fc), op=ALU.mult), reads=[("tK", fc), ("tE2", fc)], writes=[("kT", fc)])
                yield
                for fc in FC:
                    op("act", lambda e, fc=fc: e.activation(out=E2(fc), in_=E(fc), func=AF.Exp, scale=-C0), reads=[("tE", fc), ("tE2", fc)], writes=[("tE2", fc)])
                yield
                for fc in FC:
                    op("dve", lambda e, fc=fc: e.tensor_tensor(out=Fv(rT, fc), in0=R_(fc), in1=E23(fc), op=ALU.mult), reads=[("u", fc), ("tE2", fc)], writes=[("rT", fc)])
                yield
                for fc in FC:
                    e1v = tE24[:, fc, 0:NC].rearrange("p (n c) -> p n c", c=C)[:, :, C - 1:C]
                    op("pool", lambda e, fc=fc, e1v=e1v: e.tensor_copy(out=dC[:, fc, 0:NS * NCHK].rearrange("p (n o) -> p n o", o=1), in_=e1v), reads=[("tE2", fc)], writes=[("dC", fc)])
                yield
                for fc in FC:
                    op("act", lambda e, fc=fc: e.activation(out=E2(fc), in_=E(fc), func=AF.Exp, scale=C0), reads=[("tE", fc), ("tE2", fc)], writes=[("tE2", fc)])
                HH = [(2 * fc + h2, fc, slice(64 * h2, 64 * h2 + 64)) for fc in range(4) for h2 in range(2)]
                yield
                for (h, fc, rs) in HH:
                    op("dve", lambda e, fc=fc, h=h, rs=rs: e.tensor_tensor(out=bTm[rs, h, 0:NC], in0=tB[rs, fc, 0:NC], in1=tE24[rs, fc, 0:NC], op=ALU.mult),
                       reads=[("tB", fc), ("tE2", fc)], writes=[("bTm", h)])
                yield
                for (h, fc, rs) in HH:
                    op("dve", lambda e, fc=fc, h=h, rs=rs: e.tensor_tensor(out=ktTm[rs, h, 0:NC].rearrange("p (s l) -> p s l", s=NS), in0=K_(fc)[rs], in1=E23(fc)[rs], op=ALU.mult),
                       reads=[("u", 4 + fc), ("tE2", fc)], writes=[("ktTm", h)])
                yield
                for (h, fc, rs) in HH:
                    op("act", lambda e, fc=fc, h=h, rs=rs: e.copy(out=Vbm[rs, h, 0:NC].rearrange("p (s l) -> p s l", s=NS), in_=V_(fc)[rs]), reads=[("u", 8 + fc)], writes=[("Vbm", h)])
                yield
                for (h, fc, rs) in HH:
                    dcb = dC[rs, fc, 0:1]
                    dcb = bass.AP(dcb.tensor, dcb.offset, [dcb.ap[0], [1, NS * NCHK], [0, C]])
                    op("pool", lambda e, h=h, rs=rs, dcb=dcb: e.tensor_tensor(out=BhFm[rs, h, 0:NC].rearrange("p (n c) -> p n c", c=C), in0=bTm[rs, h, 0:NC].rearrange("p (n c) -> p n c", c=C), in1=dcb, op=ALU.mult),
                       reads=[("bTm", h), ("dC", fc)], writes=[("BhFm", h)])
                yield
                for (h, fc, rs) in HH:
                    dcb = dC[rs, fc, 0:1]
                    dcb = bass.AP(dcb.tensor, dcb.offset, [dcb.ap[0], [1, NS * NCHK], [0, C]])
                    op("pool", lambda e, h=h, rs=rs, dcb=dcb: e.tensor_tensor(out=KhFm[rs, h, 0:NC].rearrange("p (n c) -> p n c", c=C), in0=ktTm[rs, h, 0:NC].rearrange("p (n c) -> p n c", c=C), in1=dcb, op=ALU.mult),
                       reads=[("ktTm", h), ("dC", fc)], writes=[("KhFm", h)])

                yield

            def phase_B():
                fr_it = front_gen(ti + 1, nprow) if ti + 1 < len(tiles) else iter(())
                def lru_gen():
                    yield
                    for fc in range(4):
                        ex = 14 + fc
                        op("act", lambda e, fc=fc, ex=ex: e.activation(out=Fv(tK, fc), in_=Uv(ex, 3, W3), func=AF.Identity, scale=V("lcw", 12 + fc), bias=V("lcb", fc)),
                           reads=[("u", ex), "vecs", ("tK", fc)], writes=[("tK", fc)])
                        for j in range(3):
                            op("dve", lambda e, fc=fc, ex=ex, j=j: e.scalar_tensor_tensor(out=Fv(tK, fc), in0=Uv(ex, j, j + L), scalar=V("lcw", j * 4 + fc), in1=Fv(tK, fc), op0=ALU.mult, op1=ALU.add),
                               reads=[("u", ex), "vecs", ("tK", fc)], writes=[("tK", fc)])
                        op("act", lambda e, fc=fc: e.copy(out=xcb[:, fc, 0:NC], in_=tK[:, fc, 0:NC]), reads=[("tK", fc)], writes=[("xcb", fc)])
                    yield
                    for fc in range(4):
                        pmt, pmk = next_pm()
                        op("pe", lambda e, pmt=pmt, fc=fc: e.matmul(pmt[:, 0:NC], wabd[:, fc, :], xcb[:, fc, 0:NC], start=True, stop=True), reads=["wabd", ("xcb", fc)], writes=[pmk])
                        op("act", lambda e, pmt=pmt, fc=fc: e.activation(out=tB[:, fc, 0:NC], in_=pmt[:, 0:NC], func=AF.Sigmoid, bias=V("ba", fc)), reads=[pmk, "vecs", ("tB", fc)], writes=[("tB", fc)])
                        pmt, pmk = next_pm()
                        op("pe", lambda e, pmt=pmt, fc=fc: e.matmul(pmt[:, 0:NC], wxbd[:, fc, :], xcb[:, fc, 0:NC], start=True, stop=True), reads=["wxbd", ("xcb", fc)], writes=[pmk])
                        op("act", lambda e, pmt=pmt, fc=fc: e.activation(out=tG[:, fc, 0:NC], in_=pmt[:, 0:NC], func=AF.Sigmoid, bias=V("bx", fc)), reads=[pmk, "vecs", ("tG", fc)], writes=[("tG", fc)])
                    yield
                    for fc in range(4):
                        op("pool", lambda e, fc=fc: e.tensor_tensor(out=tG[:, fc, 0:NC], in0=tG[:, fc, 0:NC], in1=tK[:, fc, 0:NC], op=ALU.mult), reads=[("tG", fc), ("tK", fc)], writes=[("tG", fc)])
                    yield
                    for fc in range(4):
                        op("act", lambda e, fc=fc: e.activation(out=tB[:, fc, 0:NC], in_=tB[:, fc, 0:NC], func=AF.Exp, scale=vder[:, 4 + fc:5 + fc]), reads=[("tB", fc), "vder"], writes=[("tB", fc)])
                    yield
                    for fc in range(4):
                        op("pool", lambda e, fc=fc: e.tensor_tensor(out=tE4[:, fc, 0:NC], in0=tB[:, fc, 0:NC], in1=tB[:, fc, 0:NC], op=ALU.mult), reads=[("tB", fc), ("tE", fc)], writes=[("tE", fc)])
                    yield
                    for fc in range(4):
                        op("act", lambda e, fc=fc: e.activation(out=tE4[:, fc, 0:NC], in_=tE4[:, fc, 0:NC], func=AF.Ln, scale=-1.0, bias=1.0), reads=[("tE", fc)], writes=[("tE", fc)])
                    yield
                    for fc in range(4):
                        op("act", lambda e, fc=fc: e.activation(out=tE4[:, fc, 0:NC], in_=tE4[:, fc, 0:NC], func=AF.Exp, scale=0.5), reads=[("tE", fc)], writes=[("tE", fc)])
                    yield
                    for fc in range(4):
                        if kind == "meta":
                            op("dve", lambda e, fc=fc: e.memset(tE4[:, fc, 0:1], 1.0), reads=[("tE", fc)], writes=[("tE", fc)])
                        op("dve", lambda e, fc=fc: e.tensor_tensor(out=tG[:, fc, 0:NC], in0=tG[:, fc, 0:NC], in1=tE4[:, fc, 0:NC], op=ALU.mult), reads=[("tG", fc), ("tE", fc)], writes=[("tG", fc)])
                    yield
                    for fc in range(4):
                        for b in range(NS):
                            op("dve", lambda e, fc=fc, b=b: e.tensor_tensor_scan(out=tE4[:, fc, b * L:(b + 1) * L], data0=tB[:, fc, b * L:(b + 1) * L], data1=tG[:, fc, b * L:(b + 1) * L],
                                                                                initial=hst_[:, fc, b:b + 1], op0=ALU.mult, op1=ALU.add),
                               reads=[("tB", fc), ("tG", fc), KH, ("tE", fc)], writes=[("tE", fc)])
                        op("dve", lambda e, fc=fc: e.tensor_copy(out=hst_[:, fc, 0:NS].rearrange("p (s o) -> p s o", o=1), in_=Fv(tE4, fc)[:, :, L - 1:L]), reads=[("tE", fc), KH], writes=[KH])
                    yield
                    for fc in range(4):
                        eg = 18 + fc
                        op("act", lambda e, fc=fc, eg=eg: e.activation(out=Uv(eg, 3, W3), in_=Uv(eg, 3, W3), func=AF.Gelu_apprx_tanh), reads=[("u", eg)], writes=[("u", eg)])
                    yield
                    for fc in range(4):
                        eg = 18 + fc
                        op("dve", lambda e, fc=fc, eg=eg: e.tensor_tensor(out=Fv(tK, fc), in0=Fv(tE4, fc), in1=Uv(eg, 3, W3), op=ALU.mult), reads=[("tE", fc), ("u", eg), ("tK", fc)], writes=[("tK", fc)])
                        op("pool", lambda e, fc=fc: e.tensor_tensor(out=tE4[:, fc, 0:NC], in0=tK[:, fc, 0:NC], in1=tK[:, fc, 0:NC], op=ALU.mult), reads=[("tK", fc), ("tE", fc)], writes=[("tE", fc)])
                    yield
                    for fc in range(4):
                        op("pe", lambda e, fc=fc: e.matmul(pt32[:, 0:NC], ones512, tE4[:, fc, 0:NC], start=(fc == 0), stop=(fc == 3)), reads=["cones", ("tE", fc)], writes=[PT32K])
                    yield
                    op("act", lambda e: e.activation(out=E2(0), in_=pt32[:, 0:NC], func=AF.Ln, bias=1e-6), reads=[PT32K, ("tE2", 0)], writes=[("tE2", 0)])
                    op("act", lambda e: e.activation(out=E2(0), in_=E2(0), func=AF.Exp, scale=-0.5), reads=[("tE2", 0)], writes=[("tE2", 0)])
                    yield
                    for fc in range(4):
                        op("dve", lambda e, fc=fc: e.scalar_tensor_tensor(out=ycatT[:, 4 + fc, 0:NC], in0=tK[:, fc, 0:NC], scalar=V("outg", fc), in1=E2(0), op0=ALU.mult, op1=ALU.mult),
                           reads=[("tK", fc), ("tE2", 0), "vecs"], writes=[("ycatT", 4 + fc)])


                    yield

                lru_it = lru_gen()

                lru_cnt = [0]

                def lru_step(n=1):
                    for _ in range(n):
                        next(lru_it, None)
                    lru_cnt[0] += 1
                    if lru_cnt[0] >= 3:
                        next(fr_it, None)

                nlev = int(np.log2(C)) - 1
                fk = lambda nm: [(nm, fc) for fc in range(4)]
                lanes = [(b, ch) for b in range(NS) for ch in range(NCHK)]
                LPB = max(1, 1024 // (8 * C))
                assert (len(lanes) + LPB - 1) // LPB <= 2
                UPB = 512 // C

                def lane_c0(li):
                    b_, ch_ = lanes[li]
                    return b_ * L + ch_ * C

                def ubuf(li):
                    return li // LPB

                def uoff(li, h):
                    return ((li % LPB) * 8 + h) * C

                def mask_bc(m, n):
                    return bass.AP(m.tensor, m.offset, [[m.ap[0][0], C], [0, n], [1, C]])

                def unit_banks(lane_ids):
                    units = [(li, h) for li in lane_ids for h in range(8)]
                    return [units[i:i + UPB] for i in range(0, len(units), UPB)]

                def bank_key(nm, us):
                    return (nm, us[0][0], us[0][1])

                def emit_product(us, fn_ops, evac):
                    pct, pck = next_pc()
                    for ui_, (li, h) in enumerate(us):
                        lap, rap, rkeys = fn_ops(li, h)
                        op("pe", lambda e, pct=pct, ui_=ui_, lap=lap, rap=rap: e.matmul(pct[:C, ui_ * C:(ui_ + 1) * C], lap, rap, start=True, stop=True), reads=rkeys, writes=[pck])
                    evac(pct, pck, us)

                def sb_view(buf_list, us):
                    li0, h0 = us[0]
                    o0 = uoff(li0, h0)
                    return buf_list[ubuf(li0)][:C, o0:o0 + len(us) * C]

                def masked_evac(dst_list, dk, msk):
                    def ev(pct, pck, us):
                        n = len(us)
                        op("dve", lambda e: e.tensor_tensor(out=sb_view(dst_list, us).rearrange("p (u c) -> p u c", c=C),
                                                            in0=pct[:C, 0:n * C].rearrange("p (u c) -> p u c", c=C), in1=mask_bc(msk, n), op=ALU.mult),
                           reads=[pck, "cmask"], writes=[bank_key(dk, us)])
                    return ev

                def fm_ops(la, lm, ra, rm, lk, rk):
                    def f(li, h):
                        fc = h // 2
                        cs_ = slice(lane_c0(li), lane_c0(li) + C)
                        lap = la[:, h, cs_] if lm else la[:, fc, cs_]
                        rap = ra[:, h, cs_] if rm else ra[:, fc, cs_]
                        return lap, rap, [(lk, h if lm else fc), (rk, h if rm else fc)]
                    return f

                all_banks = unit_banks(range(len(lanes)))
                for us in all_banks:
                    emit_product(us, fm_ops(bTm, True, kT, False, "bTm", "kT"), masked_evac(Pb, "P", m_su))
                    emit_product(us, fm_ops(kT, False, bTm, True, "kT", "bTm"), masked_evac(PTb, "PT", m_sl))
                    lru_step()
                for us in all_banks:
                    n = len(us)
                    op("dve", lambda e, us=us, n=n: e.tensor_tensor(out=sb_view(Zb, us).rearrange("p (u c) -> p u c", c=C), in0=mask_bc(ident, n),
                                                                  in1=sb_view(Pb, us).rearrange("p (u c) -> p u c", c=C), op=ALU.subtract),
                       reads=[bank_key("P", us), "cmask"], writes=[bank_key("Z", us)])
                for lev in range(1, nlev + 1):
                    last = (lev == nlev)
                    for us in all_banks:
                        kP, kPT, kZ = bank_key("P", us), bank_key("PT", us), bank_key("Z", us)
                        pct, pck = next_pc()
                        for ui_, (li, h) in enumerate(us):
                            sl = slice(uoff(li, h), uoff(li, h) + C)
                            bi = ubuf(li)
                            op("pe", lambda e, pct=pct, ui_=ui_, sl=sl, bi=bi: e.matmul(pct[:C, ui_ * C:(ui_ + 1) * C], Pb[bi][:C, sl], PTb[bi][:C, sl], start=True, stop=True),
                               reads=[kP, kPT], writes=[pck])
                        if not last:
                            pct2, pck2 = next_pc()
                            for ui_, (li, h) in enumerate(us):
                                sl = slice(uoff(li, h), uoff(li, h) + C)
                                bi = ubuf(li)
                                op("pe", lambda e, pct2=pct2, ui_=ui_, sl=sl, bi=bi: e.matmul(pct2[:C, ui_ * C:(ui_ + 1) * C], PTb[bi][:C, sl], Pb[bi][:C, sl], start=True, stop=True),
                                   reads=[kP, kPT], writes=[pck2])
                        n = len(us)
                        op("act", lambda e, pct=pct, us=us, n=n: e.copy(out=sb_view(PTb, us), in_=pct[:C, 0:n * C]), reads=[pck, kPT], writes=[kPT])
                        if not last:
                            if all_banks.index(us) % 2 == 0:
                                op("act", lambda e, pct2=pct2, us=us, n=n: e.copy(out=sb_view(Pb, us), in_=pct2[:C, 0:n * C]), reads=[pck2, kP], writes=[kP])
                            else:
                                op("dve", lambda e, pct2=pct2, us=us, n=n: e.tensor_copy(out=sb_view(Pb, us), in_=pct2[:C, 0:n * C]), reads=[pck2, kP], writes=[kP])
                    for us in all_banks:
                        kPT, kZ = bank_key("PT", us), bank_key("Z", us)
                        n = len(us)
                        pct3, pck3 = next_pc()
                        for ui_, (li, h) in enumerate(us):
                            sl = slice(uoff(li, h), uoff(li, h) + C)
                            bi = ubuf(li)
                            op("pe", lambda e, pct3=pct3, ui_=ui_, sl=sl, bi=bi: e.matmul(pct3[:C, ui_ * C:(ui_ + 1) * C], PTb[bi][:C, sl], Zb[bi][:C, sl], start=True, stop=True),
                               reads=[kPT, kZ], writes=[pck3])
                        op("dve", lambda e, pct3=pct3, us=us, n=n: e.tensor_tensor(out=sb_view(Zb, us), in0=pct3[:C, 0:n * C], in1=sb_view(Zb, us), op=ALU.add),
                           reads=[pck3, kZ], writes=[kZ])
                    lru_step()

                Alist = lambda t: [t, t]
                for li, (b, ch) in enumerate(lanes):
                    c0 = lane_c0(li)
                    cs = slice(c0, c0 + C)
                    lbanks = unit_banks([li])
                    zkeys = [bank_key("Z", us) for us in all_banks if any(u_[0] == li for u_ in us)]
                    for (srcb, dst, nm, dnm) in ((Vbm, Vte, "Vbm", "Vte"), (BhFm, Bte, "BhFm", "Bte"), (KhFm, Kte, "KhFm", "Kte")):
                        for h in range(8):
                            op("pe", lambda e, srcb=srcb, h=h: e.transpose(ptb[:C, h * 128:(h + 1) * 128], srcb[:, h, cs], ident),
                               reads=[(nm, h), "cmask"], writes=["ptb"])
                        op("act", lambda e, dst=dst: e.copy(out=dst[:C, :, :], in_=ptb[:C, :].rearrange("p (h f) -> p h f", h=8)), reads=["ptb"], writes=[dnm])
                    def a_evac(dst, dk, msk):
                        def ev(pct, pck, us):
                            n = len(us)
                            o0 = us[0][1] * C
                            op("dve", lambda e: e.tensor_tensor(out=dst[:C, o0:o0 + n * C].rearrange("p (u c) -> p u c", c=C),
                                                                in0=pct[:C, 0:n * C].rearrange("p (u c) -> p u c", c=C), in1=mask_bc(msk, n), op=ALU.mult),
                               reads=[pck, "cmask"], writes=[(dk, us[0][1])])
                        return ev
                    akeys = {}
                    for us in lbanks:
                        emit_product(us, fm_ops(ktTm, True, kT, False, "ktTm", "kT"), a_evac(AkT, "AkT", m_su))
                        emit_product(us, fm_ops(bTm, True, rT, False, "bTm", "rT"), a_evac(BrT, "BrT", m_ui))
                        emit_product(us, fm_ops(ktTm, True, rT, False, "ktTm", "rT"), a_evac(BkT, "BkT", m_ui))
                        for (_, h) in us:
                            akeys[h] = us[0][1]
                    lru_step()
                    for h2 in range(2):
                        rs = slice(64 * h2, 64 * h2 + 64)
                        op("pool", lambda e, rs=rs, h2=h2: e.tensor_copy(out=Wbd[rs, :, 64 * h2:64 * h2 + 64], in_=Wst_[rs, b, :, :]), reads=[KW], writes=["Wbd"])
                    pct, pck = next_pc()
                    for h in range(8):
                        fc, h2 = divmod(h, 2)
                        vs = slice(64 * h2, 64 * h2 + 64)
                        op("pe", lambda e, pct=pct, h=h, fc=fc, vs=vs: e.matmul(pct[:C, h * 64:(h + 1) * 64], kT[:, fc, cs], Wbd[:, fc, vs], start=True, stop=False),
                           reads=[("kT", fc), "Wbd"], writes=[pck])
                        op("pe", lambda e, pct=pct, h=h, vs=vs: e.matmul(pct[:C, h * 64:(h + 1) * 64], AkT[:C, h * C:(h + 1) * C], Vte[:C, h, vs], start=False, stop=True),
                           reads=[("AkT", akeys[h]), "Vte"], writes=[pck])
                    op("act", lambda e, pct=pct: e.copy(out=Xb[:C, :, :], in_=pct[:C, :].rearrange("p (u v) -> p u v", v=64)), reads=[pck], writes=["Xb"])
                    lru_step()
                    pct, pck = next_pc()
                    for h in range(8):
                        zsl = slice(uoff(li, h), uoff(li, h) + C)
                        op("pe", lambda e, pct=pct, h=h, zsl=zsl: e.matmul(pct[:C, h * 64:(h + 1) * 64], Zb[ubuf(li)][:C, zsl], Xb[:C, h, :], start=True, stop=True),
                           reads=zkeys + ["Xb"], writes=[pck])
                    une0 = Une[:C, 0, 0:1]
                    une_d = bass.AP(une0.tensor, une0.offset, [[une0.ap[0][0], C], [256, 4], [192, 2], [1, 64]])
                    op("act", lambda e, pct=pct, une_d=une_d: e.activation(out=une_d, in_=pct[:C, :].rearrange("p (f t v) -> p f t v", f=4, t=2), func=AF.Copy, scale=-1.0),
                       reads=[pck], writes=["Une"])
                    lru_step()
                    pct, pck = next_pc()
                    for fc in range(4):
                        o_ = pct[:, fc * C:(fc + 1) * C]
                        op("pe", lambda e, o_=o_, fc=fc: e.matmul(o_, Wbd[:, fc, :], rT[:, fc, cs], start=True, stop=False), reads=["Wbd", ("rT", fc)], writes=[pck])
                        for h2 in range(2):
                            h = 2 * fc + h2
                            op("pe", lambda e, o_=o_, h=h: e.matmul(o_, Une[:C, h, :], BrT[:C, h * C:(h + 1) * C], start=False, stop=False),
                               reads=["Une", ("BrT", akeys[h])], writes=[pck])
                            op("pe", lambda e, o_=o_, h=h, h2=h2: e.matmul(o_, Vte[:C, h, :], BkT[:C, h * C:(h + 1) * C], start=False, stop=(h2 == 1)),
                               reads=["Vte", ("BkT", akeys[h])], writes=[pck])
                    op("dve", lambda e, pct=pct: e.tensor_copy(out=tA[:, :, cs], in_=pct[:, 0:4 * C].rearrange("p (f c) -> p f c", c=C)),
                       reads=[pck], writes=fk("tA"))
                    lru_step()
                    pct, pck = next_pc()
                    for fc in range(4):
                        o_ = pct[:, fc * 64:(fc + 1) * 64]
                        for h2 in range(2):
                            h = 2 * fc + h2
                            vs = slice(64 * h2, 64 * h2 + 64)
                            op("pe", lambda e, o_=o_, h=h, vs=vs, h2=h2: e.matmul(o_, Bte[:C, h, :], Une[:C, h, vs], start=(h2 == 0), stop=False),
                               reads=["Bte", "Une"], writes=[pck])
                            op("pe", lambda e, o_=o_, h=h, vs=vs, h2=h2: e.matmul(o_, Kte[:C, h, :], Vte[:C, h, vs], start=False, stop=(h2 == 1)),
                               reads=["Kte", "Vte"], writes=[pck])
                    ci = b * NCHK + ch
                    for fc in range(4):
                        op("dve", lambda e, pct=pct, fc=fc: e.scalar_tensor_tensor(out=Wst_[:, b, fc, :], in0=Wst_[:, b, fc, :], scalar=dC[:, fc, ci:ci + 1],
                                                                                in1=pct[:, fc * 64:(fc + 1) * 64], op0=ALU.mult, op1=ALU.add),
                           reads=[pck, ("dC", fc), KW, "Wbd"], writes=[KW])
                for _ in lru_it:
                    pass
                for _ in fr_it:
                    pass
                outproj_part((4, 5, 6, 7))
                for fc in FC:
                    pmt, pmk = next_pm()
                    op("pe", lambda e, pmt=pmt, fc=fc: e.matmul(pmt[:, 0:NC], blkavg, tA[:, fc, 0:NC], start=True, stop=True), reads=["cones", ("tA", fc)], writes=[pmk])
                    op("dve", lambda e, pmt=pmt, fc=fc: e.tensor_tensor(out=E(fc), in0=tA[:, fc, 0:NC], in1=pmt[:, 0:NC], op=ALU.subtract), reads=[pmk, ("tA", fc), ("tE", fc)], writes=[("tE", fc)])
                for fc in FC:
                    op("act", lambda e, fc=fc: e.activation(out=E2(fc), in_=E(fc), func=AF.Square), reads=[("tE", fc), ("tE2", fc)], writes=[("tE2", fc)])
                for fc in FC:
                    pmt, pmk = next_pm()
                    op("pe", lambda e, pmt=pmt, fc=fc: e.matmul(pmt[:, 0:NC], blkavg, E2(fc), start=True, stop=True), reads=["cones", ("tE2", fc)], writes=[pmk])
                    op("act", lambda e, pmt=pmt, fc=fc: e.activation(out=E2(fc), in_=pmt[:, 0:NC], func=AF.Ln, bias=64e-5), reads=[pmk, ("tE2", fc)], writes=[("tE2", fc)])
                for fc in FC:
                    op("act", lambda e, fc=fc: e.activation(out=E2(fc), in_=E2(fc), func=AF.Exp, scale=-0.5), reads=[("tE2", fc)], writes=[("tE2", fc)])
                for fc in FC:
                    op("dve", lambda e, fc=fc: e.tensor_tensor(out=E(fc), in0=E(fc), in1=E2(fc), op=ALU.mult), reads=[("tE", fc), ("tE2", fc)], writes=[("tE", fc)])
                for fc in FC:
                    op("pool", lambda e, fc=fc: e.tensor_scalar(out=E(fc), in0=E(fc), scalar1=V("gn_g", fc), scalar2=V("gn_b", fc), op0=ALU.mult, op1=ALU.add), reads=[("tE", fc), "vecs"], writes=[("tE", fc)])
                for fc in FC:
                    op("pool", lambda e, fc=fc: e.tensor_tensor(out=E(fc), in0=E(fc), in1=tBon[:, fc, 0:NC], op=ALU.add), reads=[("tE", fc), ("tBon", fc)], writes=[("tE", fc)])
                for fc in FC:
                    op("dve", lambda e, fc=fc: e.tensor_tensor(out=ycatT[:, fc, 0:NC], in0=E(fc), in1=tGate[:, fc, 0:NC], op=ALU.mult), reads=[("tE", fc), ("tGate", fc)], writes=[("ycatT", fc)])

                for _ in lru_it:
                    pass


            shared = {}

            def outproj_part(ecs):
                if "wo" not in shared:
                    shared["wo"] = [WS.get(11 + i) for i in range(4)]
                wo = shared["wo"]
                groups = [(s, half) for s in range(nt) for half in range(2)]
                if True:
                    for gi, (s, half) in enumerate(groups):
                        rows = rows_list[s]
                        for ec in ecs:
                            wv, wk = wo[ec // 2]
                            op("pe", lambda e, gi=gi, s=s, rows=rows, ec=ec, wv=wv, half=half: e.matmul(pc[gi][:rows, :], ycatT[:, ec, s * 128:s * 128 + rows], wv[:, ec % 2, half * 512:(half + 1) * 512], start=(ec == 4), stop=(ec == 3)),
                               reads=[("ycatT", ec), wk], writes=["pc%d" % gi])

            def phase_C():
                outproj_part((0, 1, 2, 3))
                groups = [(s, half) for s in range(nt) for half in range(2)]
                for gi, (s, half) in enumerate(groups):
                    rows = rows_list[s]
                    op("dve", lambda e, gi=gi, s=s, rows=rows, half=half: e.tensor_tensor(out=xtm[:rows, s, half * 512:(half + 1) * 512], in0=xtm[:rows, s, half * 512:(half + 1) * 512], in1=pc[gi][:rows, :], op=ALU.add),
                       reads=["pc%d" % gi, XT(s)], writes=[XT(s)])
                for i in range(4):
                    WS.done(11 + i)
                for _ in rms_gen(par, nt, rows_list, "g2"):
                    pass


            def phase_D(bg, BGSTEP):

                W2 = 2 + L
                accs = [(s, half) for s in range(nt) for half in range(2)]
                full = (kind != "meta")

                def emit_down(pd, wd, wdk):
                    for j in range(2):
                        f = pd * 2 + j
                        for ai, (s, half) in enumerate(accs):
                            rows = rows_list[s]
                            op("pe", lambda e, ai=ai, s=s, rows=rows, half=half, f=f, j=j: e.matmul(pc[ai][:rows, :], hidT[:, f % 6, s * 128:s * 128 + rows], wd[:, j, half * 512:(half + 1) * 512], start=(f == 0), stop=(f == 23)),
                               reads=[("hidT", f % 6), wdk], writes=["pc%d" % ai])

                for pi_ in range(12):
                    wu, wuk = WS.get(15 + 2 * pi_)
                    if full:
                        wg, wgk = WS.get(16 + 2 * pi_)
                        if pi_ >= 2:
                            wd, wdk = WS.get(39 + pi_ - 2)
                    JJ = range(2)
                    fs = [pi_ * 2 + j for j in JJ]
                    ubv = [upbuf[j][:, 0:NS * W2].rearrange("p (s w) -> p s w", s=NS) for j in JJ]
                    ubk = ["upbuf%d" % j for j in JJ]
                    ucv = [upc[j][:, 0:NC].rearrange("p (s l) -> p s l", s=NS) for j in JJ]
                    uc2v = [upc2[j][:, 0:NC].rearrange("p (s l) -> p s l", s=NS) for j in JJ]
                    uc2k = ["upc2_%d" % j for j in JJ]
                    uck = ["upc%d" % j for j in JJ]
                    pu = []
                    for j in JJ:
                        pmt, pmk = pm[pi_ % 2][:, j * 256:(j + 1) * 256], ["pm%da" % (pi_ % 2), "pm%db" % (pi_ % 2)]
                        pu.append((pmt, pmk))
                        for dc in range(8):
                            op("pe", lambda e, pmt=pmt, dc=dc, j=j: e.matmul(pmt[:, 0:NC], wu[:, dc, j * 128:(j + 1) * 128], xnT[:, dc, 0:NC], start=(dc == 0), stop=(dc == 7)),
                               reads=[wuk, XN], writes=[pmk])
                    if full and pi_ >= 2:
                        emit_down(pi_ - 2, wd, wdk)
                        WS.done(39 + pi_ - 2)
                    WS.done(15 + 2 * pi_)
                    for j in JJ:
                        op("pool", lambda e, j=j: e.tensor_copy(out=ubv[j][:, :, 0:2], in_=fcarry_[:, fs[j], 0:NS, :]), reads=[(KF, fs[j]), ubk[j]], writes=[ubk[j]])
                    for j in JJ:
                        pmt, pmk = pu[j]
                        op("act", lambda e, j=j, pmt=pmt: e.copy(out=ubv[j][:, :, 2:W2], in_=pmt[:, 0:NC].rearrange("p (s l) -> p s l", s=NS)), reads=[pmk, ubk[j]], writes=[ubk[j]])
                    for j in JJ:
                        op("pool", lambda e, j=j: e.tensor_copy(out=fcarry_[:, fs[j], 0:NS, :], in_=ubv[j][:, :, L:L + 2]), reads=[ubk[j], (KF, fs[j])], writes=[(KF, fs[j])])
                    if full:
                        for j in JJ:
                            pmt, pmk = pu[j]
                            op("act", lambda e, j=j, pmt=pmt: e.activation(out=upc[j][:, 0:NC], in_=pmt[:, 0:NC], func=AF.Identity, scale=V("fcw", 48 + fs[j]), bias=V("fcb", fs[j])),
                               reads=[pmk, "vecs", uck[j]], writes=[uck[j]])
                        for j in JJ:
                            op("pool", lambda e, j=j: e.tensor_scalar(out=uc2v[j], in0=ubv[j][:, :, 0:L], scalar1=V("fcw", fs[j]), scalar2=0.0, op0=ALU.mult, op1=ALU.add),
                               reads=[ubk[j], "vecs", uc2k[j]], writes=[uc2k[j]])
                        for j in JJ:
                            op("dve", lambda e, j=j: e.scalar_tensor_tensor(out=ucv[j], in0=ubv[j][:, :, 1:1 + L], scalar=V("fcw", 24 + fs[j]), in1=ucv[j], op0=ALU.mult, op1=ALU.add),
                               reads=[ubk[j], "vecs", uck[j]], writes=[uck[j]])
                        for j in JJ:
                            op("pool", lambda e, j=j: e.tensor_tensor(out=ucv[j], in0=ucv[j], in1=uc2v[j], op=ALU.add), reads=[uck[j], uc2k[j]], writes=[uck[j]])
                        for j in JJ:
                            op("act", lambda e, j=j: e.activation(out=upc[j][:, 0:NC], in_=upc[j][:, 0:NC], func=AF.Gelu_apprx_tanh), reads=[uck[j]], writes=[uck[j]])
                        pg = []
                        for j in JJ:
                            pmt, pmk = pt32[:, j * 256:(j + 1) * 256], PT32K
                            pg.append((pmt, pmk))
                            for dc in range(8):
                                op("pe", lambda e, pmt=pmt, dc=dc, j=j: e.matmul(pmt[:, 0:NC], wg[:, dc, j * 128:(j + 1) * 128], xnT[:, dc, 0:NC], start=(dc == 0), stop=(dc == 7)),
                                   reads=[wgk, XN], writes=[pmk])
                        for j in JJ:
                            pmt, pmk = pg[j]
                            op("dve", lambda e, j=j, pmt=pmt: e.tensor_tensor(out=hidT[:, fs[j] % 6, 0:NC], in0=upc[j][:, 0:NC], in1=pmt[:, 0:NC], op=ALU.mult), reads=[pmk, uck[j]], writes=[("hidT", fs[j] % 6)])
                    if full:
                        WS.done(16 + 2 * pi_)
                    for _ in range(BGSTEP):
                        next(bg, None)
                if full:
                    for pd in (10, 11):
                        wd, wdk = WS.get(39 + pd)
                        emit_down(pd, wd, wdk)
                        WS.done(39 + pd)
                if full:
                    for ai, (s, half) in enumerate(accs):
                        rows = rows_list[s]
                        op("dve", lambda e, ai=ai, s=s, rows=rows, half=half: e.tensor_tensor(out=xtm[:rows, s, half * 512:(half + 1) * 512], in0=xtm[:rows, s, half * 512:(half + 1) * 512], in1=pc[ai][:rows, :], op=ALU.add),
                           reads=["pc%d" % ai, XT(s)], writes=[XT(s)])
                    for s in range(nt):
                        rows = rows_list[s]
                        op("act", lambda e, s=s, rows=rows: e.activation(out=xsb[:rows, :], in_=xtm[:rows, s, :], func=AF.Square, accum_out=stat[:rows, 4:5]),
                           reads=[XT(s)], writes=["xsb", "stat4"])
                        op("act", lambda e, rows=rows: e.activation(out=stat[:rows, 5:6], in_=stat[:rows, 4:5], func=AF.Ln, scale=1.0 / D, bias=1e-6), reads=["stat4"], writes=["stat5"])
                        op("act", lambda e, rows=rows: e.activation(out=stat[:rows, 6:7], in_=stat[:rows, 5:6], func=AF.Exp, scale=-0.5), reads=["stat5"], writes=["stat6"])
                        op("dve", lambda e, s=s, rows=rows: e.scalar_tensor_tensor(out=xtm[:rows, s, :], in0=xtm[:rows, s, :], scalar=stat[:rows, 6:7], in1=gfbc[:rows, :], op0=ALU.mult, op1=ALU.mult),
                           reads=[XT(s), "stat6", "gfbc"], writes=[XT(s)])
                        if kind == "sample":
                            tk.dma("sp", lambda e: e.dma_start(out=y_s_d, in_=xtm[:64, 0, :]), "yst%d0" % par, reads=[XT(0)])
                        else:
                            tk.dma("sp", lambda e, s=s, r0=prow + s * 128: e.dma_start(out=y_p_d[r0:r0 + 128, :], in_=xtm[:, s, :]), "yst%d%d" % (par, s), reads=[XT(s)])
                last_prompt = (kind == "prompt" and all(t[0] != "prompt" for t in tiles[ti + 1:]))
                if kind == "sample" or last_prompt:
                    tk.dma("sp", lambda e: e.dma_start(out=o_u_d[skind], in_=carry_[:, :, 0:NS, :]), "o_u", reads=[KC])
                    tk.dma("sp", lambda e: e.dma_start(out=o_w_d[skind], in_=Wst_[:, 0:NS, :, :]), "o_w", reads=[KW])
                    tk.dma("sp", lambda e: e.dma_start(out=o_h_d[skind], in_=hst_[:, :, 0:NS], allow_slow_non_contiguous=True), "o_h", reads=[KH])
                    tk.dma("sp", lambda e: e.dma_start(out=o_f_d[skind], in_=fcarry_[:, :, 0:NS, :]), "o_f", reads=[(KF, f_) for f_ in range(24)])


                for _ in bg:
                    pass

            return phase_A, phase_B, phase_C, phase_D

        prows = []
        _p = 0
        for (k_, _, _) in tiles:
            prows.append(_p)
            if k_ == "prompt":
                _p += 256
        tk.dma("sp", lambda e: e.dma_start(out=carry[:], in_=st_u_d), "st_u", writes=["carry_sample"])
        tk.dma("sp", lambda e: e.dma_start(out=hst[:], in_=st_h_d), "st_h", writes=["hst_sample"])
        tk.dma("sp", lambda e: e.dma_start(out=fcarry[:], in_=st_f_d), "st_f", writes=[("fcarry_sample", f_) for f_ in range(24)])
        for _ in front_gen(0, 0):
            pass
        nextA = None
        for ti in range(len(tiles)):
            pA, pB, pC, pD = make_tile(ti)
            if nextA is None:
                for _ in pA():
                    pass
            pB()
            pC()
            if ti == 0:
                tk.dma("sp", lambda e: e.dma_start(out=Wst[:], in_=st_w_d), "st_w", writes=["Wst_sample"])
            if ti + 1 < len(tiles):
                nA = make_tile(ti + 1)
                bg = nA[0]()
                nextA = True
            else:
                bg = iter(())
                nextA = None
            pD(bg, 3)

        tk.final_wait("sp")
        if info is not None:
            info["worder"] = list(WS.rec)
        if WORDER is not None:
            tk.emit()
    return nc


def _chunks(v, n):
    v = np.asarray(v, np.float32).reshape(n, 128)
    return np.ascontiguousarray(v.T)


_NC_CACHE = {}


def kernel(x_prompt, x_sample, state_tm_shift, state_tm_wkv, state_lru_conv, state_lru_h, state_ffn_conv,
           meta_tokens, norm1_g, w_in, tm_mu, tm_w0, tm_w_up, tm_a0, tm_a_up, tm_g_up, tm_k_k, tm_k_a, tm_r_k,
           tm_gn_g, tm_gn_b, lru_conv_w, lru_conv_b, lru_wa, lru_ba, lru_wx, lru_bx, lru_lambda, lru_out_g,
           w_out, norm2_g, ffn_w_up, ffn_w_gate, ffn_conv_w, ffn_conv_b, ffn_w_down, norm_f_g):
    f = lambda a: np.ascontiguousarray(np.asarray(a, np.float32))
    cols = {"mu": _chunks(tm_mu[0], 14), "w0": _chunks(tm_w0[0], 4), "a0": _chunks(tm_a0[0], 4), "k_k": _chunks(tm_k_k[0], 4),
            "k_a": _chunks(tm_k_a[0], 4), "r_k": _chunks(np.asarray(tm_r_k[0]).reshape(-1), 4), "gn_g": _chunks(tm_gn_g[0], 4),
            "gn_b": _chunks(tm_gn_b[0], 4),
            "lcw": np.concatenate([_chunks(lru_conv_w[0][j], 4) for j in range(4)], 1), "lcb": _chunks(lru_conv_b[0], 4),
            "ba": _chunks(lru_ba[0], 4), "bx": _chunks(lru_bx[0], 4), "lam": _chunks(lru_lambda[0], 4), "outg": _chunks(lru_out_g[0], 4),
            "fcw": np.concatenate([_chunks(ffn_conv_w[0][j], 24) for j in range(3)], 1), "fcb": _chunks(ffn_conv_b[0], 24),
            "g1": _chunks(norm1_g[0], 8), "g2": _chunks(norm2_g[0], 8)}
    vecs = np.ascontiguousarray(np.concatenate([cols[n] for n, _ in VEC_SPEC], 1).astype(np.float32))
    assert vecs.shape == (128, NV)

    def bd(w):
        w = np.asarray(w, np.float32)
        out = np.zeros((4, 128, 128), np.float32)
        for c in range(4):
            out[c, 0:64, 0:64] = w[2 * c]
            out[c, 64:128, 64:128] = w[2 * c + 1]
        return out

    eye = np.eye(128, dtype=np.float32)
    su = np.triu(np.ones((128, 128), np.float32), 1)
    cmask = np.stack([eye, su, np.ascontiguousarray(su.T), np.triu(np.ones((128, 128), np.float32), 0)])
    blk = np.zeros((128, 128), np.float32)
    blk[0:64, 0:64] = 1.0
    blk[64:, 64:] = 1.0
    cones = np.stack([blk / 64.0, blk, np.full((128, 128), 1.0 / 512.0, np.float32)])

    shared = {"vecs": vecs, "gf": f(norm_f_g), "w_in": f(w_in[0]), "w_out": f(w_out[0]), "w_upf": f(ffn_w_up[0]),
              "w_gate": f(ffn_w_gate[0]), "w_down": f(ffn_w_down[0]), "tmwup": f(tm_w_up[0]), "tmaup": f(tm_a_up[0]),
              "tmgup": f(tm_g_up[0]), "wabd": bd(lru_wa[0]), "wxbd": bd(lru_wx[0]), "cmask": cmask, "cones": cones,
              "meta": f(meta_tokens)}
    xp = np.asarray(x_prompt, np.float32)
    xs = np.asarray(x_sample, np.float32)
    sh = np.asarray(state_tm_shift, np.float32)[0]
    wk = np.asarray(state_tm_wkv, np.float32)[0]
    lc = np.asarray(state_lru_conv, np.float32)[0]
    lh = np.asarray(state_lru_h, np.float32)[0]
    fcv = np.asarray(state_ffn_conv, np.float32)[0]
    in_maps = []
    for c in range(NCORE):
        bs = slice(16 * c, 16 * c + 16)
        st_u = np.zeros((128, 22, 16, 3), np.float32)
        st_u[:, 0:14, :, 2] = sh[bs].reshape(16, 14, 128).transpose(2, 1, 0)
        st_u[:, 14:18, :, :] = lc[bs].reshape(16, 3, 4, 128).transpose(3, 2, 0, 1)
        st_w = wk[bs].reshape(16, 4, 2, 64, 64).transpose(2, 4, 0, 1, 3).reshape(128, 16, 4, 64)
        st_h = lh[bs].reshape(16, 4, 128).transpose(2, 1, 0)
        st_f = fcv[bs].reshape(16, 2, 24, 128).transpose(3, 2, 0, 1)
        m = dict(shared)
        m.update({"xp": f(xp[c]), "xs": f(xs[bs].reshape(64, D)), "st_u": f(st_u), "st_w": f(st_w), "st_h": f(st_h), "st_f": f(st_f)})
        in_maps.append(m)
    if "nc" not in _NC_CACHE:
        info = {}
        build(info=info)
        _NC_CACHE["nc"] = build(WORDER=info["worder"])
    nc = _NC_CACHE["nc"]
    res = run_bass_kernel_spmd(nc, in_maps, core_ids=list(range(NCORE)))
    R = res.results
    y_prompt = np.stack([R[c]["y_p"] for c in range(NCORE)]).astype(np.float32)
    y_sample = np.concatenate([R[c]["y_s"].reshape(16, 4, D) for c in range(NCORE)], 0).astype(np.float32)

    def unpack(pre, nb):
        shift, wkv, lconv, lhh, fconv = [], [], [], [], []
        for c in range(NCORE):
            ou = R[c][pre + "_u"]
            shift.append(ou[:, 0:14, :, 2].transpose(2, 1, 0).reshape(nb, DTM))
            lconv.append(ou[:, 14:18, :, :].transpose(2, 3, 1, 0).reshape(nb, 3, 512))
            ow = R[c][pre + "_w"]
            wkv.append(ow.reshape(2, 64, nb, 4, 64).transpose(2, 3, 0, 4, 1).reshape(nb, 8, 64, 64))
            lhh.append(R[c][pre + "_h"].transpose(2, 1, 0).reshape(nb, 512))
            fconv.append(R[c][pre + "_f"].transpose(2, 3, 1, 0).reshape(nb, 2, DFF))
        cat = lambda l: np.ascontiguousarray(np.concatenate(l, 0)[None].astype(np.float32))
        return cat(shift), cat(wkv), cat(lconv), cat(lhh), cat(fconv)

    p = unpack("op", 1)
    s = unpack("os", 16)
    return (y_prompt, y_sample) + p + s
```

```python
import numpy as np
from contextlib import ExitStack
import concourse.bass as bass
import concourse.mybir as mybir
from concourse.bass_utils import run_bass_kernel_spmd

F32 = mybir.dt.float32
BF16 = mybir.dt.bfloat16
AF = mybir.ActivationFunctionType
ALU = mybir.AluOpType

COMPUTE = ("pe", "act", "dve", "pool")
QUEUES = ("sp",)
SAME_ENGINE_SYNC = {"pe": False, "act": True, "dve": True, "pool": True, "sp": False}

D = 1024
DTM = 1792
DIN = 2816
DFF = 3072
NCORE = 8
SEQ = 2048
C0 = 0.6065306597126334


class _Rec:
    def __init__(self):
        self.call = None

    def __getattr__(self, name):
        def f(*a, **k):
            self.call = (name, a, k)
            return self
        return f


def _record(fn):
    r = _Rec()
    fn(r)
    assert r.call is not None
    return r.call


class TK:
    def __init__(self, nc, stack):
        self.nc = nc
        self.stack = stack
        self.streams = {e: [] for e in COMPUTE + QUEUES}
        self.count = {e: 0 for e in COMPUTE}
        self.known = {e: {} for e in COMPUTE + QUEUES}
        self.keys = {}
        self.sems = {}
        self.dcount = {}
        self.snaps = {}
        self.n_waits = 0
        for e in COMPUTE:
            self.sems[("eng", e)] = stack.enter_context(nc.semaphore("prog_" + e))

    def _dsem(self, name):
        k = ("dma", name)
        if k not in self.sems:
            self.sems[k] = self.stack.enter_context(self.nc.semaphore("d_" + name))
            self.dcount[name] = 0
        return self.sems[k]

    @staticmethod
    def _flat(keys):
        out = []
        for k in keys:
            if isinstance(k, list):
                out.extend(TK._flat(k))
            else:
                out.append(k)
        return out

    def _deps(self, eng, reads, writes):
        reads = self._flat(reads)
        writes = self._flat(writes)
        need = {}

        def add(d):
            if d is None:
                return
            kind, name, val = d
            if kind == "eng" and name == eng and not SAME_ENGINE_SYNC[eng]:
                return
            k = (kind, name)
            if self.known[eng].get(k, 0) >= val:
                return
            if need.get(k, 0) < val:
                need[k] = val

        for k in reads:
            st = self.keys.get(k)
            if st is not None:
                add(st["w"])
        for k in writes:
            st = self.keys.get(k)
            if st is not None:
                add(st["w"])
                for kk, vv in st["r"].items():
                    add((kk[0], kk[1], vv))
        waits = []
        for k, val in sorted(need.items(), key=lambda kv: -kv[1]):
            if self.known[eng].get(k, 0) >= val:
                continue
            waits.append((self.sems[k], val))
            self.known[eng][k] = val
            sn = self.snaps.get((k[0], k[1], val))
            if sn:
                kn = self.known[eng]
                for kk, vv in sn.items():
                    if kn.get(kk, 0) < vv:
                        kn[kk] = vv
        self.n_waits += len(waits)
        return waits

    def _update(self, mydep, reads, writes):
        reads = self._flat(reads)
        writes = self._flat(writes)
        kind, name, val = mydep
        for k in reads:
            st = self.keys.setdefault(k, {"w": None, "r": {}})
            if st["r"].get((kind, name), 0) < val:
                st["r"][(kind, name)] = val
        for k in writes:
            self.keys[k] = {"w": mydep, "r": {}}

    def op(self, eng, fn, reads=(), writes=()):
        waits = self._deps(eng, reads, writes)
        self.count[eng] += 1
        mydep = ("eng", eng, self.count[eng])
        sn = dict(self.known[eng])
        if SAME_ENGINE_SYNC[eng] is False or True:
            sn[("eng", eng)] = self.count[eng]
        self.snaps[mydep] = sn
        sem = self.sems[("eng", eng)]

        call = _record(fn)

        def closure(e, waits=waits, call=call, sem=sem):
            for s, v in waits:
                e.wait_ge(s, v)
            getattr(e, call[0])(*call[1], **call[2]).then_inc(sem, 1)

        self.streams[eng].append(closure)
        self._update(mydep, reads, writes)

    def dma(self, q, fn, semname, reads=(), writes=()):
        waits = self._deps(q, reads, writes)
        sem = self._dsem(semname)
        self.dcount[semname] += 16
        mydep = ("dma", semname, self.dcount[semname])
        self.snaps[mydep] = dict(self.known[q])

        call = _record(fn)

        def closure(e, waits=waits, call=call, sem=sem):
            for s, v in waits:
                e.wait_ge(s, v)
            getattr(e, call[0])(*call[1], **call[2]).then_inc(sem, 16)

        self.streams[q].append(closure)
        self._update(mydep, reads, writes)

    def final_wait(self, q="sp"):
        waits = []
        for name, c in self.dcount.items():
            if c > 0:
                waits.append((self.sems[("dma", name)], c))
        for e in COMPUTE:
            if self.count[e] > 0:
                waits.append((self.sems[("eng", e)], self.count[e]))

        def closure(e, waits=waits):
            for s, v in waits:
                e.wait_ge(s, v)

        self.streams[q].append(closure)

    def emit(self):
        nc = self.nc
        with nc.Block() as block:
            @block.tensor
            def _(e):
                for c in self.streams["pe"]:
                    c(e)

            @block.scalar
            def _(e):
                for c in self.streams["act"]:
                    c(e)

            @block.vector
            def _(e):
                for c in self.streams["dve"]:
                    c(e)

            @block.gpsimd
            def _(e):
                for c in self.streams["pool"]:
                    c(e)

            @block.sync
            def _(e):
                for c in self.streams["sp"]:
                    c(e)


VEC_SPEC = [("mu", 14), ("w0", 4), ("a0", 4), ("k_k", 4), ("k_a", 4), ("r_k", 4), ("gn_g", 4), ("gn_b", 4),
            ("lcw", 16), ("lcb", 4), ("ba", 4), ("bx", 4), ("lam", 4), ("outg", 4), ("fcw", 72), ("fcb", 24),
            ("g1", 8), ("g2", 8)]
VOFF = {}
_o = 0
for _n, _c in VEC_SPEC:
    VOFF[_n] = _o
    _o += _c
NV = _o

TILES = [("meta", 1, 16)] + [("prompt", 1, 256)] * 8 + [("sample", 16, 4)]
NSLOT = 5


def build(tiles=None, STAGE=99, WORDER=None, info=None):
    import os
    tiles = TILES if tiles is None else tiles
    nc = bass.Bass("TRN2", target_bir_lowering=False)
    di = lambda name, shape: nc.dram_tensor(name, shape, F32, kind="ExternalInput")
    do = lambda name, shape: nc.dram_tensor(name, shape, F32, kind="ExternalOutput")
    xp_d = di("xp", [SEQ, D]).ap()
    meta_d = di("meta", [16, D]).ap()
    xs_d = di("xs", [64, D]).ap()
    st_u_d = di("st_u", [128, 22, 16, 3]).ap()
    st_w_d = di("st_w", [128, 16, 4, 64]).ap()
    st_h_d = di("st_h", [128, 4, 16]).ap()
    st_f_d = di("st_f", [128, 24, 16, 2]).ap()
    vecs_d = di("vecs", [128, NV]).ap()
    gf_t = di("gf", [D])
    w_in_d = di("w_in", [D, DIN]).ap()
    w_out_d = di("w_out", [D, D]).ap()
    w_upf_d = di("w_upf", [D, DFF]).ap()
    w_gate_d = di("w_gate", [D, DFF]).ap()
    w_down_d = di("w_down", [DFF, D]).ap()
    tmwup_d = di("tmwup", [64, 512]).ap()
    tmaup_d = di("tmaup", [64, 512]).ap()
    tmgup_d = di("tmgup", [128, 512]).ap()
    wabd_d = di("wabd", [4, 128, 128]).ap()
    wxbd_d = di("wxbd", [4, 128, 128]).ap()
    cmask_d = di("cmask", [4, 128, 128]).ap()
    cones_d = di("cones", [3, 128, 128]).ap()

    wsc_d = nc.dram_tensor("wsc", [51, 128, 2048], BF16).ap()
    y_p_d = do("y_p", [SEQ, D]).ap()
    y_s_d = do("y_s", [64, D]).ap()
    o_u_d = {"prompt": do("op_u", [128, 22, 1, 3]).ap(), "sample": do("os_u", [128, 22, 16, 3]).ap()}
    o_w_d = {"prompt": do("op_w", [128, 1, 4, 64]).ap(), "sample": do("os_w", [128, 16, 4, 64]).ap()}
    o_h_d = {"prompt": do("op_h", [128, 4, 1]).ap(), "sample": do("os_h", [128, 4, 16]).ap()}
    o_f_d = {"prompt": do("op_f", [128, 24, 1, 2]).ap(), "sample": do("os_f", [128, 24, 16, 2]).ap()}

    with ExitStack() as st:
        tk = TK(nc, st)
        sb = lambda name, shape, dt=F32: st.enter_context(nc.sbuf_tensor("s_" + name, shape, dt))
        ps = lambda name, shape, dt=F32: st.enter_context(nc.psum_tensor("p_" + name, shape, dt))
        op = tk.op

        xtm2 = [sb("xtm%d" % i, [128, 2, D]) for i in range(2)]
        gfbc = sb("gfbc", [128, D])
        xsb = sb("xsb", [128, D], BF16)
        xnT2 = [sb("xnT%d" % i, [128, 8, 256], BF16) for i in range(2)]
        UW = 259
        u = sb("u", [128, 22, UW])
        carry = sb("carry", [128, 22, 16, 3])
        carryP = sb("carryP", [128, 22, 1, 3])
        WstP = sb("WstP", [128, 1, 4, 64])
        hstP = sb("hstP", [128, 4, 1])
        fcarryP = sb("fcarryP", [128, 24, 1, 2])
        ring = [sb("ring%d" % i, [128, 2048], BF16) for i in range(NSLOT)]
        vecs = sb("vecs", [128, NV])
        vder = sb("vder", [128, 8])
        stat = sb("stat", [128, 8])
        tA = sb("tA", [128, 4, 256])
        tK = sb("tK", [128, 4, 256])
        tB = sb("tB", [128, 4, 256])
        tG = sb("tG", [128, 4, 256])
        tGate = sb("tGate", [128, 4, 256])
        tBon = sb("tBon", [128, 4, 256])
        tE4 = sb("tE4", [128, 4, 256])
        tE24 = sb("tE24", [128, 4, 256])
        ones_t = sb("ones_t", [128, 128])
        dC = sb("dC", [128, 4, 16])
        rT = sb("rT", [128, 4, 256], BF16)
        kT = sb("kT", [128, 4, 256], BF16)
        bTm = sb("bTm", [128, 8, 256], BF16)
        ktTm = sb("ktTm", [128, 8, 256], BF16)
        BhFm = sb("BhFm", [128, 8, 256], BF16)
        KhFm = sb("KhFm", [128, 8, 256], BF16)
        Vbm = sb("Vbm", [128, 8, 256], BF16)
        thb = sb("thb", [128, 256], BF16)
        xab = sb("xab", [128, 256], BF16)
        sgb = sb("sgb", [128, 256], BF16)
        xcb = sb("xcb", [128, 4, 256], BF16)
        ycatT = sb("ycatT", [128, 8, 256], BF16)
        hidT = sb("hidT", [128, 6, 256], BF16)
        upbuf = [sb("upbuf%d" % i, [128, 258]) for i in range(2)]
        upc = [sb("upc%d" % i, [128, 256]) for i in range(2)]
        upc2 = [sb("upc2_%d" % i, [128, 256]) for i in range(2)]
        fcarry = sb("fcarry", [128, 24, 16, 2])
        Wst = sb("Wst", [128, 16, 4, 64])
        Wbd = sb("Wbd", [128, 4, 128], BF16)
        hst = sb("hst", [128, 4, 16])
        Pb = [sb("Pb%d" % i, [128, 1024], BF16) for i in range(2)]
        PTb = [sb("PTb%d" % i, [128, 1024], BF16) for i in range(2)]
        Zb = [sb("Zb%d" % i, [128, 1024], BF16) for i in range(2)]
        AkT = sb("AkT", [128, 1024], BF16)
        BrT = sb("BrT", [128, 1024], BF16)
        BkT = sb("BkT", [128, 1024], BF16)
        Vte = sb("Vte", [128, 8, 128], BF16)
        Bte = sb("Bte", [128, 8, 128], BF16)
        Kte = sb("Kte", [128, 8, 128], BF16)
        Xb = sb("Xb", [128, 8, 64], BF16)
        Une = sb("Une", [128, 8, 128], BF16)
        wupb = sb("wupb", [128, 512], BF16)
        aupb = sb("aupb", [128, 512], BF16)
        gupb = sb("gupb", [128, 512], BF16)
        wabd = sb("wabd", [128, 4, 128], BF16)
        wxbd = sb("wxbd", [128, 4, 128], BF16)
        cmask = sb("cmask", [128, 4, 128], BF16)
        cones = sb("cones", [128, 3, 128])
        pm = [ps("pm%d" % i, [128, 512]) for i in range(2)]
        ptb = ps("ptb", [128, 1024], BF16)
        pt32 = ps("pt32", [128, 512])
        pc = [ps("pc%d" % i, [128, 512]) for i in range(4)]
        pmi = [0]
        pci = [0]

        PT32K = ["pt32a", "pt32b"]

        def next_pm():
            pmi[0] ^= 1
            return pm[pmi[0]], ["pm%da" % pmi[0], "pm%db" % pmi[0]]

        def next_pc():
            pci[0] = (pci[0] + 1) % 4
            return pc[pci[0]], "pc%d" % pci[0]

        V = lambda name, j=0, n=1: vecs[:, VOFF[name] + j:VOFF[name] + j + n]

        tk.dma("sp", lambda e: e.dma_start(out=vecs[:], in_=vecs_d), "c_vecs", writes=["vecs"])
        tk.dma("sp", lambda e: e.dma_start(out=gfbc[:], in_=bass.AP(gf_t, 0, [[0, 128], [1, D]])), "c_gf", writes=["gfbc"])
        tk.dma("sp", lambda e: e.dma_start(out=cones[:], in_=cones_d.rearrange("c p j -> p c j")), "c_ones", writes=["cones"])
        tk.dma("pool", lambda e: e.dma_start(out=wupb[0:64, :], in_=tmwup_d), "c_w1", writes=["wupb"])
        tk.dma("pool", lambda e: e.dma_start(out=aupb[64:128, :], in_=tmaup_d), "c_w2", writes=["aupb"])
        tk.dma("pool", lambda e: e.dma_start(out=gupb[:], in_=tmgup_d), "c_w3", writes=["gupb"])
        tk.dma("pool", lambda e: e.dma_start(out=wabd[:], in_=wabd_d.rearrange("c p j -> p c j")), "c_w4", writes=["wabd"])
        tk.dma("pool", lambda e: e.dma_start(out=wxbd[:], in_=wxbd_d.rearrange("c p j -> p c j")), "c_w5", writes=["wxbd"])
        tk.dma("pool", lambda e: e.dma_start(out=cmask[:], in_=cmask_d.rearrange("c p j -> p c j")), "c_w6", writes=["cmask"])
        ident = cmask[:, 0, :]
        m_su = cmask[:, 1, :]
        m_sl = cmask[:, 2, :]
        m_ui = cmask[:, 3, :]
        blkavg = cones[:, 0, :]
        blkones = cones[:, 1, :]
        ones512 = cones[:, 2, :]
        op("dve", lambda e: e.memset(ones_t[:], 1.0), writes=["ones_t"])
        for (bf_, nm_) in ((bTm, "bTm"), (ktTm, "ktTm"), (BhFm, "BhFm"), (KhFm, "KhFm"), (Vbm, "Vbm")):
            op("pool", lambda e, bf_=bf_: e.memset(bf_[:], 0.0), writes=[(nm_, h) for h in range(8)])
        op("pool", lambda e: e.memset(Wbd[:], 0.0), writes=["Wbd"])
        op("pool", lambda e: e.memset(Une[:], 0.0), writes=["Une"])
        op("pool", lambda e: e.memset(wupb[64:128, :], 0.0), writes=["wupb"])
        op("pool", lambda e: e.memset(aupb[0:64, :], 0.0), writes=["aupb"])
        op("dve", lambda e: e.tensor_scalar(out=vder[:, 0:4], in0=V("k_a", 0, 4), scalar1=-1.0, scalar2=1.0, op0=ALU.mult, op1=ALU.add),
           reads=["vecs"], writes=["vder"])
        op("act", lambda e: e.activation(out=vder[:, 4:8], in_=V("lam", 0, 4), func=AF.Exp, scale=-1.0), reads=["vecs", "vder"], writes=["vder"])
        op("act", lambda e: e.activation(out=vder[:, 4:8], in_=vder[:, 4:8], func=AF.Ln, bias=1.0), reads=["vder"], writes=["vder"])
        op("dve", lambda e: e.tensor_scalar(out=vder[:, 4:8], in0=vder[:, 4:8], scalar1=-8.0, scalar2=None, op0=ALU.mult), reads=["vder"], writes=["vder"])

        converted = set()

        class GWS:
            def __init__(self, order, ahead=4):
                self.order = order
                self.ahead = ahead
                self.rec = []
                self.pos = 0
                self.nissued = 0
                self.free = list(range(NSLOT))
                self.inst = {}
                self.live = {}

            def _issue(self, seq, pidx):
                slot = self.free.pop(0)
                a, b = WSRC[pidx][1]
                view = ring[slot][:, 0:a * b].rearrange("p (a b) -> p a b", a=a)
                key = "ring%d" % slot
                if pidx not in converted:
                    converted.add(pidx)
                    tk.dma("pool", lambda e: e.dma_start(out=view, in_=WSRC[pidx][0]), "ringS%d" % slot, writes=[key])
                    tk.dma("sp", lambda e: e.dma_start(out=wsc_d[pidx], in_=ring[slot][:, :]), "ringW%d" % slot, reads=[key], writes=[("wsc", pidx)])
                else:
                    tk.dma("sp", lambda e: e.dma_start(out=ring[slot][:, :], in_=wsc_d[pidx]), "ringH%d" % slot, reads=[("wsc", pidx)], writes=[key])
                self.inst[seq] = (view, key, slot, pidx)

            def get(self, pidx):
                seq = self.pos
                self.pos += 1
                self.rec.append(pidx)
                if self.order is not None:
                    assert self.order[seq] == pidx, (seq, pidx, self.order[seq])
                while self.nissued <= seq:
                    assert self.free, "weight ring exhausted"
                    self._issue(self.nissued, pidx if self.order is None else self.order[self.nissued])
                    self.nissued += 1
                if self.order is not None:
                    while self.nissued < len(self.order) and self.nissued <= seq + self.ahead and len(self.free) > 0:
                        self._issue(self.nissued, self.order[self.nissued])
                        self.nissued += 1
                view, key, slot, _ = self.inst[seq]
                self.live.setdefault(pidx, []).append(seq)
                return view, key

            def done(self, pidx):
                seq = self.live[pidx].pop(0)
                self.free.append(self.inst[seq][2])

        def weight_items_src():
            items = []
            w_in_v = w_in_d.rearrange("(dc p) e -> p dc e", p=128)
            for i in range(11):
                items.append((w_in_v[:, :, i * 256:(i + 1) * 256], (8, 256)))
            w_out_v = w_out_d.rearrange("(ec p) d -> p ec d", p=128)
            for i in range(4):
                items.append((w_out_v[:, 2 * i:2 * i + 2, :], (2, 1024)))
            w_up_v = w_upf_d.rearrange("(dc p) f -> p dc f", p=128)
            w_gate_v = w_gate_d.rearrange("(dc p) f -> p dc f", p=128)
            for i in range(12):
                items.append((w_up_v[:, :, i * 256:(i + 1) * 256], (8, 256)))
                items.append((w_gate_v[:, :, i * 256:(i + 1) * 256], (8, 256)))
            w_down_v = w_down_d.rearrange("(fc p) d -> p fc d", p=128)
            for i in range(12):
                items.append((w_down_v[:, 2 * i:2 * i + 2, :], (2, 1024)))
            return items

        WSRC = weight_items_src()
        WS = GWS(WORDER)

        def rms_gen(par, nt, rows_list, gname):
            xtm_ = xtm2[par]
            xnT_ = xnT2[par]
            for s in range(nt):
                rows = rows_list[s]
                xk = ("xtm", par, s)
                op("act", lambda e, s=s, rows=rows: e.activation(out=xsb[:rows, :], in_=xtm_[:rows, s, :], func=AF.Square, accum_out=stat[:rows, 0:1]),
                   reads=[xk], writes=["xsb", "stat"])
                op("act", lambda e, rows=rows: e.activation(out=stat[:rows, 1:2], in_=stat[:rows, 0:1], func=AF.Ln, scale=1.0 / D, bias=1e-6),
                   reads=["stat"], writes=["stat1"])
                op("act", lambda e, rows=rows: e.activation(out=stat[:rows, 2:3], in_=stat[:rows, 1:2], func=AF.Exp, scale=-0.5), reads=["stat1"], writes=["stat2"])
                yield
                op("act", lambda e, s=s, rows=rows: e.activation(out=xsb[:rows, :], in_=xtm_[:rows, s, :], func=AF.Copy, scale=stat[:rows, 2:3]),
                   reads=[xk, "stat2"], writes=["xsb"])
                for dc in range(8):
                    op("pe", lambda e, dc=dc, rows=rows: e.transpose(ptb[:, dc * 128:dc * 128 + rows], xsb[:rows, dc * 128:(dc + 1) * 128], ident[:rows, :rows]),
                       reads=["xsb", "cmask"], writes=["ptb"])
                yield
                pv = ptb[:, :].rearrange("p (a b) -> p a b", a=8)[:, :, 0:rows]
                gsc = V(gname, 0, 8)
                gbc = bass.AP(gsc.tensor, gsc.offset, [gsc.ap[0], [1, 8], [0, rows]])
                op("dve", lambda e, s=s, rows=rows, pv=pv, gbc=gbc: e.tensor_tensor(out=xnT_[:, :, s * 128:s * 128 + rows], in0=pv, in1=gbc, op=ALU.mult),
                   reads=["ptb", "vecs"], writes=[("xnT", par)])
                yield

        def tile_geom(t):
            kind_, NS_, L_ = t
            NC_ = NS_ * L_
            nt_ = (NC_ + 127) // 128
            return NC_, nt_, [min(128, NC_ - s_ * 128) for s_ in range(nt_)]

        def front_gen(tidx, prow_):
            par = tidx % 2
            kind_ = tiles[tidx][0]
            NC_, nt_, rows_ = tile_geom(tiles[tidx])
            xtm_ = xtm2[par]
            if kind_ == "sample":
                tk.dma("sp", lambda e: e.dma_start(out=xtm_[:64, 0, :], in_=xs_d), "xld%d0" % par, writes=[("xtm", par, 0)])
            elif kind_ == "meta":
                tk.dma("sp", lambda e: e.dma_start(out=xtm_[:16, 0, :], in_=meta_d), "xld%d0" % par, writes=[("xtm", par, 0)])
            else:
                for s_ in range(2):
                    tk.dma("sp", lambda e, s_=s_, r0=prow_ + s_ * 128: e.dma_start(out=xtm_[:, s_, :], in_=xp_d[r0:r0 + 128, :]), "xld%d%d" % (par, s_), writes=[("xtm", par, s_)])
            yield
            for _ in rms_gen(par, nt_, rows_, "g1"):
                yield

        def make_tile(ti):
            kind, NS, L = tiles[ti]
            NC = NS * L
            C = min(L, 128)
            NCHK = L // C
            nt = (NC + 127) // 128
            rows_list = [min(128, NC - s * 128) for s in range(nt)]
            skind = "sample" if kind == "sample" else "prompt"
            W3 = 3 + L
            par = ti % 2
            xtm = xtm2[par]
            xnT = xnT2[par]
            XN = ("xnT", par)
            XT = lambda s_: ("xtm", par, s_)

            def Uv(e_, lo, hi):
                return u[:, e_, 0:NS * W3].rearrange("p (s w) -> p s w", s=NS)[:, :, lo:hi]

            def Fv(buf, fc):
                return buf[:, fc, 0:NC].rearrange("p (s l) -> p s l", s=NS)

            def F2(buf2d):
                return buf2d[:, 0:NC].rearrange("p (s l) -> p s l", s=NS)


            R_ = lambda fc: Uv(fc, 3, W3)
            K_ = lambda fc: Uv(4 + fc, 3, W3)
            V_ = lambda fc: Uv(8 + fc, 3, W3)
            E = lambda fc: tE4[:, fc, 0:NC]
            E2 = lambda fc: tE24[:, fc, 0:NC]
            E3 = lambda fc: tE4[:, fc, 0:NC].rearrange("p (s l) -> p s l", s=NS)
            E23 = lambda fc: tE24[:, fc, 0:NC].rearrange("p (s l) -> p s l", s=NS)
            FC = range(4)
            ukeys = [("u", e_) for e_ in range(22)]
            prow = prows[ti]
            nprow = prows[ti + 1] if ti + 1 < len(tiles) else 0

            if skind == "sample":
                carry_, Wst_, hst_, fcarry_ = carry, Wst, hst, fcarry
            else:
                carry_, Wst_, hst_, fcarry_ = carryP, WstP, hstP, fcarryP
            KC, KW, KH, KF = "carry_" + skind, "Wst_" + skind, "hst_" + skind, "fcarry_" + skind

            def phase_A():
                ukeys = [("u", e_) for e_ in range(22)]
                if kind == "meta":
                    op("dve", lambda e: e.memset(carry_[:], 0.0), reads=[KC], writes=[KC])
                    op("dve", lambda e: e.memset(Wst_[:], 0.0), reads=[KW], writes=[KW])
                    op("dve", lambda e: e.memset(hst_[:], 0.0), reads=[KH], writes=[KH])
                    op("dve", lambda e: e.memset(fcarry_[:], 0.0), reads=[(KF, f_) for f_ in range(24)], writes=[(KF, f_) for f_ in range(24)])
                uh = u[:, :, 0:NS * W3].rearrange("p e (s w) -> p e s w", s=NS)[:, :, :, 0:3]
                for e0 in range(0, 22, 11):
                    op("dve", lambda e, e0=e0: e.tensor_copy(out=u[:, e0:e0 + 11, 0:NS * W3].rearrange("p e (s w) -> p e s w", s=NS)[:, :, :, 0:3],
                                                             in_=carry_[:, e0:e0 + 11, 0:NS, :]),
                       reads=[KC], writes=ukeys[e0:e0 + 11])


                for pi_ in range(11):
                    wv, wk = WS.get(pi_)
                    for j in range(2):
                        e_ = pi_ * 2 + j
                        pmt, pmk = next_pm()
                        for dc in range(8):
                            op("pe", lambda e, wv=wv, dc=dc, j=j, pmt=pmt: e.matmul(pmt[:, 0:NC], wv[:, dc, j * 128:(j + 1) * 128], xnT[:, dc, 0:NC], start=(dc == 0), stop=(dc == 7)),
                               reads=[wk, XN], writes=[pmk])
                        op("act", lambda e, e_=e_, pmt=pmt: e.copy(out=Uv(e_, 3, W3), in_=pmt[:, 0:NC].rearrange("p (s l) -> p s l", s=NS)),
                           reads=[pmk], writes=[("u", e_)])
                    WS.done(pi_)
                    yield
                for e0 in range(0, 22, 11):
                    op("dve", lambda e, e0=e0: e.tensor_copy(out=carry_[:, e0:e0 + 11, 0:NS, :],
                                                             in_=u[:, e0:e0 + 11, 0:NS * W3].rearrange("p e (s w) -> p e s w", s=NS)[:, :, :, L:L + 3]),
                       reads=ukeys[e0:e0 + 11], writes=[KC])

                tEs = [tE4[:, 0, :], tE4[:, 1, :]]
                yield
                for e0 in range(0, 14, 2):
                    for q in range(2):
                        e_ = e0 + q
                        op("pool", lambda e, e_=e_, q=q: e.tensor_tensor(out=tEs[q][:, 0:NC].rearrange("p (s l) -> p s l", s=NS), in0=Uv(e_, 2, 2 + L), in1=Uv(e_, 3, W3), op=ALU.subtract),
                           reads=[("u", e_)], writes=[("tE", q)])
                    for q in range(2):
                        e_ = e0 + q
                        op("dve", lambda e, e_=e_, q=q: e.scalar_tensor_tensor(out=Uv(e_, 3, W3), in0=tEs[q][:, 0:NC].rearrange("p (s l) -> p s l", s=NS), scalar=V("mu", e_), in1=Uv(e_, 3, W3), op0=ALU.mult, op1=ALU.add),
                           reads=[("tE", q), ("u", e_), "vecs"], writes=[("u", e_)])
                yield
                op("act", lambda e: e.activation(out=F2(thb), in_=Uv(12, 3, W3), func=AF.Tanh), reads=[("u", 12)], writes=["thb"])
                yield
                op("dve", lambda e: e.tensor_copy(out=F2(xab), in_=Uv(12, 3, W3)), reads=[("u", 12)], writes=["xab"])
                yield
                op("act", lambda e: e.activation(out=F2(sgb), in_=Uv(13, 3, W3), func=AF.Sigmoid), reads=[("u", 13)], writes=["sgb"])
                yield
                for fc in range(4):
                    cs = slice(fc * 128, (fc + 1) * 128)
                    pmt, pmk = next_pm()
                    op("pe", lambda e, pmt=pmt, cs=cs: e.matmul(pmt[:, 0:NC], wupb[:, cs], thb[:, 0:NC], start=True, stop=True), reads=["wupb", "thb"], writes=[pmk])
                    op("act", lambda e, pmt=pmt, fc=fc: e.activation(out=tG[:, fc, 0:NC], in_=pmt[:, 0:NC], func=AF.Sigmoid, bias=V("w0", fc)), reads=[pmk, "vecs"], writes=[("tG", fc)])
                    pmt, pmk = next_pm()
                    op("pe", lambda e, pmt=pmt, cs=cs: e.matmul(pmt[:, 0:NC], aupb[:, cs], xab[:, 0:NC], start=True, stop=True), reads=["aupb", "xab"], writes=[pmk])
                    op("act", lambda e, pmt=pmt, fc=fc: e.activation(out=tA[:, fc, 0:NC], in_=pmt[:, 0:NC], func=AF.Sigmoid, bias=V("a0", fc)), reads=[pmk, "vecs"], writes=[("tA", fc)])
                    pmt, pmk = next_pm()
                    op("pe", lambda e, pmt=pmt, cs=cs: e.matmul(pmt[:, 0:NC], gupb[:, cs], sgb[:, 0:NC], start=True, stop=True), reads=["gupb", "sgb"], writes=[pmk])
                    op("act", lambda e, pmt=pmt, fc=fc: e.copy(out=tGate[:, fc, 0:NC], in_=pmt[:, 0:NC]), reads=[pmk], writes=[("tGate", fc)])
                yield
                for fc in FC:
                    op("pool", lambda e, fc=fc: e.tensor_scalar(out=Fv(tK, fc), in0=K_(fc), scalar1=V("k_k", fc), scalar2=0.0, op0=ALU.mult, op1=ALU.add),
                       reads=[("u", 4 + fc), "vecs"], writes=[("tK", fc)])
                yield
                for fc in FC:
                    op("pool", lambda e, fc=fc: e.tensor_tensor(out=E(fc), in0=tK[:, fc, 0:NC], in1=tK[:, fc, 0:NC], op=ALU.mult), reads=[("tK", fc)], writes=[("tE", fc)])
                yield
                for fc in FC:
                    pmt, pmk = next_pm()
                    op("pe", lambda e, pmt=pmt, fc=fc: e.matmul(pmt[:, 0:NC], blkones, E(fc), start=True, stop=True), reads=["cones", ("tE", fc)], writes=[pmk])
                    op("dve", lambda e, pmt=pmt, fc=fc: e.tensor_scalar(out=E2(fc), in0=pmt[:, 0:NC], scalar1=1e-24, scalar2=None, op0=ALU.max), reads=[pmk], writes=[("tE2", fc)])
                yield
                for fc in FC:
                    op("act", lambda e, fc=fc: e.activation(out=E2(fc), in_=E2(fc), func=AF.Ln), reads=[("tE2", fc)], writes=[("tE2", fc)])
                yield
                for fc in FC:
                    op("act", lambda e, fc=fc: e.activation(out=E2(fc), in_=E2(fc), func=AF.Exp, scale=-0.5), reads=[("tE2", fc)], writes=[("tE2", fc)])
                yield
                for fc in FC:
                    op("pool", lambda e, fc=fc: e.tensor_scalar(out=E(fc), in0=tA[:, fc, 0:NC], scalar1=V("k_a", fc), scalar2=vder[:, fc:fc + 1], op0=ALU.mult, op1=ALU.add),
                       reads=[("tA", fc), "vecs", "vder", ("tE", fc)], writes=[("tE", fc)])
                yield
                for fc in FC:
                    op("dve", lambda e, fc=fc: e.tensor_tensor(out=tK[:, fc, 0:NC], in0=tK[:, fc, 0:NC], in1=E2(fc), op=ALU.mult), reads=[("tK", fc), ("tE2", fc)], writes=[("tK", fc)])
                yield
                for fc in FC:
                    op("dve", lambda e, fc=fc: e.tensor_tensor(out=K_(fc), in0=K_(fc), in1=E3(fc), op=ALU.mult), reads=[("u", 4 + fc), ("tE", fc)], writes=[("u", 4 + fc)])
                yield
                for fc in FC:
                    op("pool", lambda e, fc=fc: e.tensor_tensor(out=tB[:, fc, 0:NC], in0=tK[:, fc, 0:NC], in1=tA[:, fc, 0:NC], op=ALU.mult), reads=[("tK", fc), ("tA", fc)], writes=[("tB", fc)])
                yield
                for fc in FC:
                    op("dve", lambda e, fc=fc: e.scalar_tensor_tensor(out=E3(fc), in0=R_(fc), scalar=V("r_k", fc), in1=K_(fc), op0=ALU.mult, op1=ALU.mult),
                       reads=[("u", fc), ("u", 4 + fc), "vecs", ("tE", fc)], writes=[("tE", fc)])
                yield
                for fc in FC:
                    pmt, pmk = next_pm()
                    op("pe", lambda e, pmt=pmt, fc=fc: e.matmul(pmt[:, 0:NC], blkones, E(fc), start=True, stop=True), reads=["cones", ("tE", fc)], writes=[pmk])
                    op("dve", lambda e, pmt=pmt, fc=fc: e.tensor_tensor(out=Fv(tBon, fc), in0=pmt[:, 0:NC].rearrange("p (s l) -> p s l", s=NS), in1=V_(fc), op=ALU.mult),
                       reads=[pmk, ("u", 8 + fc)], writes=[("tBon", fc)])
                yield
                for b in range(NS):
                    for ch in range(NCHK):
                        c0 = b * L + ch * C
                        for fc in FC:
                            op("dve", lambda e, fc=fc, c0=c0: e.tensor_tensor_scan(out=tE4[:, fc, c0:c0 + C], data0=ones_t[:, 0:C], data1=tG[:, fc, c0:c0 + C], initial=0.0, op0=ALU.mult, op1=ALU.add),
                               reads=[("tG", fc), "ones_t", ("tE", fc)], writes=[("tE", fc)])
                yield
                for fc in FC:
                    op("pool", lambda e, fc=fc: e.tensor_tensor(out=E2(fc), in0=E(fc), in1=tG[:, fc, 0:NC], op=ALU.subtract), reads=[("tE", fc), ("tG", fc), ("tE2", fc)], writes=[("tE2", fc)])
                yield
                for fc in FC:
                    op("act", lambda e, fc=fc: e.activation(out=E2(fc), in_=E2(fc), func=AF.Exp, scale=-C0), reads=[("tE2", fc)], writes=[("tE2", fc)])
                yield
                for fc in FC:
                    op("dve", lambda e, fc=fc: e.tensor_tensor(out=kT[:, fc, 0:NC], in0=tK[:, fc, 0:NC], in1=E2(fc), op=ALU.mult), reads=[("tK", fc), ("tE2", fc)], writes=[("kT", fc)])
                yield
                for fc in FC:
                    op("act", lambda e, fc=fc: e.activation(out=E2(fc), in_=E(fc), func=AF.Exp, scale=-C0), reads=[("tE", fc), ("tE2", fc)], writes=[("tE2", fc)])
                yield
                for fc in FC:
                    op("dve", lambda e, fc=fc: e.tensor_tensor(out=Fv(rT, fc), in0=R_(fc), in1=E23(fc), op=ALU.mult), reads=[("u", fc), ("tE2", fc)], writes=[("rT", fc)])
                yield
                for fc in FC:
                    e1v = tE24[:, fc, 0:NC].rearrange("p (n c) -> p n c", c=C)[:, :, C - 1:C]
                    op("pool", lambda e, fc=fc, e1v=e1v: e.tensor_copy(out=dC[:, fc, 0:NS * NCHK].rearrange("p (n o) -> p n o", o=1), in_=e1v), reads=[("tE2", fc)], writes=[("dC", fc)])
                yield
                for fc in FC:
                    op("act", lambda e, fc=fc: e.activation(out=E2(fc), in_=E(fc), func=AF.Exp, scale=C0), reads=[("tE", fc), ("tE2", fc)], writes=[("tE2", fc)])
                HH = [(2 * fc + h2, fc, slice(64 * h2, 64 * h2 + 64)) for fc in range(4) for h2 in range(2)]
                yield
                for (h, fc, rs) in HH:
                    op("dve", lambda e, fc=fc, h=h, rs=rs: e.tensor_tensor(out=bTm[rs, h, 0:NC], in0=tB[rs, fc, 0:NC], in1=tE24[rs, fc, 0:NC], op=ALU.mult),
                       reads=[("tB", fc), ("tE2", fc)], writes=[("bTm", h)])
                yield
                for (h, fc, rs) in HH:
                    op("dve", lambda e, fc=fc, h=h, rs=rs: e.tensor_tensor(out=ktTm[rs, h, 0:NC].rearrange("p (s l) -> p s l", s=NS), in0=K_(fc)[rs], in1=E23(fc)[rs], op=ALU.mult),
                       reads=[("u", 4 + fc), ("tE2", fc)], writes=[("ktTm", h)])
                yield
                for (h, fc, rs) in HH:
                    op("act", lambda e, fc=fc, h=h, rs=rs: e.copy(out=Vbm[rs, h, 0:NC].rearrange("p (s l) -> p s l", s=NS), in_=V_(fc)[rs]), reads=[("u", 8 + fc)], writes=[("Vbm", h)])
                yield
                for (h, fc, rs) in HH:
                    dcb = dC[rs, fc, 0:1]
                    dcb = bass.AP(dcb.tensor, dcb.offset, [dcb.ap[0], [1, NS * NCHK], [0, C]])
                    op("pool", lambda e, h=h, rs=rs, dcb=dcb: e.tensor_tensor(out=BhFm[rs, h, 0:NC].rearrange("p (n c) -> p n c", c=C), in0=bTm[rs, h, 0:NC].rearrange("p (n c) -> p n c", c=C), in1=dcb, op=ALU.mult),
                       reads=[("bTm", h), ("dC", fc)], writes=[("BhFm", h)])
                yield
                for (h, fc, rs) in HH:
                    dcb = dC[rs, fc, 0:1]
                    dcb = bass.AP(dcb.tensor, dcb.offset, [dcb.ap[0], [1, NS * NCHK], [0, C]])
                    op("pool", lambda e, h=h, rs=rs, dcb=dcb: e.tensor_tensor(out=KhFm[rs, h, 0:NC].rearrange("p (n c) -> p n c", c=C), in0=ktTm[rs, h, 0:NC].rearrange("p (n c) -> p n c", c=C), in1=dcb, op=ALU.mult),
                       reads=[("ktTm", h), ("dC", fc)], writes=[("KhFm", h)])

                yield

            def phase_B():
                fr_it = front_gen(ti + 1, nprow) if ti + 1 < len(tiles) else iter(())
                def lru_gen():
                    yield
                    for fc in range(4):
                        ex = 14 + fc
                        op("act", lambda e, fc=fc, ex=ex: e.activation(out=Fv(tK, fc), in_=Uv(ex, 3, W3), func=AF.Identity, scale=V("lcw", 12 + fc), bias=V("lcb", fc)),
                           reads=[("u", ex), "vecs", ("tK", fc)], writes=[("tK", fc)])
                        for j in range(3):
                            op("dve", lambda e, fc=fc, ex=ex, j=j: e.scalar_tensor_tensor(out=Fv(tK, fc), in0=Uv(ex, j, j + L), scalar=V("lcw", j * 4 + fc), in1=Fv(tK, fc), op0=ALU.mult, op1=ALU.add),
                               reads=[("u", ex), "vecs", ("tK", fc)], writes=[("tK", fc)])
                        op("act", lambda e, fc=fc: e.copy(out=xcb[:, fc, 0:NC], in_=tK[:, fc, 0:NC]), reads=[("tK", fc)], writes=[("xcb", fc)])
                    yield
                    for fc in range(4):
                        pmt, pmk = next_pm()
                        op("pe", lambda e, pmt=pmt, fc=fc: e.matmul(pmt[:, 0:NC], wabd[:, fc, :], xcb[:, fc, 0:NC], start=True, stop=True), reads=["wabd", ("xcb", fc)], writes=[pmk])
                        op("act", lambda e, pmt=pmt, fc=fc: e.activation(out=tB[:, fc, 0:NC], in_=pmt[:, 0:NC], func=AF.Sigmoid, bias=V("ba", fc)), reads=[pmk, "vecs", ("tB", fc)], writes=[("tB", fc)])
                        pmt, pmk = next_pm()
                        op("pe", lambda e, pmt=pmt, fc=fc: e.matmul(pmt[:, 0:NC], wxbd[:, fc, :], xcb[:, fc, 0:NC], start=True, stop=True), reads=["wxbd", ("xcb", fc)], writes=[pmk])
                        op("act", lambda e, pmt=pmt, fc=fc: e.activation(out=tG[:, fc, 0:NC], in_=pmt[:, 0:NC], func=AF.Sigmoid, bias=V("bx", fc)), reads=[pmk, "vecs", ("tG", fc)], writes=[("tG", fc)])
                    yield
                    for fc in range(4):
                        op("pool", lambda e, fc=fc: e.tensor_tensor(out=tG[:, fc, 0:NC], in0=tG[:, fc, 0:NC], in1=tK[:, fc, 0:NC], op=ALU.mult), reads=[("tG", fc), ("tK", fc)], writes=[("tG", fc)])
                    yield
                    for fc in range(4):
                        op("act", lambda e, fc=fc: e.activation(out=tB[:, fc, 0:NC], in_=tB[:, fc, 0:NC], func=AF.Exp, scale=vder[:, 4 + fc:5 + fc]), reads=[("tB", fc), "vder"], writes=[("tB", fc)])
                    yield
                    for fc in range(4):
                        op("pool", lambda e, fc=fc: e.tensor_tensor(out=tE4[:, fc, 0:NC], in0=tB[:, fc, 0:NC], in1=tB[:, fc, 0:NC], op=ALU.mult), reads=[("tB", fc), ("tE", fc)], writes=[("tE", fc)])
                    yield
                    for fc in range(4):
                        op("act", lambda e, fc=fc: e.activation(out=tE4[:, fc, 0:NC], in_=tE4[:, fc, 0:NC], func=AF.Ln, scale=-1.0, bias=1.0), reads=[("tE", fc)], writes=[("tE", fc)])
                    yield
                    for fc in range(4):
                        op("act", lambda e, fc=fc: e.activation(out=tE4[:, fc, 0:NC], in_=tE4[:, fc, 0:NC], func=AF.Exp, scale=0.5), reads=[("tE", fc)], writes=[("tE", fc)])
                    yield
                    for fc in range(4):
                        if kind == "meta":
                            op("dve", lambda e, fc=fc: e.memset(tE4[:, fc, 0:1], 1.0), reads=[("tE", fc)], writes=[("tE", fc)])
                        op("dve", lambda e, fc=fc: e.tensor_tensor(out=tG[:, fc, 0:NC], in0=tG[:, fc, 0:NC], in1=tE4[:, fc, 0:NC], op=ALU.mult), reads=[("tG", fc), ("tE", fc)], writes=[("tG", fc)])
                    yield
                    for fc in range(4):
                        for b in range(NS):
                            op("dve", lambda e, fc=fc, b=b: e.tensor_tensor_scan(out=tE4[:, fc, b * L:(b + 1) * L], data0=tB[:, fc, b * L:(b + 1) * L], data1=tG[:, fc, b * L:(b + 1) * L],
                                                                                initial=hst_[:, fc, b:b + 1], op0=ALU.mult, op1=ALU.add),
                               reads=[("tB", fc), ("tG", fc), KH, ("tE", fc)], writes=[("tE", fc)])
                        op("dve", lambda e, fc=fc: e.tensor_copy(out=hst_[:, fc, 0:NS].rearrange("p (s o) -> p s o", o=1), in_=Fv(tE4, fc)[:, :, L - 1:L]), reads=[("tE", fc), KH], writes=[KH])
                    yield
                    for fc in range(4):
                        eg = 18 + fc
                        op("act", lambda e, fc=fc, eg=eg: e.activation(out=Uv(eg, 3, W3), in_=Uv(eg, 3, W3), func=AF.Gelu_apprx_tanh), reads=[("u", eg)], writes=[("u", eg)])
                    yield
                    for fc in range(4):
                        eg = 18 + fc
                        op("dve", lambda e, fc=fc, eg=eg: e.tensor_tensor(out=Fv(tK, fc), in0=Fv(tE4, fc), in1=Uv(eg, 3, W3), op=ALU.mult), reads=[("tE", fc), ("u", eg), ("tK", fc)], writes=[("tK", fc)])
                        op("pool", lambda e, fc=fc: e.tensor_tensor(out=tE4[:, fc, 0:NC], in0=tK[:, fc, 0:NC], in1=tK[:, fc, 0:NC], op=ALU.mult), reads=[("tK", fc), ("tE", fc)], writes=[("tE", fc)])
                    yield
                    for fc in range(4):
                        op("pe", lambda e, fc=fc: e.matmul(pt32[:, 0:NC], ones512, tE4[:, fc, 0:NC], start=(fc == 0), stop=(fc == 3)), reads=["cones", ("tE", fc)], writes=[PT32K])
                    yield
                    op("act", lambda e: e.activation(out=E2(0), in_=pt32[:, 0:NC], func=AF.Ln, bias=1e-6), reads=[PT32K, ("tE2", 0)], writes=[("tE2", 0)])
                    op("act", lambda e: e.activation(out=E2(0), in_=E2(0), func=AF.Exp, scale=-0.5), reads=[("tE2", 0)], writes=[("tE2", 0)])
                    yield
                    for fc in range(4):
                        op("dve", lambda e, fc=fc: e.scalar_tensor_tensor(out=ycatT[:, 4 + fc, 0:NC], in0=tK[:, fc, 0:NC], scalar=V("outg", fc), in1=E2(0), op0=ALU.mult, op1=ALU.mult),
                           reads=[("tK", fc), ("tE2", 0), "vecs"], writes=[("ycatT", 4 + fc)])


                    yield

                lru_it = lru_gen()

                lru_cnt = [0]

                def lru_step(n=1):
                    for _ in range(n):
                        next(lru_it, None)
                    lru_cnt[0] += 1
                    if lru_cnt[0] >= 3:
                        next(fr_it, None)

                nlev = int(np.log2(C)) - 1
                fk = lambda nm: [(nm, fc) for fc in range(4)]
                lanes = [(b, ch) for b in range(NS) for ch in range(NCHK)]
                LPB = max(1, 1024 // (8 * C))
                assert (len(lanes) + LPB - 1) // LPB <= 2
                UPB = 512 // C

                def lane_c0(li):
                    b_, ch_ = lanes[li]
                    return b_ * L + ch_ * C

                def ubuf(li):
                    return li // LPB

                def uoff(li, h):
                    return ((li % LPB) * 8 + h) * C

                def mask_bc(m, n):
                    return bass.AP(m.tensor, m.offset, [[m.ap[0][0], C], [0, n], [1, C]])

                def unit_banks(lane_ids):
                    units = [(li, h) for li in lane_ids for h in range(8)]
                    return [units[i:i + UPB] for i in range(0, len(units), UPB)]

                def bank_key(nm, us):
                    return (nm, us[0][0], us[0][1])

                def emit_product(us, fn_ops, evac):
                    pct, pck = next_pc()
                    for ui_, (li, h) in enumerate(us):
                        lap, rap, rkeys = fn_ops(li, h)
                        op("pe", lambda e, pct=pct, ui_=ui_, lap=lap, rap=rap: e.matmul(pct[:C, ui_ * C:(ui_ + 1) * C], lap, rap, start=True, stop=True), reads=rkeys, writes=[pck])
                    evac(pct, pck, us)

                def sb_view(buf_list, us):
                    li0, h0 = us[0]
                    o0 = uoff(li0, h0)
                    return buf_list[ubuf(li0)][:C, o0:o0 + len(us) * C]

                def masked_evac(dst_list, dk, msk):
                    def ev(pct, pck, us):
                        n = len(us)
                        op("dve", lambda e: e.tensor_tensor(out=sb_view(dst_list, us).rearrange("p (u c) -> p u c", c=C),
                                                            in0=pct[:C, 0:n * C].rearrange("p (u c) -> p u c", c=C), in1=mask_bc(msk, n), op=ALU.mult),
                           reads=[pck, "cmask"], writes=[bank_key(dk, us)])
                    return ev

                def fm_ops(la, lm, ra, rm, lk, rk):
                    def f(li, h):
                        fc = h // 2
                        cs_ = slice(lane_c0(li), lane_c0(li) + C)
                        lap = la[:, h, cs_] if lm else la[:, fc, cs_]
                        rap = ra[:, h, cs_] if rm else ra[:, fc, cs_]
                        return lap, rap, [(lk, h if lm else fc), (rk, h if rm else fc)]
                    return f

                all_banks = unit_banks(range(len(lanes)))
                for us in all_banks:
                    emit_product(us, fm_ops(bTm, True, kT, False, "bTm", "kT"), masked_evac(Pb, "P", m_su))
                    emit_product(us, fm_ops(kT, False, bTm, True, "kT", "bTm"), masked_evac(PTb, "PT", m_sl))
                    lru_step()
                for us in all_banks:
                    n = len(us)
                    op("pool", lambda e, us=us, n=n: e.tensor_tensor(out=sb_view(Zb, us).rearrange("p (u c) -> p u c", c=C), in0=mask_bc(ident, n),
                                                                  in1=sb_view(Pb, us).rearrange("p (u c) -> p u c", c=C), op=ALU.subtract),
                       reads=[bank_key("P", us), "cmask"], writes=[bank_key("Z", us)])
                for lev in range(1, nlev + 1):
                    last = (lev == nlev)
                    for us in all_banks:
                        kP, kPT, kZ = bank_key("P", us), bank_key("PT", us), bank_key("Z", us)
                        pct, pck = next_pc()
                        for ui_, (li, h) in enumerate(us):
                            sl = slice(uoff(li, h), uoff(li, h) + C)
                            bi = ubuf(li)
                            op("pe", lambda e, pct=pct, ui_=ui_, sl=sl, bi=bi: e.matmul(pct[:C, ui_ * C:(ui_ + 1) * C], Pb[bi][:C, sl], PTb[bi][:C, sl], start=True, stop=True),
                               reads=[kP, kPT], writes=[pck])
                        if not last:
                            pct2, pck2 = next_pc()
                            for ui_, (li, h) in enumerate(us):
                                sl = slice(uoff(li, h), uoff(li, h) + C)
                                bi = ubuf(li)
                                op("pe", lambda e, pct2=pct2, ui_=ui_, sl=sl, bi=bi: e.matmul(pct2[:C, ui_ * C:(ui_ + 1) * C], PTb[bi][:C, sl], Pb[bi][:C, sl], start=True, stop=True),
                                   reads=[kP, kPT], writes=[pck2])
                        n = len(us)
                        op("act", lambda e, pct=pct, us=us, n=n: e.copy(out=sb_view(PTb, us), in_=pct[:C, 0:n * C]), reads=[pck, kPT], writes=[kPT])
                        if not last:
                            if all_banks.index(us) % 2 == 0:
                                op("act", lambda e, pct2=pct2, us=us, n=n: e.copy(out=sb_view(Pb, us), in_=pct2[:C, 0:n * C]), reads=[pck2, kP], writes=[kP])
                            else:
                                op("dve", lambda e, pct2=pct2, us=us, n=n: e.tensor_copy(out=sb_view(Pb, us), in_=pct2[:C, 0:n * C]), reads=[pck2, kP], writes=[kP])
                    for us in all_banks:
                        kPT, kZ = bank_key("PT", us), bank_key("Z", us)
                        n = len(us)
                        pct3, pck3 = next_pc()
                        for ui_, (li, h) in enumerate(us):
                            sl = slice(uoff(li, h), uoff(li, h) + C)
                            bi = ubuf(li)
                            op("pe", lambda e, pct3=pct3, ui_=ui_, sl=sl, bi=bi: e.matmul(pct3[:C, ui_ * C:(ui_ + 1) * C], PTb[bi][:C, sl], Zb[bi][:C, sl], start=True, stop=True),
                               reads=[kPT, kZ], writes=[pck3])
                        op("dve", lambda e, pct3=pct3, us=us, n=n: e.tensor_tensor(out=sb_view(Zb, us), in0=pct3[:C, 0:n * C], in1=sb_view(Zb, us), op=ALU.add),
                           reads=[pck3, kZ], writes=[kZ])
                    lru_step()

                Alist = lambda t: [t, t]
                for li, (b, ch) in enumerate(lanes):
                    c0 = lane_c0(li)
                    cs = slice(c0, c0 + C)
                    lbanks = unit_banks([li])
                    zkeys = [bank_key("Z", us) for us in all_banks if any(u_[0] == li for u_ in us)]
                    for (srcb, dst, nm, dnm) in ((Vbm, Vte, "Vbm", "Vte"), (BhFm, Bte, "BhFm", "Bte"), (KhFm, Kte, "KhFm", "Kte")):
                        for h in range(8):
                            op("pe", lambda e, srcb=srcb, h=h: e.transpose(ptb[:C, h * 128:(h + 1) * 128], srcb[:, h, cs], ident),
                               reads=[(nm, h), "cmask"], writes=["ptb"])
                        op("act", lambda e, dst=dst: e.copy(out=dst[:C, :, :], in_=ptb[:C, :].rearrange("p (h f) -> p h f", h=8)), reads=["ptb"], writes=[dnm])
                    def a_evac(dst, dk, msk):
                        def ev(pct, pck, us):
                            n = len(us)
                            o0 = us[0][1] * C
                            op("dve", lambda e: e.tensor_tensor(out=dst[:C, o0:o0 + n * C].rearrange("p (u c) -> p u c", c=C),
                                                                in0=pct[:C, 0:n * C].rearrange("p (u c) -> p u c", c=C), in1=mask_bc(msk, n), op=ALU.mult),
                               reads=[pck, "cmask"], writes=[(dk, us[0][1])])
                        return ev
                    akeys = {}
                    for us in lbanks:
                        emit_product(us, fm_ops(ktTm, True, kT, False, "ktTm", "kT"), a_evac(AkT, "AkT", m_su))
                        emit_product(us, fm_ops(bTm, True, rT, False, "bTm", "rT"), a_evac(BrT, "BrT", m_ui))
                        emit_product(us, fm_ops(ktTm, True, rT, False, "ktTm", "rT"), a_evac(BkT, "BkT", m_ui))
                        for (_, h) in us:
                            akeys[h] = us[0][1]
                    lru_step()
                    for h2 in range(2):
                        rs = slice(64 * h2, 64 * h2 + 64)
                        op("pool", lambda e, rs=rs, h2=h2: e.tensor_copy(out=Wbd[rs, :, 64 * h2:64 * h2 + 64], in_=Wst_[rs, b, :, :]), reads=[KW], writes=["Wbd"])
                    pct, pck = next_pc()
                    for h in range(8):
                        fc, h2 = divmod(h, 2)
                        vs = slice(64 * h2, 64 * h2 + 64)
                        op("pe", lambda e, pct=pct, h=h, fc=fc, vs=vs: e.matmul(pct[:C, h * 64:(h + 1) * 64], kT[:, fc, cs], Wbd[:, fc, vs], start=True, stop=False),
                           reads=[("kT", fc), "Wbd"], writes=[pck])
                        op("pe", lambda e, pct=pct, h=h, vs=vs: e.matmul(pct[:C, h * 64:(h + 1) * 64], AkT[:C, h * C:(h + 1) * C], Vte[:C, h, vs], start=False, stop=True),
                           reads=[("AkT", akeys[h]), "Vte"], writes=[pck])
                    op("act", lambda e, pct=pct: e.copy(out=Xb[:C, :, :], in_=pct[:C, :].rearrange("p (u v) -> p u v", v=64)), reads=[pck], writes=["Xb"])
                    lru_step()
                    pct, pck = next_pc()
                    for h in range(8):
                        zsl = slice(uoff(li, h), uoff(li, h) + C)
                        op("pe", lambda e, pct=pct, h=h, zsl=zsl: e.matmul(pct[:C, h * 64:(h + 1) * 64], Zb[ubuf(li)][:C, zsl], Xb[:C, h, :], start=True, stop=True),
                           reads=zkeys + ["Xb"], writes=[pck])
                    une0 = Une[:C, 0, 0:1]
                    une_d = bass.AP(une0.tensor, une0.offset, [[une0.ap[0][0], C], [256, 4], [192, 2], [1, 64]])
                    op("act", lambda e, pct=pct, une_d=une_d: e.activation(out=une_d, in_=pct[:C, :].rearrange("p (f t v) -> p f t v", f=4, t=2), func=AF.Copy, scale=-1.0),
                       reads=[pck], writes=["Une"])
                    lru_step()
                    pct, pck = next_pc()
                    for fc in range(4):
                        o_ = pct[:, fc * C:(fc + 1) * C]
                        op("pe", lambda e, o_=o_, fc=fc: e.matmul(o_, Wbd[:, fc, :], rT[:, fc, cs], start=True, stop=False), reads=["Wbd", ("rT", fc)], writes=[pck])
                        for h2 in range(2):
                            h = 2 * fc + h2
                            op("pe", lambda e, o_=o_, h=h: e.matmul(o_, Une[:C, h, :], BrT[:C, h * C:(h + 1) * C], start=False, stop=False),
                               reads=["Une", ("BrT", akeys[h])], writes=[pck])
                            op("pe", lambda e, o_=o_, h=h, h2=h2: e.matmul(o_, Vte[:C, h, :], BkT[:C, h * C:(h + 1) * C], start=False, stop=(h2 == 1)),
                               reads=["Vte", ("BkT", akeys[h])], writes=[pck])
                    op("dve", lambda e, pct=pct: e.tensor_copy(out=tA[:, :, cs], in_=pct[:, 0:4 * C].rearrange("p (f c) -> p f c", c=C)),
                       reads=[pck], writes=fk("tA"))
                    lru_step()
                    pct, pck = next_pc()
                    for fc in range(4):
                        o_ = pct[:, fc * 64:(fc + 1) * 64]
                        for h2 in range(2):
                            h = 2 * fc + h2
                            vs = slice(64 * h2, 64 * h2 + 64)
                            op("pe", lambda e, o_=o_, h=h, vs=vs, h2=h2: e.matmul(o_, Bte[:C, h, :], Une[:C, h, vs], start=(h2 == 0), stop=False),
                               reads=["Bte", "Une"], writes=[pck])
                            op("pe", lambda e, o_=o_, h=h, vs=vs, h2=h2: e.matmul(o_, Kte[:C, h, :], Vte[:C, h, vs], start=False, stop=(h2 == 1)),
                               reads=["Kte", "Vte"], writes=[pck])
                    ci = b * NCHK + ch
                    for fc in range(4):
                        op("dve", lambda e, pct=pct, fc=fc: e.scalar_tensor_tensor(out=Wst_[:, b, fc, :], in0=Wst_[:, b, fc, :], scalar=dC[:, fc, ci:ci + 1],
                                                                                in1=pct[:, fc * 64:(fc + 1) * 64], op0=ALU.mult, op1=ALU.add),
                           reads=[pck, ("dC", fc), KW, "Wbd"], writes=[KW])
                for _ in lru_it:
                    pass
                for _ in fr_it:
                    pass
                outproj_part((4, 5, 6, 7))
                for fc in FC:
                    pmt, pmk = next_pm()
                    op("pe", lambda e, pmt=pmt, fc=fc: e.matmul(pmt[:, 0:NC], blkavg, tA[:, fc, 0:NC], start=True, stop=True), reads=["cones", ("tA", fc)], writes=[pmk])
                    op("dve", lambda e, pmt=pmt, fc=fc: e.tensor_tensor(out=E(fc), in0=tA[:, fc, 0:NC], in1=pmt[:, 0:NC], op=ALU.subtract), reads=[pmk, ("tA", fc), ("tE", fc)], writes=[("tE", fc)])
                for fc in FC:
                    op("act", lambda e, fc=fc: e.activation(out=E2(fc), in_=E(fc), func=AF.Square), reads=[("tE", fc), ("tE2", fc)], writes=[("tE2", fc)])
                for fc in FC:
                    pmt, pmk = next_pm()
                    op("pe", lambda e, pmt=pmt, fc=fc: e.matmul(pmt[:, 0:NC], blkavg, E2(fc), start=True, stop=True), reads=["cones", ("tE2", fc)], writes=[pmk])
                    op("act", lambda e, pmt=pmt, fc=fc: e.activation(out=E2(fc), in_=pmt[:, 0:NC], func=AF.Ln, bias=64e-5), reads=[pmk, ("tE2", fc)], writes=[("tE2", fc)])
                for fc in FC:
                    op("act", lambda e, fc=fc: e.activation(out=E2(fc), in_=E2(fc), func=AF.Exp, scale=-0.5), reads=[("tE2", fc)], writes=[("tE2", fc)])
                for fc in FC:
                    op("dve", lambda e, fc=fc: e.tensor_tensor(out=E(fc), in0=E(fc), in1=E2(fc), op=ALU.mult), reads=[("tE", fc), ("tE2", fc)], writes=[("tE", fc)])
                for fc in FC:
                    op("pool", lambda e, fc=fc: e.tensor_scalar(out=E(fc), in0=E(fc), scalar1=V("gn_g", fc), scalar2=V("gn_b", fc), op0=ALU.mult, op1=ALU.add), reads=[("tE", fc), "vecs"], writes=[("tE", fc)])
                for fc in FC:
                    op("pool", lambda e, fc=fc: e.tensor_tensor(out=E(fc), in0=E(fc), in1=tBon[:, fc, 0:NC], op=ALU.add), reads=[("tE", fc), ("tBon", fc)], writes=[("tE", fc)])
                for fc in FC:
                    op("dve", lambda e, fc=fc: e.tensor_tensor(out=ycatT[:, fc, 0:NC], in0=E(fc), in1=tGate[:, fc, 0:NC], op=ALU.mult), reads=[("tE", fc), ("tGate", fc)], writes=[("ycatT", fc)])

                for _ in lru_it:
                    pass


            shared = {}

            def outproj_part(ecs):
                if "wo" not in shared:
                    shared["wo"] = [WS.get(11 + i) for i in range(4)]
                wo = shared["wo"]
                groups = [(s, half) for s in range(nt) for half in range(2)]
                if True:
                    for gi, (s, half) in enumerate(groups):
                        rows = rows_list[s]
                        for ec in ecs:
                            wv, wk = wo[ec // 2]
                            op("pe", lambda e, gi=gi, s=s, rows=rows, ec=ec, wv=wv, half=half: e.matmul(pc[gi][:rows, :], ycatT[:, ec, s * 128:s * 128 + rows], wv[:, ec % 2, half * 512:(half + 1) * 512], start=(ec == 4), stop=(ec == 3)),
                               reads=[("ycatT", ec), wk], writes=["pc%d" % gi])

            def phase_C():
                outproj_part((0, 1, 2, 3))
                groups = [(s, half) for s in range(nt) for half in range(2)]
                for gi, (s, half) in enumerate(groups):
                    rows = rows_list[s]
                    op("dve", lambda e, gi=gi, s=s, rows=rows, half=half: e.tensor_tensor(out=xtm[:rows, s, half * 512:(half + 1) * 512], in0=xtm[:rows, s, half * 512:(half + 1) * 512], in1=pc[gi][:rows, :], op=ALU.add),
                       reads=["pc%d" % gi, XT(s)], writes=[XT(s)])
                for i in range(4):
                    WS.done(11 + i)
                for _ in rms_gen(par, nt, rows_list, "g2"):
                    pass


            def phase_D(bg, BGSTEP):

                W2 = 2 + L
                accs = [(s, half) for s in range(nt) for half in range(2)]
                full = (kind != "meta")

                def emit_down(pd, wd, wdk):
                    for j in range(2):
                        f = pd * 2 + j
                        for ai, (s, half) in enumerate(accs):
                            rows = rows_list[s]
                            op("pe", lambda e, ai=ai, s=s, rows=rows, half=half, f=f, j=j: e.matmul(pc[ai][:rows, :], hidT[:, f % 6, s * 128:s * 128 + rows], wd[:, j, half * 512:(half + 1) * 512], start=(f == 0), stop=(f == 23)),
                               reads=[("hidT", f % 6), wdk], writes=["pc%d" % ai])

                for pi_ in range(12):
                    wu, wuk = WS.get(15 + 2 * pi_)
                    if full:
                        wg, wgk = WS.get(16 + 2 * pi_)
                        if pi_ >= 2:
                            wd, wdk = WS.get(39 + pi_ - 2)
                    JJ = range(2)
                    fs = [pi_ * 2 + j for j in JJ]
                    ubv = [upbuf[j][:, 0:NS * W2].rearrange("p (s w) -> p s w", s=NS) for j in JJ]
                    ubk = ["upbuf%d" % j for j in JJ]
                    ucv = [upc[j][:, 0:NC].rearrange("p (s l) -> p s l", s=NS) for j in JJ]
                    uc2v = [upc2[j][:, 0:NC].rearrange("p (s l) -> p s l", s=NS) for j in JJ]
                    uc2k = ["upc2_%d" % j for j in JJ]
                    uck = ["upc%d" % j for j in JJ]
                    pu = []
                    for j in JJ:
                        pmt, pmk = pm[pi_ % 2][:, j * 256:(j + 1) * 256], ["pm%da" % (pi_ % 2), "pm%db" % (pi_ % 2)]
                        pu.append((pmt, pmk))
                        for dc in range(8):
                            op("pe", lambda e, pmt=pmt, dc=dc, j=j: e.matmul(pmt[:, 0:NC], wu[:, dc, j * 128:(j + 1) * 128], xnT[:, dc, 0:NC], start=(dc == 0), stop=(dc == 7)),
                               reads=[wuk, XN], writes=[pmk])
                    if full and pi_ >= 2:
                        emit_down(pi_ - 2, wd, wdk)
                        WS.done(39 + pi_ - 2)
                    WS.done(15 + 2 * pi_)
                    for j in JJ:
                        op("pool", lambda e, j=j: e.tensor_copy(out=ubv[j][:, :, 0:2], in_=fcarry_[:, fs[j], 0:NS, :]), reads=[(KF, fs[j]), ubk[j]], writes=[ubk[j]])
                    for j in JJ:
                        pmt, pmk = pu[j]
                        op("act", lambda e, j=j, pmt=pmt: e.copy(out=ubv[j][:, :, 2:W2], in_=pmt[:, 0:NC].rearrange("p (s l) -> p s l", s=NS)), reads=[pmk, ubk[j]], writes=[ubk[j]])
                    for j in JJ:
                        op("pool", lambda e, j=j: e.tensor_copy(out=fcarry_[:, fs[j], 0:NS, :], in_=ubv[j][:, :, L:L + 2]), reads=[ubk[j], (KF, fs[j])], writes=[(KF, fs[j])])
                    if full:
                        for j in JJ:
                            pmt, pmk = pu[j]
                            op("act", lambda e, j=j, pmt=pmt: e.activation(out=upc[j][:, 0:NC], in_=pmt[:, 0:NC], func=AF.Identity, scale=V("fcw", 48 + fs[j]), bias=V("fcb", fs[j])),
                               reads=[pmk, "vecs", uck[j]], writes=[uck[j]])
                        for j in JJ:
                            op("pool", lambda e, j=j: e.tensor_scalar(out=uc2v[j], in0=ubv[j][:, :, 0:L], scalar1=V("fcw", fs[j]), scalar2=0.0, op0=ALU.mult, op1=ALU.add),
                               reads=[ubk[j], "vecs", uc2k[j]], writes=[uc2k[j]])
                        for j in JJ:
                            op("dve", lambda e, j=j: e.scalar_tensor_tensor(out=ucv[j], in0=ubv[j][:, :, 1:1 + L], scalar=V("fcw", 24 + fs[j]), in1=ucv[j], op0=ALU.mult, op1=ALU.add),
                               reads=[ubk[j], "vecs", uck[j]], writes=[uck[j]])
                        for j in JJ:
                            op("pool", lambda e, j=j: e.tensor_tensor(out=ucv[j], in0=ucv[j], in1=uc2v[j], op=ALU.add), reads=[uck[j], uc2k[j]], writes=[uck[j]])
                        for j in JJ:
                            op("act", lambda e, j=j: e.activation(out=upc[j][:, 0:NC], in_=upc[j][:, 0:NC], func=AF.Gelu_apprx_tanh), reads=[uck[j]], writes=[uck[j]])
                        pg = []
                        for j in JJ:
                            pmt, pmk = pt32[:, j * 256:(j + 1) * 256], PT32K
                            pg.append((pmt, pmk))
                            for dc in range(8):
                                op("pe", lambda e, pmt=pmt, dc=dc, j=j: e.matmul(pmt[:, 0:NC], wg[:, dc, j * 128:(j + 1) * 128], xnT[:, dc, 0:NC], start=(dc == 0), stop=(dc == 7)),
                                   reads=[wgk, XN], writes=[pmk])
                        for j in JJ:
                            pmt, pmk = pg[j]
                            op("dve", lambda e, j=j, pmt=pmt: e.tensor_tensor(out=hidT[:, fs[j] % 6, 0:NC], in0=upc[j][:, 0:NC], in1=pmt[:, 0:NC], op=ALU.mult), reads=[pmk, uck[j]], writes=[("hidT", fs[j] % 6)])
                    if full:
                        WS.done(16 + 2 * pi_)
                    for _ in range(BGSTEP):
                        next(bg, None)
                if full:
                    for pd in (10, 11):
                        wd, wdk = WS.get(39 + pd)
                        emit_down(pd, wd, wdk)
                        WS.done(39 + pd)
                if full:
                    for ai, (s, half) in enumerate(accs):
                        rows = rows_list[s]
                        op("dve", lambda e, ai=ai, s=s, rows=rows, half=half: e.tensor_tensor(out=xtm[:rows, s, half * 512:(half + 1) * 512], in0=xtm[:rows, s, half * 512:(half + 1) * 512], in1=pc[ai][:rows, :], op=ALU.add),
                           reads=["pc%d" % ai, XT(s)], writes=[XT(s)])
                    for s in range(nt):
                        rows = rows_list[s]
                        op("act", lambda e, s=s, rows=rows: e.activation(out=xsb[:rows, :], in_=xtm[:rows, s, :], func=AF.Square, accum_out=stat[:rows, 4:5]),
                           reads=[XT(s)], writes=["xsb", "stat4"])
                        op("act", lambda e, rows=rows: e.activation(out=stat[:rows, 5:6], in_=stat[:rows, 4:5], func=AF.Ln, scale=1.0 / D, bias=1e-6), reads=["stat4"], writes=["stat5"])
                        op("act", lambda e, rows=rows: e.activation(out=stat[:rows, 6:7], in_=stat[:rows, 5:6], func=AF.Exp, scale=-0.5), reads=["stat5"], writes=["stat6"])
                        op("dve", lambda e, s=s, rows=rows: e.scalar_tensor_tensor(out=xtm[:rows, s, :], in0=xtm[:rows, s, :], scalar=stat[:rows, 6:7], in1=gfbc[:rows, :], op0=ALU.mult, op1=ALU.mult),
                           reads=[XT(s), "stat6", "gfbc"], writes=[XT(s)])
                        if kind == "sample":
                            tk.dma("sp", lambda e: e.dma_start(out=y_s_d, in_=xtm[:64, 0, :]), "yst%d0" % par, reads=[XT(0)])
                        else:
                            tk.dma("sp", lambda e, s=s, r0=prow + s * 128: e.dma_start(out=y_p_d[r0:r0 + 128, :], in_=xtm[:, s, :]), "yst%d%d" % (par, s), reads=[XT(s)])
                last_prompt = (kind == "prompt" and all(t[0] != "prompt" for t in tiles[ti + 1:]))
                if kind == "sample" or last_prompt:
                    tk.dma("sp", lambda e: e.dma_start(out=o_u_d[skind], in_=carry_[:, :, 0:NS, :]), "o_u", reads=[KC])
                    tk.dma("sp", lambda e: e.dma_start(out=o_w_d[skind], in_=Wst_[:, 0:NS, :, :]), "o_w", reads=[KW])
                    tk.dma("sp", lambda e: e.dma_start(out=o_h_d[skind], in_=hst_[:, :, 0:NS], allow_slow_non_contiguous=True), "o_h", reads=[KH])
                    tk.dma("sp", lambda e: e.dma_start(out=o_f_d[skind], in_=fcarry_[:, :, 0:NS, :]), "o_f", reads=[(KF, f_) for f_ in range(24)])


                for _ in bg:
                    pass

            return phase_A, phase_B, phase_C, phase_D

        prows = []
        _p = 0
        for (k_, _, _) in tiles:
            prows.append(_p)
            if k_ == "prompt":
                _p += 256
        tk.dma("sp", lambda e: e.dma_start(out=carry[:], in_=st_u_d), "st_u", writes=["carry_sample"])
        tk.dma("sp", lambda e: e.dma_start(out=hst[:], in_=st_h_d), "st_h", writes=["hst_sample"])
        tk.dma("sp", lambda e: e.dma_start(out=fcarry[:], in_=st_f_d), "st_f", writes=[("fcarry_sample", f_) for f_ in range(24)])
        for _ in front_gen(0, 0):
            pass
        nextA = None
        for ti in range(len(tiles)):
            pA, pB, pC, pD = make_tile(ti)
            if nextA is None:
                for _ in pA():
                    pass
            pB()
            pC()
            if ti == 0:
                tk.dma("sp", lambda e: e.dma_start(out=Wst[:], in_=st_w_d), "st_w", writes=["Wst_sample"])
            if ti + 1 < len(tiles):
                nA = make_tile(ti + 1)
                bg = nA[0]()
                nextA = True
            else:
                bg = iter(())
                nextA = None
            pD(bg, 3)

        tk.final_wait("sp")
        if info is not None:
            info["worder"] = list(WS.rec)
        if WORDER is not None:
            tk.emit()
    return nc


def _chunks(v, n):
    v = np.asarray(v, np.float32).reshape(n, 128)
    return np.ascontiguousarray(v.T)


_NC_CACHE = {}


def kernel(x_prompt, x_sample, state_tm_shift, state_tm_wkv, state_lru_conv, state_lru_h, state_ffn_conv,
           meta_tokens, norm1_g, w_in, tm_mu, tm_w0, tm_w_up, tm_a0, tm_a_up, tm_g_up, tm_k_k, tm_k_a, tm_r_k,
           tm_gn_g, tm_gn_b, lru_conv_w, lru_conv_b, lru_wa, lru_ba, lru_wx, lru_bx, lru_lambda, lru_out_g,
           w_out, norm2_g, ffn_w_up, ffn_w_gate, ffn_conv_w, ffn_conv_b, ffn_w_down, norm_f_g):
    f = lambda a: np.ascontiguousarray(np.asarray(a, np.float32))
    cols = {"mu": _chunks(tm_mu[0], 14), "w0": _chunks(tm_w0[0], 4), "a0": _chunks(tm_a0[0], 4), "k_k": _chunks(tm_k_k[0], 4),
            "k_a": _chunks(tm_k_a[0], 4), "r_k": _chunks(np.asarray(tm_r_k[0]).reshape(-1), 4), "gn_g": _chunks(tm_gn_g[0], 4),
            "gn_b": _chunks(tm_gn_b[0], 4),
            "lcw": np.concatenate([_chunks(lru_conv_w[0][j], 4) for j in range(4)], 1), "lcb": _chunks(lru_conv_b[0], 4),
            "ba": _chunks(lru_ba[0], 4), "bx": _chunks(lru_bx[0], 4), "lam": _chunks(lru_lambda[0], 4), "outg": _chunks(lru_out_g[0], 4),
            "fcw": np.concatenate([_chunks(ffn_conv_w[0][j], 24) for j in range(3)], 1), "fcb": _chunks(ffn_conv_b[0], 24),
            "g1": _chunks(norm1_g[0], 8), "g2": _chunks(norm2_g[0], 8)}
    vecs = np.ascontiguousarray(np.concatenate([cols[n] for n, _ in VEC_SPEC], 1).astype(np.float32))
    assert vecs.shape == (128, NV)

    def bd(w):
        w = np.asarray(w, np.float32)
        out = np.zeros((4, 128, 128), np.float32)
        for c in range(4):
            out[c, 0:64, 0:64] = w[2 * c]
            out[c, 64:128, 64:128] = w[2 * c + 1]
        return out

    eye = np.eye(128, dtype=np.float32)
    su = np.triu(np.ones((128, 128), np.float32), 1)
    cmask = np.stack([eye, su, np.ascontiguousarray(su.T), np.triu(np.ones((128, 128), np.float32), 0)])
    blk = np.zeros((128, 128), np.float32)
    blk[0:64, 0:64] = 1.0
    blk[64:, 64:] = 1.0
    cones = np.stack([blk / 64.0, blk, np.full((128, 128), 1.0 / 512.0, np.float32)])

    shared = {"vecs": vecs, "gf": f(norm_f_g), "w_in": f(w_in[0]), "w_out": f(w_out[0]), "w_upf": f(ffn_w_up[0]),
              "w_gate": f(ffn_w_gate[0]), "w_down": f(ffn_w_down[0]), "tmwup": f(tm_w_up[0]), "tmaup": f(tm_a_up[0]),
              "tmgup": f(tm_g_up[0]), "wabd": bd(lru_wa[0]), "wxbd": bd(lru_wx[0]), "cmask": cmask, "cones": cones,
              "meta": f(meta_tokens)}
    xp = np.asarray(x_prompt, np.float32)
    xs = np.asarray(x_sample, np.float32)
    sh = np.asarray(state_tm_shift, np.float32)[0]
    wk = np.asarray(state_tm_wkv, np.float32)[0]
    lc = np.asarray(state_lru_conv, np.float32)[0]
    lh = np.asarray(state_lru_h, np.float32)[0]
    fcv = np.asarray(state_ffn_conv, np.float32)[0]
    in_maps = []
    for c in range(NCORE):
        bs = slice(16 * c, 16 * c + 16)
        st_u = np.zeros((128, 22, 16, 3), np.float32)
        st_u[:, 0:14, :, 2] = sh[bs].reshape(16, 14, 128).transpose(2, 1, 0)
        st_u[:, 14:18, :, :] = lc[bs].reshape(16, 3, 4, 128).transpose(3, 2, 0, 1)
        st_w = wk[bs].reshape(16, 4, 2, 64, 64).transpose(2, 4, 0, 1, 3).reshape(128, 16, 4, 64)
        st_h = lh[bs].reshape(16, 4, 128).transpose(2, 1, 0)
        st_f = fcv[bs].reshape(16, 2, 24, 128).transpose(3, 2, 0, 1)
        m = dict(shared)
        m.update({"xp": f(xp[c]), "xs": f(xs[bs].reshape(64, D)), "st_u": f(st_u), "st_w": f(st_w), "st_h": f(st_h), "st_f": f(st_f)})
        in_maps.append(m)
    if "nc" not in _NC_CACHE:
        info = {}
        build(info=info)
        _NC_CACHE["nc"] = build(WORDER=info["worder"])
    nc = _NC_CACHE["nc"]
    res = run_bass_kernel_spmd(nc, in_maps, core_ids=list(range(NCORE)))
    R = res.results
    y_prompt = np.stack([R[c]["y_p"] for c in range(NCORE)]).astype(np.float32)
    y_sample = np.concatenate([R[c]["y_s"].reshape(16, 4, D) for c in range(NCORE)], 0).astype(np.float32)

    def unpack(pre, nb):
        shift, wkv, lconv, lhh, fconv = [], [], [], [], []
        for c in range(NCORE):
            ou = R[c][pre + "_u"]
            shift.append(ou[:, 0:14, :, 2].transpose(2, 1, 0).reshape(nb, DTM))
            lconv.append(ou[:, 14:18, :, :].transpose(2, 3, 1, 0).reshape(nb, 3, 512))
            ow = R[c][pre + "_w"]
            wkv.append(ow.reshape(2, 64, nb, 4, 64).transpose(2, 3, 0, 4, 1).reshape(nb, 8, 64, 64))
            lhh.append(R[c][pre + "_h"].transpose(2, 1, 0).reshape(nb, 512))
            fconv.append(R[c][pre + "_f"].transpose(2, 3, 1, 0).reshape(nb, 2, DFF))
        cat = lambda l: np.ascontiguousarray(np.concatenate(l, 0)[None].astype(np.float32))
        return cat(shift), cat(wkv), cat(lconv), cat(lhh), cat(fconv)

    p = unpack("op", 1)
    s = unpack("os", 16)
    return (y_prompt, y_sample) + p + s
```
